# Optimizing a Trainium2 kernel written in Bass

```python
import jax, jax.numpy as jnp
from jax import lax
import numpy as np

D_MODEL = 1024
BATCH = 4
SEQ = 4096
DEPTH = 1
DEC_BATCH = 128
DEC_SEQ = 4
PAST_LEN = 2048
PAGE_SIZE = 128

A_HEADS = 8
A_HEAD_DIM = 64
A_WIDTH = A_HEADS * A_HEAD_DIM
Q_BLOCK = 128
B_HEADS = 4
B_HEAD_DIM = 128
B_WIDTH = B_HEADS * B_HEAD_DIM
CONV_W = 4
MLSTM_CHUNK = 64
D_FF = 4 * D_MODEL
N_MOD = 6
RMS_EPS = 1e-6
NEG_INF = -1e30

kernel_name = 'fox_mlstm_hybrid_step'


def _split_points():
    sizes = (A_WIDTH, A_WIDTH, A_WIDTH, A_HEADS, B_WIDTH, B_WIDTH, B_WIDTH, B_HEADS, B_HEADS, B_WIDTH, D_MODEL, D_MODEL)
    return [int(s) for s in np.cumsum(sizes)[:-1]]


def _n_in():
    return 3 * A_WIDTH + A_HEADS + 4 * B_WIDTH + 2 * B_HEADS + 2 * D_MODEL


def rmsnorm(x, g):
    xf = x.astype(jnp.float32)
    r = lax.rsqrt(jnp.mean(xf * xf, axis=-1, keepdims=True) + RMS_EPS)
    return (xf * r).astype(x.dtype) * g


def fox_attention(q, fq, qpos, k, v, fk, kpos):
    b, l, h, d = q.shape
    qb = min(Q_BLOCK, l)
    nb = l // qb
    scale = d ** -0.5
    fk_t = jnp.transpose(fk, (0, 2, 1))

    def block(args):
        qi, fqi, pi = args
        s = jnp.einsum('bqhd,bkhd->bhqk', qi, k).astype(jnp.float32) * scale
        s = s + jnp.transpose(fqi, (0, 2, 1))[..., None] - fk_t[:, :, None, :]
        s = jnp.where(kpos[None, None, None, :] <= pi[None, None, :, None], s, NEG_INF)
        p = jax.nn.softmax(s, axis=-1).astype(v.dtype)
        return jnp.einsum('bhqk,bkhd->bqhd', p, v)

    qs = jnp.moveaxis(q.reshape(b, nb, qb, h, d), 1, 0)
    fs = jnp.moveaxis(fq.reshape(b, nb, qb, h), 1, 0)
    ps = qpos.reshape(nb, qb)
    out = lax.map(block, (qs, fs, ps))
    return jnp.moveaxis(out, 0, 1).reshape(b, l, h, d)


def mlstm_chunkwise(q, k, v, i_pre, logf, c0, n0, m0):
    f32 = jnp.float32
    b, l, h, dk = q.shape
    dv = v.shape[-1]
    ch = min(MLSTM_CHUNK, l)
    nc = l // ch

    def to_chunks(a):
        return jnp.moveaxis(a.astype(f32).reshape((b, nc, ch) + a.shape[2:]), 1, 0)

    causal = jnp.tril(jnp.ones((ch, ch), dtype=bool))

    def step(carry, xs):
        c, n, m = carry
        qc, kc, vc, ic, fc = xs
        bcum = jnp.cumsum(fc, axis=1)
        dlog = bcum[:, :, None, :] - bcum[:, None, :, :] + ic[:, None, :, :]
        dlog = jnp.where(causal[None, :, :, None], dlog, NEG_INF)
        inter = bcum + m[:, None, :]
        mt = jnp.maximum(inter, jnp.max(dlog, axis=2))
        a = jnp.exp(dlog - mt[:, :, None, :]) * jnp.einsum('bthd,bshd->btsh', qc, kc)
        si = jnp.exp(inter - mt)
        num = jnp.einsum('btsh,bshe->bthe', a, vc) + si[..., None] * jnp.einsum('bthd,bhde->bthe', qc, c)
        den = jnp.sum(a, axis=2) + si * jnp.einsum('bthd,bhd->bth', qc, n)
        hc = num / jnp.maximum(jnp.abs(den), jnp.exp(-mt))[..., None]
        bl = bcum[:, -1, :]
        wlog = bl[:, None, :] - bcum + ic
        m_new = jnp.maximum(bl + m, jnp.max(wlog, axis=1))
        ws = jnp.exp(wlog - m_new[:, None, :])
        decay = jnp.exp(bl + m - m_new)
        c_new = decay[..., None, None] * c + jnp.einsum('bsh,bshd,bshe->bhde', ws, kc, vc)
        n_new = decay[..., None] * n + jnp.einsum('bsh,bshd->bhd', ws, kc)
        return (c_new, n_new, m_new), hc

    xs = (to_chunks(q), to_chunks(k), to_chunks(v), to_chunks(i_pre), to_chunks(logf))
    (c_f, n_f, m_f), hs = lax.scan(step, (c0.astype(f32), n0.astype(f32), m0.astype(f32)), xs)
    return jnp.moveaxis(hs, 0, 1).reshape(b, l, h, dv), c_f, n_f, m_f


def causal_conv(x, prev, w, bias):
    l = x.shape[1]
    xp = jnp.concatenate([prev.astype(x.dtype), x], axis=1)
    y = bias + w[0] * xp[:, 0:l]
    for j in range(1, CONV_W):
        y = y + w[j] * xp[:, j:j + l]
    return jax.nn.silu(y), xp[:, l:]


def hybrid_layer(x, c, k_past, v_past, logf_past, c_state, n_state, m_state, conv_prev, lw):
    (w_ada, b_ada, g_pre_mix, g_post_mix, g_pre_mlp, g_post_mlp, w_in, b_fox_f, b_ml_i, b_ml_f,
     conv_w, conv_b, w_proj_a, w_proj_b, w_out, w_up, w_down) = lw
    f32 = jnp.float32
    bsz, l = x.shape[:2]
    mod = jax.nn.silu(c) @ w_ada + b_ada
    sh_m, sc_m, gt_m, sh_f, sc_f, gt_f = [t[:, None, :] for t in jnp.split(mod, N_MOD, axis=-1)]

    h = rmsnorm(x, g_pre_mix) * (1 + sc_m) + sh_m
    z = h @ w_in
    aq, ak, av, af, bq, bk, bv, bi, bf, bo, ga, gb = jnp.split(z, _split_points(), axis=-1)

    aq = aq.reshape(bsz, l, A_HEADS, A_HEAD_DIM)
    ak = ak.reshape(bsz, l, A_HEADS, A_HEAD_DIM)
    av = av.reshape(bsz, l, A_HEADS, A_HEAD_DIM)
    a_logf = jax.nn.log_sigmoid(af.astype(f32) + b_fox_f)
    if k_past is None:
        p = 0
        keys, vals, lf_all = ak, av, a_logf
    else:
        p = k_past.shape[1]
        keys = jnp.concatenate([k_past.astype(ak.dtype), ak], axis=1)
        vals = jnp.concatenate([v_past.astype(av.dtype), av], axis=1)
        lf_all = jnp.concatenate([logf_past.astype(f32), a_logf], axis=1)
    fk = jnp.cumsum(lf_all, axis=1)
    kpos = jnp.arange(p + l, dtype=jnp.int32)
    ya = fox_attention(aq, fk[:, p:], kpos[p:], keys, vals, fk, kpos)
    ya = ya.reshape(bsz, l, A_WIDTH) @ w_proj_a

    qk_c, conv_new = causal_conv(jnp.concatenate([bq, bk], axis=-1), conv_prev, conv_w, conv_b)
    mq, mk = jnp.split(qk_c, 2, axis=-1)
    mq = mq.reshape(bsz, l, B_HEADS, B_HEAD_DIM)
    mk = mk.reshape(bsz, l, B_HEADS, B_HEAD_DIM) * (B_HEAD_DIM ** -0.5)
    mv = bv.reshape(bsz, l, B_HEADS, B_HEAD_DIM)
    i_pre = bi.astype(f32) + b_ml_i
    b_logf = jax.nn.log_sigmoid(bf.astype(f32) + b_ml_f)
    hb, c_new, n_new, m_new = mlstm_chunkwise(mq, mk, mv, i_pre, b_logf, c_state, n_state, m_state)
    yb = (jax.nn.sigmoid(bo) * hb.reshape(bsz, l, B_WIDTH).astype(x.dtype)) @ w_proj_b

    y = (jax.nn.sigmoid(ga) * ya + jax.nn.sigmoid(gb) * yb) @ w_out
    x = x + gt_m * rmsnorm(y, g_post_mix)

    h2 = rmsnorm(x, g_pre_mlp) * (1 + sc_f) + sh_f
    u = jnp.square(jax.nn.relu(h2 @ w_up))
    x = x + gt_f * rmsnorm(u @ w_down, g_post_mlp)
    return x, ak, av, a_logf, c_new, n_new, m_new, conv_new


def setup_inputs(seed: int = 0) -> dict:
    key = jax.random.key(seed)
    ks = jax.random.split(key, 32)
    f32 = jnp.float32
    n_pages = PAST_LEN // PAGE_SIZE
    n_used = DEC_BATCH * n_pages
    n_phys = n_used + n_used // 4
    n_in = _n_in()

    def nrm(k, shape, s):
        return jax.random.normal(k, shape, f32) * s

    page_table = jax.random.permutation(ks[0], n_phys)[:n_used].reshape(DEC_BATCH, n_pages).astype(jnp.int32)
    return {
        'x_prompt': nrm(ks[1], (BATCH, SEQ, D_MODEL), 1.0),
        'x_sample': nrm(ks[2], (DEC_BATCH, DEC_SEQ, D_MODEL), 1.0),
        'cache_k': nrm(ks[3], (DEPTH, n_phys, PAGE_SIZE, A_HEADS, A_HEAD_DIM), 1.0),
        'cache_v': nrm(ks[4], (DEPTH, n_phys, PAGE_SIZE, A_HEADS, A_HEAD_DIM), 1.0),
        'cache_logf': jax.nn.log_sigmoid(2.0 + nrm(ks[5], (DEPTH, n_phys, PAGE_SIZE, A_HEADS), 1.0)),
        'page_table': page_table,
        'state_C': nrm(ks[6], (DEPTH, DEC_BATCH, B_HEADS, B_HEAD_DIM, B_HEAD_DIM), 0.3),
        'state_n': nrm(ks[7], (DEPTH, DEC_BATCH, B_HEADS, B_HEAD_DIM), 0.3),
        'state_m': nrm(ks[8], (DEPTH, DEC_BATCH, B_HEADS), 1.0),
        'state_conv': nrm(ks[9], (DEPTH, DEC_BATCH, CONV_W - 1, 2 * B_WIDTH), 1.0),
        'c_prompt': nrm(ks[10], (BATCH, D_MODEL), 1.0),
        'c_sample': nrm(ks[11], (DEC_BATCH, D_MODEL), 1.0),
        'w_ada': nrm(ks[12], (DEPTH, D_MODEL, N_MOD * D_MODEL), 0.5 * D_MODEL ** -0.5),
        'b_ada': nrm(ks[13], (DEPTH, N_MOD * D_MODEL), 0.02),
        'g_pre_mix': 1.0 + nrm(ks[14], (DEPTH, D_MODEL), 0.02),
        'g_post_mix': 1.0 + nrm(ks[15], (DEPTH, D_MODEL), 0.02),
        'g_pre_mlp': 1.0 + nrm(ks[16], (DEPTH, D_MODEL), 0.02),
        'g_post_mlp': 1.0 + nrm(ks[17], (DEPTH, D_MODEL), 0.02),
        'w_in': nrm(ks[18], (DEPTH, D_MODEL, n_in), D_MODEL ** -0.5),
        'b_fox_f': 2.0 + nrm(ks[19], (DEPTH, A_HEADS), 0.5),
        'b_ml_i': nrm(ks[20], (DEPTH, B_HEADS), 0.1),
        'b_ml_f': jnp.linspace(3.0, 6.0, B_HEADS, dtype=f32)[None, :] + nrm(ks[21], (DEPTH, B_HEADS), 0.1),
        'conv_w': nrm(ks[22], (DEPTH, CONV_W, 2 * B_WIDTH), CONV_W ** -0.5),
        'conv_b': nrm(ks[23], (DEPTH, 2 * B_WIDTH), 0.02),
        'w_proj_a': nrm(ks[24], (DEPTH, A_WIDTH, D_MODEL), A_WIDTH ** -0.5),
        'w_proj_b': nrm(ks[25], (DEPTH, B_WIDTH, D_MODEL), B_WIDTH ** -0.5),
        'w_out': nrm(ks[26], (DEPTH, D_MODEL, D_MODEL), D_MODEL ** -0.5),
        'w_up': nrm(ks[27], (DEPTH, D_MODEL, D_FF), D_MODEL ** -0.5),
        'w_down': nrm(ks[28], (DEPTH, D_FF, D_MODEL), D_FF ** -0.5),
    }


def reference(x_prompt, x_sample, cache_k, cache_v, cache_logf, page_table, state_C, state_n, state_m, state_conv,
              c_prompt, c_sample, w_ada, b_ada, g_pre_mix, g_post_mix, g_pre_mlp, g_post_mlp, w_in, b_fox_f,
              b_ml_i, b_ml_f, conv_w, conv_b, w_proj_a, w_proj_b, w_out, w_up, w_down):
    f32 = jnp.float32
    bsz = x_prompt.shape[0]
    dbsz = x_sample.shape[0]
    n_pages = page_table.shape[1]
    past = n_pages * PAGE_SIZE
    y_prompt, y_sample = x_prompt, x_sample
    kp, vp, lfp, cp, nps, mp, cvp = [], [], [], [], [], [], []
    ksm, vsm, lfs, cs, nss, ms, cvs = [], [], [], [], [], [], []
    for layer in range(DEPTH):
        lw = (w_ada[layer], b_ada[layer], g_pre_mix[layer], g_post_mix[layer], g_pre_mlp[layer], g_post_mlp[layer],
              w_in[layer], b_fox_f[layer], b_ml_i[layer], b_ml_f[layer], conv_w[layer], conv_b[layer],
              w_proj_a[layer], w_proj_b[layer], w_out[layer], w_up[layer], w_down[layer])
        c0 = jnp.zeros((bsz, B_HEADS, B_HEAD_DIM, B_HEAD_DIM), f32)
        n0 = jnp.zeros((bsz, B_HEADS, B_HEAD_DIM), f32)
        m0 = jnp.zeros((bsz, B_HEADS), f32)
        cv0 = jnp.zeros((bsz, CONV_W - 1, 2 * B_WIDTH), x_prompt.dtype)
        y_prompt, k1, v1, lf1, c1, n1, m1, cv1 = hybrid_layer(y_prompt, c_prompt, None, None, None, c0, n0, m0, cv0, lw)
        k_past = cache_k[layer][page_table].reshape(dbsz, past, A_HEADS, A_HEAD_DIM)
        v_past = cache_v[layer][page_table].reshape(dbsz, past, A_HEADS, A_HEAD_DIM)
        lf_past = cache_logf[layer][page_table].reshape(dbsz, past, A_HEADS)
        y_sample, k2, v2, lf2, c2, n2, m2, cv2 = hybrid_layer(
            y_sample, c_sample, k_past, v_past, lf_past, state_C[layer], state_n[layer], state_m[layer],
            state_conv[layer], lw)
        kp.append(k1); vp.append(v1); lfp.append(lf1); cp.append(c1); nps.append(n1); mp.append(m1); cvp.append(cv1)
        ksm.append(k2); vsm.append(v2); lfs.append(lf2); cs.append(c2); nss.append(n2); ms.append(m2); cvs.append(cv2)
    new_k_prompt = jnp.stack(kp)
    new_v_prompt = jnp.stack(vp)
    new_logf_prompt = jnp.stack(lfp)
    new_C_prompt = jnp.stack(cp)
    new_n_prompt = jnp.stack(nps)
    new_m_prompt = jnp.stack(mp)
    new_conv_prompt = jnp.stack(cvp)
    new_k_sample = jnp.stack(ksm)
    new_v_sample = jnp.stack(vsm)
    new_logf_sample = jnp.stack(lfs)
    new_C_sample = jnp.stack(cs)
    new_n_sample = jnp.stack(nss)
    new_m_sample = jnp.stack(ms)
    new_conv_sample = jnp.stack(cvs)
    return (y_prompt, y_sample, new_k_prompt, new_v_prompt, new_logf_prompt, new_C_prompt, new_n_prompt,
            new_m_prompt, new_conv_prompt, new_k_sample, new_v_sample, new_logf_sample, new_C_sample,
            new_n_sample, new_m_sample, new_conv_sample)
```

```python
import contextlib
import numpy as np
import concourse.bass as bass
import concourse.mybir as mybir
from concourse.bass_utils import run_bass_kernel_spmd

F32 = mybir.dt.float32
BF16 = mybir.dt.bfloat16
I32 = mybir.dt.int32
ALU = mybir.AluOpType
AF = mybir.ActivationFunctionType
AX = mybir.AxisListType
ENGS = ("tensor", "vector", "scalar", "gpsimd", "sync")

D = 1024
NCORE = 8
TOK = 2048
T = 256
NT = TOK // T
TS = 64
NSEQ = 16
A_H, A_D = 8, 64
B_H, B_D = 4, 128
NIN = 5648
USE_CACHE = True
import os
KSKIP = os.environ.get("KSKIP", "")
O_AQ, O_AK, O_AV, O_AF, O_BQ, O_BK, O_BV, O_BI, O_BF, O_BO, O_GA, O_GB = (
    0, 512, 1024, 1536, 1544, 2056, 2568, 3080, 3084, 3088, 3600, 4624)
C_AQ, C_AK, C_AV, C_BQ, C_BK, C_BV, C_BO, C_GA, C_GB = (0, 512, 1024, 1536, 2048, 2560, 3072, 3584, 4608)
NINP = 5632


class WB:
    def __init__(self, ap4, base=0, width=None):
        self.ap4, self.base, self.width = ap4, base, width

    def __getitem__(self, key):
        _, ks, cs = key
        c0 = self.base + (cs.start or 0)
        c1 = self.base + (cs.stop if cs.stop is not None else self.width)
        if c1 - c0 <= 512:
            assert c0 % 512 == 0 and ks == slice(None)
            return self.ap4[:, c0 // 512, :, 0:c1 - c0]
        return WB(self.ap4, c0, c1 - c0)


class Sched:
    def __init__(self, nc):
        self.nc = nc
        self.phase = 0
        self.reset()

    def reset(self):
        self.phase += 1
        self.ops = {e: [] for e in ENGS}
        self.cnt = {}
        self.known = {e: {} for e in ENGS}
        self.keys = {}
        self.pseudo = {}
        self.pending = {e: ([], []) for e in ENGS}
        self.semnames = ["E_" + e for e in ENGS]

    def _need(self, eng, ev, waits, kind="raw"):
        if ev is None:
            return
        s, v = ev
        if s == "E_" + eng:
            if eng == "tensor":
                return
            if eng in ("vector", "scalar") and kind != "raw":
                return
        if self.known[eng].get(s, 0) >= v:
            return
        self.known[eng][s] = v
        waits.append((s, v))

    def op(self, eng, fn, reads=(), writes=(), inc=True, dma=None):
        pacc = [k for k in reads if k.startswith(("ps", "acc"))]
        if pacc:
            reads = [k for k in reads if k not in pacc]
            writes = list(writes) + pacc
        waits = []
        for k in reads:
            st = self.keys.get(k)
            if st:
                self._need(eng, st[0], waits, "raw")
        for k in writes:
            st = self.keys.get(k)
            if st:
                if k in pacc:
                    self._need(eng, st[0], waits, "war" if self.pseudo.get(k) else "raw")
                else:
                    self._need(eng, st[0], waits, "waw")
                for ev in st[1]:
                    self._need(eng, ev, waits, "war")
        for k in writes:
            self.pseudo[k] = (k in pacc)
        if dma is not None:
            s = "D_" + dma
            if s not in self.cnt:
                self.semnames.append(s)
            self.cnt[s] = self.cnt.get(s, 0) + 16
            self._apply((s, self.cnt[s]), reads, writes)
            self.ops[eng].append((waits, fn, (s, 16)))
            return
        pr, pw = self.pending[eng]
        if not inc:
            pr.extend(reads)
            pw.extend(writes)
            self.ops[eng].append((waits, fn, None))
            return
        s = "E_" + eng
        self.cnt[s] = self.cnt.get(s, 0) + 1
        self._apply((s, self.cnt[s]), list(reads) + pr, list(writes) + pw)
        self.pending[eng] = ([], [])
        self.ops[eng].append((waits, fn, (s, 1)))

    def _apply(self, ev, reads, writes):
        for k in reads:
            self.keys.setdefault(k, [None, []])[1].append(ev)
        for k in writes:
            self.keys[k] = [ev, []]

    def barrier(self):
        for e in ENGS:
            waits = []
            for s, v in self.cnt.items():
                self._need(e, (s, v), waits)
            self.ops[e].append((waits, None, None))

    def emit(self):
        nc = self.nc
        with contextlib.ExitStack() as st:
            sems = {n: st.enter_context(nc.semaphore("%s_p%d" % (n, self.phase))) for n in self.semnames}
            block = st.enter_context(nc.Block())

            def run(engname):
                def body(e):
                    for waits, fn, inc in self.ops[engname]:
                        for s, v in waits:
                            e.wait_ge(sems[s], v)
                        if fn is None:
                            continue
                        ins = fn(e)
                        if inc is not None:
                            ins.then_inc(sems[inc[0]], inc[1])
                return body
            block.tensor(run("tensor"))
            block.vector(run("vector"))
            block.scalar(run("scalar"))
            block.gpsimd(run("gpsimd"))
            block.sync(run("sync"))


def bc_last(ap, n):
    return bass.AP(ap.tensor, ap.offset, [list(x) for x in ap.ap] + [[0, n]])


class Ring:
    def __init__(self, tiles, name):
        self.tiles = tiles
        self.name = name
        self.i = 0

    def next(self):
        j = self.i % len(self.tiles)
        self.i += 1
        return self.tiles[j], "%s%d" % (self.name, j)


def build():
    nc = bass.Bass("TRN2", target_bir_lowering=False)

    def din(name, shape, dt=F32):
        return nc.dram_tensor(name, list(shape), dt, kind="ExternalInput").ap()

    def dout(name, shape, dt=F32):
        return nc.dram_tensor(name, list(shape), dt, kind="ExternalOutput").ap()

    xT_own = din("xT_own", [128, 8, TOK])
    xT_smp = din("xT_smp", [128, 8, TS])
    cT = din("cT", [128, 8, 17])
    w_ada = WB(din("w_ada", [128, 12, 8, 512]))
    b_ada = din("b_ada", [128, 48])
    g4 = din("g4", [128, 4, 8])
    w_in = WB(din("w_in", [128, NINP // 512, 8, 512]))
    w_g = din("w_g", [128, 8, 16])
    b_g = din("b_g", [128, 16])
    cst = din("cst", [128, 6, 128])
    xT_pre = din("xT_pre", [128, 8, TOK])
    flag = din("flag", [128, 1])
    conv_w = din("conv_w", [128, 8, 4])
    conv_b = din("conv_b", [128, 8])
    sC = din("sC", [128, NSEQ, 4, 128])
    sN = din("sN", [128, NSEQ, 4])
    sM = din("sM", [1, NSEQ, 4])
    sCV = din("sCV", [128, 8, NSEQ, 3])
    cst2 = din("cst2", [128, 3, 256])
    w_pa = WB(din("w_pa", [128, 2, 4, 512]))
    w_pb = WB(din("w_pb", [128, 2, 4, 512]))
    w_o = WB(din("w_o", [128, 2, 8, 512]))
    w_u = WB(din("w_u", [128, 8, 8, 512]))
    w_d = din("w_d", [128, 8, 8, 512])
    if USE_CACHE:
        cache_k = din("cache_k", [2560 * 128, 512])
        cache_v = din("cache_v", [2560 * 128, 512])
        cache_lf = din("cache_lf", [2560 * 128, 8])
    pt = din("pt", [1, NSEQ * 16], I32)
    qscr = dout("qscr", [TS, 512])

    kT_o = dout("kT_o", [512, TOK])
    v_o = dout("v_o", [TOK, 512])
    lf_o = dout("lf_o", [TOK, 8])
    cv_o = dout("cv_o", [128, 8, 3])
    kT_s = dout("kT_s", [512, TS])
    v_s = dout("v_s", [TS, 512])
    lf_s = dout("lf_s", [TS, 8])
    cv_s = dout("cv_s", [128, 8, NSEQ * 7])
    yT_o = dout("yT_o", [128, 8, TOK])
    yT_s = dout("yT_s", [128, 8, TS])
    C_o = dout("C_o", [128, 4, 128])
    n_o = dout("n_o", [128, 4])
    m_o = dout("m_o", [1, 4])
    C_s = dout("C_s", [128, NSEQ * 4, 128])
    n_s = dout("n_s", [128, NSEQ * 4])
    m_s = dout("m_s", [1, NSEQ * 4])

    S = Sched(nc)
    with contextlib.ExitStack() as st:
        def sb(name, shape, dt=F32):
            return st.enter_context(nc.sbuf_tensor(name, list(shape), dt))

        PS = Ring([st.enter_context(nc.psum_tensor("ps%d" % i, [128, 512], F32)) for i in range(8)], "ps")
        WP = Ring([sb("wp%d" % i, [128, 8, 512], BF16) for i in range(2)], "wp")
        STG = Ring([sb("stg%d" % i, [128, 512], F32) for i in range(2)], "stg")

        cstf = sb("cstf", [128, 6, 128])
        cstb = sb("cstb", [128, 6, 128], BF16)
        g4t = sb("g4t", [128, 4, 8])
        badat = sb("badat", [128, 48])
        bgt = sb("bgt", [128, 16])
        ctt = sb("ctt", [128, 8, 17])
        csil = sb("csil", [128, 8, 17], BF16)
        MOD = sb("MOD", [128, 48, 17])
        AM = sb("AM", [128, 8, 17])
        GM = sb("GM", [128, 8, 17])
        AFm = sb("AFm", [128, 8, 17])
        GFm = sb("GFm", [128, 8, 17])
        wgf = sb("wgf", [128, 8, 16], BF16)
        onec = sb("onec", [128, 1])
        epsc = sb("epsc", [128, 1])

        ld = lambda dst, src, key: S.op("sync", lambda e: e.dma_start(out=dst, in_=src), writes=[key], dma=key)
        ld(cstf[:], cst, "cstf")
        ld(g4t[:], g4, "g4t")
        ld(badat[:], b_ada, "badat")
        ld(bgt[:], b_g, "bgt")
        ld(ctt[:], cT, "ctt")
        S.op("gpsimd", lambda e: e.dma_start(out=wgf[:], in_=w_g), writes=["wgf"], dma="wgf")
        S.op("vector", lambda e: e.tensor_copy(out=cstb[:], in_=cstf[:]), reads=["cstf"], writes=["cstb"])
        S.op("vector", lambda e: e.memset(onec[:], 1.0), writes=["onec"])
        S.op("vector", lambda e: e.memset(epsc[:], 1e-6), writes=["epsc"])
        identb = cstb[:, 0, :]
        onesb = cstb[:, 1, :]
        trif = cstf[:, 2, :]

        def wload(src_ap, kc, ncols):
            tile, key = WP.next()
            if ncols == 512:
                S.op("gpsimd", lambda e: e.dma_start(out=tile[:, 0:kc, :].rearrange("p k c -> p (k c)"),
                                                     in_=src_ap.rearrange("p k c -> p (k c)"), max_dma_last_dim=8192),
                     writes=[key], dma=key)
            else:
                S.op("gpsimd", lambda e: e.dma_start(out=tile[:, 0:kc, 0:ncols], in_=src_ap), writes=[key], dma=key)
            return tile, key

        S.op("scalar", lambda e: e.activation(out=csil[:], in_=ctt[:], func=AF.Silu), reads=["ctt"], writes=["csil"])
        for blk in range(12):
            wt, wk = wload(w_ada[:, :, blk * 512:(blk + 1) * 512], 8, 512)
            ps, pk = PS.next()
            for jj in range(4):
                for kc in range(8):
                    S.op("tensor", lambda e, jj=jj, kc=kc, wt=wt, ps=ps: e.matmul(
                        ps[:, jj * 17:(jj + 1) * 17], lhsT=wt[:, kc, jj * 128:(jj + 1) * 128], rhs=csil[:, kc, :],
                        start=(kc == 0), stop=(kc == 7)), reads=[wk, "csil"], writes=[pk], inc=(kc == 7 and jj == 3))
            for jj in range(4):
                j = blk * 4 + jj
                S.op("vector", lambda e, jj=jj, j=j, ps=ps: e.tensor_scalar(
                    out=MOD[:, j, :], in0=ps[:, jj * 17:(jj + 1) * 17], scalar1=badat[:, j:j + 1], scalar2=None,
                    op0=ALU.add), reads=[pk, "badat"], writes=["MOD"])
        for (dst, dk, gi, mj, plus1) in ((AM, "AM", 0, 8, True), (GM, "GM", 1, 16, False),
                                         (AFm, "AFm", 2, 32, True), (GFm, "GFm", 3, 40, False)):
            gb = bc_last(g4t[:, gi, :], 17)
            if plus1:
                S.op("vector", lambda e, dst=dst, mj=mj, gb=gb: e.scalar_tensor_tensor(
                    out=dst[:], in0=MOD[:, mj:mj + 8, :], scalar=1.0, in1=gb, op0=ALU.add, op1=ALU.mult),
                    reads=["MOD", "g4t"], writes=[dk])
            else:
                S.op("vector", lambda e, dst=dst, mj=mj, gb=gb: e.tensor_tensor(
                    out=dst[:], in0=MOD[:, mj:mj + 8, :], in1=gb, op=ALU.mult), reads=["MOD", "g4t"], writes=[dk])

        xt = sb("xt", [128, 8, T])
        sq = sb("sq", [128, 8, T], BF16)
        hT = sb("hT", [128, 8, T], BF16)
        rstd = sb("rstd", [128, T])
        tmpA = Ring([sb("tmpA%d" % i, [128, T]) for i in range(2)], "tmpA")
        gatT = sb("gatT", [128, 2, 16])
        gex = sb("gex", [128, 16])
        lfT = sb("lfT", [128, 2, 16])
        rawp = sb("rawp", [128, 8, T + 3])
        cacc = sb("cacc", [128, 8, T])
        ctmp = sb("ctmp", [128, 8, T])
        mqk = sb("mqk", [128, 8, T], BF16)
        kf = sb("kf", [128, 4, T])
        MV = sb("MV", [128, 2, 512], BF16)
        SBO = sb("SBO", [128, 4, T], BF16)
        hbT = sb("hbT", [128, 4, T], BF16)
        convw = sb("convw", [128, 8, 4])
        convb = sb("convb", [128, 8])
        flg = sb("flg", [128, 1])
        ld(convw[:], conv_w, "convw")
        ld(convb[:], conv_b, "convb")
        ld(flg[:], flag, "flg")
        a_sb = sb("a_sb", [128, 4]); b_sb = sb("b_sb", [128, 4]); w_sb = sb("w_sb", [128, 4]); wa_sb = sb("wa_sb", [128, 4])
        Ra = sb("Ra", [128, 4, 128])
        brow = sb("brow", [128, 512]); thr = sb("thr", [128, 512]); hd = brow; hd2 = thr
        amax = sb("amax", [128, 4, 16]); Mt = sb("Mt", [128, 4, 16]); dsi = sb("dsi", [128, 4, 16]); si = sb("si", [128, 4, 16])
        tmm = sb("tmm", [128, 4, 16]); Mtok = sb("Mtok", [128, 4])
        mkT = sb("mkT", [128, 4, 128], BF16)
        vp = sb("vp", [128, 4, 129], BF16); vpb = sb("vpb", [128, 4, 129], BF16)
        WM = Ra; AT = sb("AT", [128, 4, 128], BF16)
        CsR = Ring([sb("Cs%d" % i, [128, 4, 128], BF16) for i in range(2)], "Cs")
        nrR = Ring([sb("nr%d" % i, [128, 4, 128], BF16) for i in range(2)], "nr")
        S.op("vector", lambda e: e.memset(rawp[:], 0.0), writes=["rawp"])

        identf = cstf[:, 0, :]
        onesf = cstf[:, 1, :]
        ACC = [PS.tiles[4 + i] for i in range(4)]
        PS.tiles = PS.tiles[0:4]

        sp_r = sb("sp_r", [128, 8])
        sp_p = [sb("sp_p%d" % i, [128, 8], BF16) for i in range(3)]
        sq_p = [sb("sq_p%d" % i, [128, 8], BF16) for i in range(3)]
        RP = Ring([sb("RP%d" % i, [128, 4, 128], BF16) for i in range(2)], "RP")

        def split3(src, skey, rows, ncol, dst, dkey):
            V(lambda e: e.tensor_copy(out=dst[0][0:rows, 0:ncol], in_=src), [skey], [dkey + "0"])
            V(lambda e: e.tensor_tensor(out=sp_r[0:rows, 0:ncol], in0=src, in1=dst[0][0:rows, 0:ncol], op=ALU.subtract),
              [skey, dkey + "0"], ["sp_r"])
            V(lambda e: e.tensor_copy(out=dst[1][0:rows, 0:ncol], in_=sp_r[0:rows, 0:ncol]), ["sp_r"], [dkey + "1"])
            V(lambda e: e.tensor_tensor(out=sp_r[0:rows, 0:ncol], in0=sp_r[0:rows, 0:ncol], in1=dst[1][0:rows, 0:ncol], op=ALU.subtract),
              ["sp_r", dkey + "1"], ["sp_r"])
            V(lambda e: e.tensor_copy(out=dst[2][0:rows, 0:ncol], in_=sp_r[0:rows, 0:ncol]), ["sp_r"], [dkey + "2"])

        def cumsum_tm(ps, pk, TRIb, src, skey, rows, ncol):
            split3(src, skey, rows, ncol, sp_p, "sp_p")
            for j in range(3):
                S.op("tensor", lambda e, j=j: e.matmul(ps[0:rows, 0:ncol], lhsT=TRIb, rhs=sp_p[j][0:rows, 0:ncol],
                                                       start=(j == 0), stop=(j == 2)), reads=["sp_p%d" % j, "cstb"], writes=[pk], inc=(j == 2))

        def rowbc(ps, pk, src, skey, rows, h0, nh):
            for j in range(3):
                rp, rk = RP.next()
                V(lambda e, j=j, rp=rp: e.tensor_tensor(out=rp[0:rows, 0:nh, 0:rows], in0=bc_mid(identb[0:rows, 0:rows], nh),
                                                        in1=bc_last(src[j][0:rows, h0:h0 + nh], rows), op=ALU.mult),
                  [skey + str(j), "cstb"], [rk])
                S.op("tensor", lambda e, j=j, rp=rp: e.matmul(ps[:, 0:nh * rows].rearrange("p (h t) -> p h t", h=nh), lhsT=onesb[0:rows, :],
                                                             rhs=rp[0:rows, 0:nh, 0:rows], start=(j == 0), stop=(j == 2)),
                     reads=[rk, "cstb"], writes=[pk], inc=(j == 2))

        def bc_mid(ap, n):
            a = [list(x) for x in ap.ap]
            return bass.AP(ap.tensor, ap.offset, [a[0], [0, n]] + a[1:])

        def rms_rstd(src, skey, n):
            S.op("scalar", lambda e: e.activation(out=sq[:, :, 0:n], in_=src, func=AF.Square), reads=[skey], writes=["sq"])
            ps, pk = PS.next()
            for kc in range(8):
                S.op("tensor", lambda e, kc=kc, ps=ps: e.matmul(ps[:, 0:n], lhsT=onesb, rhs=sq[:, kc, 0:n],
                                                                start=(kc == 0), stop=(kc == 7)),
                     reads=["sq", "cstb"], writes=[pk], inc=(kc == 7))
            S.op("scalar", lambda e, ps=ps: e.activation(out=rstd[:, 0:n], in_=ps[:, 0:n], func=AF.Sqrt,
                                                         bias=epsc[:, 0:1], scale=1.0 / D),
                 reads=[pk, "epsc"], writes=["rstd"])
            S.op("vector", lambda e: e.reciprocal(out=rstd[:, 0:n], in_=rstd[:, 0:n]), reads=["rstd"], writes=["rstd"])

        def modulate(src, skey, n, nseg, L, s0, Amat, akey, sh_j):
            v3 = lambda a: a.rearrange("p (s l) -> p s l", l=L)
            for kc in range(8):
                tt, tk = tmpA.next()
                S.op("vector", lambda e, kc=kc, tt=tt: e.tensor_tensor(out=tt[:, 0:n], in0=src[:, kc, :], in1=rstd[:, 0:n],
                                                                       op=ALU.mult), reads=[skey, "rstd"], writes=[tk])
                S.op("vector", lambda e, kc=kc, tt=tt: e.tensor_tensor(
                    out=v3(tt[:, 0:n]), in0=v3(tt[:, 0:n]), in1=bc_last(Amat[:, kc, s0:s0 + nseg], L), op=ALU.mult),
                    reads=[tk, akey], writes=[tk])
                S.op("vector", lambda e, kc=kc, tt=tt: e.tensor_tensor(
                    out=v3(hT[:, kc, 0:n]), in0=v3(tt[:, 0:n]), in1=bc_last(MOD[:, sh_j + kc, s0:s0 + nseg], L), op=ALU.add),
                    reads=[tk, "MOD"], writes=["hT"])

        def lin_fm(wsrc, ncols, n, consume):
            for c0 in range(0, ncols, 512):
                w = min(512, ncols - c0)
                wt, wk = wload(wsrc[:, :, c0:c0 + w], 8, w)
                for cb in range(w // 128):
                    ps, pk = PS.next()
                    for kc in range(8):
                        S.op("tensor", lambda e, kc=kc, cb=cb, wt=wt, ps=ps: e.matmul(
                            ps[:, 0:n], lhsT=wt[:, kc, cb * 128:(cb + 1) * 128], rhs=hT[:, kc, 0:n],
                            start=(kc == 0), stop=(kc == 7)), reads=[wk, "hT"], writes=[pk], inc=(kc == 7))
                    consume(c0 // 128 + cb, ps, pk)

        def lin_tm(wsrc, n, consume):
            wt, wk = wload(wsrc, 8, 512)
            for tb in range((n + 127) // 128):
                rows = min(128, n - tb * 128)
                ps, pk = PS.next()
                for kc in range(8):
                    S.op("tensor", lambda e, kc=kc, tb=tb, rows=rows, wt=wt, ps=ps: e.matmul(
                        ps[0:rows, :], lhsT=hT[:, kc, tb * 128:tb * 128 + rows], rhs=wt[:, kc, :],
                        start=(kc == 0), stop=(kc == 7)), reads=[wk, "hT"], writes=[pk], inc=(kc == 7))
                consume(tb, rows, ps, pk)

        def store(dst, src, skey, chan):
            S.op("sync", lambda e: e.dma_start(out=dst, in_=src), reads=[skey], dma=chan)

        V = lambda fn, r, w: S.op("vector", fn, reads=r, writes=w)
        A = lambda fn, r, w: S.op("scalar", fn, reads=r, writes=w)

        def mchunk(Tc, nseg, L, tb, c0, TRI, IND, Cseg, nseg_ap, mst, full, TRIb):
            W4 = 4 * Tc
            i_ap = gatT[0:Tc, tb, 8:12]
            lf_ap = lfT[0:Tc, tb, 12:16]
            v4 = lambda a: a.rearrange("p (h s l) -> p h s l", h=4, l=L)
            ps, pk = PS.next()
            cumsum_tm(ps, pk, TRIb, lf_ap, "lfT", Tc, 4)
            V(lambda e: e.tensor_tensor(out=a_sb[0:Tc, :], in0=i_ap, in1=ps[0:Tc, 0:4], op=ALU.subtract), ["gatT", pk], ["a_sb"])
            A(lambda e: e.copy(out=b_sb[0:Tc, :], in_=ps[0:Tc, 0:4]), [pk], ["b_sb"])
            psA, pkA = PS.next()
            split3(a_sb[0:Tc, :], "a_sb", Tc, 4, sq_p, "sq_p")
            rowbc(psA, pkA, sq_p, "sq_p", Tc, 0, 4)
            V(lambda e: e.tensor_reduce(out=amax[:, :, 0:nseg], in_=v4(psA[:, 0:W4]), axis=AX.X, op=ALU.max), [pkA], ["amax"])
            psB, pkB = PS.next()
            split3(b_sb[0:Tc, :], "b_sb", Tc, 4, sq_p, "sq_p")
            rowbc(psB, pkB, sq_p, "sq_p", Tc, 0, 4)
            A(lambda e: e.copy(out=brow[:, 0:W4], in_=psB[:, 0:W4]), [pkB], ["brow"])
            blv = v4(brow[:, 0:W4])[:, :, :, L - 1]
            V(lambda e: e.tensor_tensor(out=Mt[:, :, 0:nseg], in0=mst, in1=amax[:, :, 0:nseg], op=ALU.max), ["mst", "amax"], ["Mt"])
            V(lambda e: e.tensor_tensor(out=dsi[:, :, 0:nseg], in0=mst, in1=Mt[:, :, 0:nseg], op=ALU.subtract), ["mst", "Mt"], ["dsi"])
            A(lambda e: e.activation(out=si[:, :, 0:nseg], in_=dsi[:, :, 0:nseg], func=AF.Exp), ["dsi"], ["si"])
            V(lambda e: e.tensor_tensor(out=mst, in0=blv, in1=Mt[:, :, 0:nseg], op=ALU.add), ["brow", "Mt"], ["mst"])
            V(lambda e: e.tensor_tensor(out=tmm[0:Tc, :, 0:nseg], in0=Mt[0:Tc, :, 0:nseg], in1=bc_mid(IND, 4), op=ALU.mult),
              ["Mt", "cstf"], ["tmm"])
            V(lambda e: e.tensor_reduce(out=Mtok[0:Tc, :], in_=tmm[0:Tc, :, 0:nseg], axis=AX.X, op=ALU.add), ["tmm"], ["Mtok"])
            V(lambda e: e.tensor_tensor(out=wa_sb[0:Tc, :], in0=a_sb[0:Tc, :], in1=Mtok[0:Tc, :], op=ALU.subtract),
              ["a_sb", "Mtok"], ["wa_sb"])
            A(lambda e: e.activation(out=w_sb[0:Tc, :], in_=wa_sb[0:Tc, :], func=AF.Exp), ["wa_sb"], ["w_sb"])
            psT, pkT = PS.next()
            for h in range(4):
                S.op("tensor", lambda e, h=h: e.transpose(out=psT[0:Tc, h * 128:(h + 1) * 128], in_=kf[:, h, c0:c0 + Tc], identity=identf),
                     reads=["kf", "cstf"], writes=[pkT], inc=(h == 3))
            A(lambda e: e.copy(out=mkT[0:Tc, :, :], in_=psT[0:Tc, :].rearrange("p (h d) -> p h d", h=4)), [pkT], ["mkT"])
            V(lambda e: e.tensor_tensor(out=vp[0:Tc, :, 0:128], in0=MV[0:Tc, tb, :].rearrange("p (h e) -> p h e", h=4),
                                        in1=bc_last(w_sb[0:Tc, :], 128), op=ALU.mult), ["MV", "w_sb"], ["vp"])
            V(lambda e: e.tensor_copy(out=vp[0:Tc, :, 128], in_=w_sb[0:Tc, :]), ["w_sb"], ["vp"])
            if full:
                V(lambda e: e.tensor_tensor(out=v4(thr[:, 0:W4]), in0=v4(brow[:, 0:W4]), in1=bc_last(Mt[:, :, 0:nseg], L), op=ALU.add),
                  ["brow", "Mt"], ["thr"])
                A(lambda e: e.activation(out=thr[:, 0:W4], in_=thr[:, 0:W4], func=AF.Exp, scale=-1.0), ["thr"], ["thr"])
                psQ, pkQ = PS.next()
                for h in range(4):
                    S.op("tensor", lambda e, h=h: e.matmul(psQ[0:Tc, h * Tc:(h + 1) * Tc], lhsT=mqk[:, 4 + h, c0:c0 + Tc],
                                                           rhs=mqk[:, h, c0:c0 + Tc], start=True, stop=True),
                         reads=["mqk"], writes=[pkQ], inc=(h == 3))
                V(lambda e: e.tensor_tensor(out=WM[0:Tc, :, 0:Tc], in0=bc_mid(TRI, 4), in1=bc_last(w_sb[0:Tc, :], Tc), op=ALU.mult),
                  ["w_sb", "cstf"], ["Ra"])
                V(lambda e: e.tensor_tensor(out=AT[0:Tc, :, 0:Tc], in0=psQ[0:Tc, 0:W4].rearrange("p (h t) -> p h t", h=4),
                                            in1=WM[0:Tc, :, 0:Tc], op=ALU.mult), [pkQ, "Ra"], ["AT"])
                num, den = ACC[0], ACC[1]
                V(lambda e: e.memset(num[:, :], 0.0), [], ["acc0"])
                V(lambda e: e.memset(den[:, :], 0.0), [], ["acc1"])
                for h in range(4):
                    S.op("tensor", lambda e, h=h: e.matmul(num[:, h * Tc:(h + 1) * Tc], lhsT=MV[0:Tc, tb, h * 128:(h + 1) * 128],
                                                           rhs=AT[0:Tc, h, 0:Tc], start=False, stop=False, skip_group_check=True),
                         reads=["MV", "AT"], writes=["acc0"], inc=False)
                    S.op("tensor", lambda e, h=h: e.matmul(den[:, h * Tc:(h + 1) * Tc], lhsT=onesb[0:Tc, :],
                                                           rhs=AT[0:Tc, h, 0:Tc], start=False, stop=False, skip_group_check=True),
                         reads=["cstb", "AT"], writes=["acc1"], inc=False)
                for b in range(nseg):
                    Cs, ck = CsR.next()
                    nr, nk = nrR.next()
                    V(lambda e, b=b, Cs=Cs: e.tensor_tensor(out=Cs[:], in0=Cseg(b), in1=bc_last(si[:, :, b], 128), op=ALU.mult),
                      ["Cst", "si"], [ck])
                    V(lambda e, b=b, nr=nr: e.tensor_tensor(out=nr[:], in0=bc_last(nseg_ap(b), 128), in1=bc_last(si[:, :, b], 128),
                                                            op=ALU.mult), ["nst", "si"], [nk])
                    for h in range(4):
                        last = (b == nseg - 1 and h == 3)
                        cs0 = h * Tc + b * L
                        S.op("tensor", lambda e, h=h, b=b, Cs=Cs, cs0=cs0: e.matmul(
                            num[:, cs0:cs0 + L], lhsT=Cs[:, h, :], rhs=mqk[:, h, c0 + b * L:c0 + (b + 1) * L], start=False, stop=True,
                            skip_group_check=True),
                            reads=[ck, "mqk"], writes=["acc0"], inc=False)
                        S.op("tensor", lambda e, h=h, b=b, nr=nr, cs0=cs0: e.matmul(
                            den[:, cs0:cs0 + L], lhsT=nr[:, h, :], rhs=mqk[:, h, c0 + b * L:c0 + (b + 1) * L], start=False, stop=True,
                            skip_group_check=True),
                            reads=[nk, "mqk"], writes=["acc1"], inc=(h == 3))
                A(lambda e: e.activation(out=hd[:, 0:W4], in_=den[:, 0:W4], func=AF.Abs), ["acc1"], ["brow"])
                V(lambda e: e.tensor_tensor(out=hd[:, 0:W4], in0=hd[:, 0:W4], in1=thr[:, 0:W4], op=ALU.max), ["brow", "thr"], ["brow"])
                V(lambda e: e.reciprocal(out=hd[:, 0:W4], in_=hd[:, 0:W4]), ["brow"], ["brow"])
                V(lambda e: e.tensor_tensor(out=hd2[:, 0:W4], in0=num[:, 0:W4], in1=hd[:, 0:W4], op=ALU.mult), ["acc0", "brow"], ["thr"])
                V(lambda e: e.tensor_tensor(out=hbT[:, :, c0:c0 + Tc], in0=hd2[:, 0:W4].rearrange("p (h t) -> p h t", h=4),
                                            in1=SBO[:, :, c0:c0 + Tc], op=ALU.mult), ["thr", "SBO"], ["hbT"])
            for b in range(nseg):
                if nseg > 1:
                    V(lambda e, b=b: e.tensor_scalar(out=vpb[0:Tc, :, :], in0=vp[0:Tc, :, :], scalar1=IND[:, b:b + 1], scalar2=None,
                                                     op0=ALU.mult), ["vp", "cstf"], ["vpb"])
                    vv, vk = vpb, "vpb"
                else:
                    vv, vk = vp, "vp"
                psC, pkC = PS.next()
                for h in range(4):
                    S.op("tensor", lambda e, h=h, vv=vv: e.matmul(psC[:, h * 128:(h + 1) * 128], lhsT=mkT[0:Tc, h, :], rhs=vv[0:Tc, h, 0:128],
                                                                  start=True, stop=True), reads=["mkT", vk], writes=[pkC], inc=(h == 3))
                psN, pkN = PS.next()
                for h in range(4):
                    S.op("tensor", lambda e, h=h, vv=vv: e.matmul(psN[:, h:h + 1], lhsT=mkT[0:Tc, h, :], rhs=vv[0:Tc, h, 128:129],
                                                                  start=True, stop=True), reads=["mkT", vk], writes=[pkN], inc=(h == 3))
                V(lambda e, b=b: e.tensor_tensor(out=Cseg(b), in0=Cseg(b), in1=bc_last(si[:, :, b], 128), op=ALU.mult), ["Cst", "si"], ["Cst"])
                V(lambda e, b=b: e.tensor_tensor(out=Cseg(b), in0=Cseg(b), in1=psC[:, :].rearrange("p (h e) -> p h e", h=4), op=ALU.add),
                  ["Cst", pkC], ["Cst"])
                V(lambda e, b=b: e.tensor_tensor(out=nseg_ap(b), in0=nseg_ap(b), in1=si[:, :, b], op=ALU.mult), ["nst", "si"], ["nst"])
                V(lambda e, b=b: e.tensor_tensor(out=nseg_ap(b), in0=nseg_ap(b), in1=psN[:, 0:4], op=ALU.add), ["nst", pkN], ["nst"])

        cst2b = sb("cst2b", [128, 3, 256], BF16)
        S.op("gpsimd", lambda e: e.dma_start(out=cst2b[:], in_=cst2), writes=["cst2b"], dma="cst2b")
        MASKD = lambda j: cst2b[:, j, 0:T]
        MASKS = cst2b[:, 2, 0:TS]
        AUGK = cst2b[:, 2, 64:192]
        iotaf = cstf[:, 4, 127:128]
        FLAGM = sb("FLAGM", [128, 1])
        V(lambda e: e.tensor_scalar(out=FLAGM[:], in0=flg[:], scalar1=-1.0, scalar2=30000.0, op0=ALU.add, op1=ALU.mult), ["flg"], ["FLAGM"])
        BIG = sb("BIG", [128, 32 * T], BF16)
        SGA = BIG[:, 0:8 * T].rearrange("p (c t) -> p c t", c=8)
        SGB = BIG[:, 8 * T:16 * T].rearrange("p (c t) -> p c t", c=8)
        YPRE = BIG[:, 16 * T:24 * T].rearrange("p (c t) -> p c t", c=8)
        attT = BIG[:, 24 * T:28 * T].rearrange("p (c t) -> p c t", c=4)
        QTp = BIG[:, 28 * T:32 * T].rearrange("p (c t) -> p c t", c=4)
        UT = BIG[:, :].rearrange("p (c t) -> p c t", c=32)
        BIGK = ["SGA", "SGB", "YPRE", "attT", "QTp"]
        Yf = cacc_alias = None
        Yf = ctmp
        x1 = cacc
        Ff = cacc[0:64, :, :]
        cs_sb = sb("cs_sb", [128, 8]); tot_sb = sb("tot_sb", [128, 8])
        carry = sb("carry", [128, 8]); crel = sb("crel", [128, 8]); Rt = sb("Rt", [128, 8])
        rec = sb("rec", [128, T])
        utmp = tmpA

        def lin_gen(wsrc, nk, ncols, n, rhs_fn, rkeys, consume):
            for c0_ in range(0, ncols, 512):
                w = min(512, ncols - c0_)
                wt, wk = wload(wsrc[:, :, c0_:c0_ + w], nk, w)
                for cb in range(w // 128):
                    ps, pk = PS.next()
                    for kc in range(nk):
                        S.op("tensor", lambda e, kc=kc, cb=cb, wt=wt, ps=ps: e.matmul(
                            ps[:, 0:n], lhsT=wt[:, kc, cb * 128:(cb + 1) * 128], rhs=rhs_fn(kc),
                            start=(kc == 0), stop=(kc == nk - 1)), reads=[wk] + rkeys, writes=[pk], inc=(kc == nk - 1))
                    consume(c0_ // 128 + cb, ps, pk)

        def w_in_phase(mode, xsrc, n, nseg, L, s0, kT_dst, v_dst, lf_dst, tok0, kcol0, KT, Vr, vnew):
            own = mode != "pre"
            smp = mode == "smp"
            ld(xt[:, :, 0:n], xsrc, "xt")
            rms_rstd(xt[:, :, 0:n], "xt", n)
            modulate(xt[:, :, 0:n], "xt", n, nseg, L, s0, AM, "AM", 0)

            def k_cons(cb, ps, pk):
                if "K" not in KSKIP:
                    V(lambda e: e.tensor_copy(out=KT[:, cb, kcol0:kcol0 + n], in_=ps[:, 0:n]), [pk], ["KT"])
                if own:
                    sg, sk = STG.next()
                    A(lambda e: e.copy(out=sg[:, 0:n], in_=ps[:, 0:n]), [pk], [sk])
                    store(kT_dst[cb * 128:(cb + 1) * 128, tok0:tok0 + n], sg[:, 0:n], sk, "st_" + sk)
            lin_fm(w_in[:, :, C_AK:C_AK + 512], 512, n, k_cons)

            def v_cons(tb, rows, ps, pk):
                if smp:
                    A(lambda e: e.copy(out=vnew[0:rows, :], in_=ps[0:rows, :]), [pk], ["vnew"])
                    store(v_dst[tok0 + tb * 128:tok0 + tb * 128 + rows, :], vnew[0:rows, :], "vnew", "st_vnew")
                    return
                if "K" not in KSKIP:
                    V(lambda e: e.tensor_copy(out=Vr[0:rows, kcol0 // 128 + tb, :], in_=ps[0:rows, :]), [pk], ["Vr"])
                if own:
                    sg, sk = STG.next()
                    A(lambda e: e.copy(out=sg[0:rows, :], in_=ps[0:rows, :]), [pk], [sk])
                    store(v_dst[tok0 + tb * 128:tok0 + tb * 128 + rows, :], sg[0:rows, :], sk, "st_" + sk)
            lin_tm(w_in[:, :, C_AV:C_AV + 512], n, v_cons)

            for tb in range((n + 127) // 128):
                rows = min(128, n - tb * 128)
                ps, pk = PS.next()
                for kc in range(8):
                    S.op("tensor", lambda e, kc=kc, tb=tb, rows=rows, ps=ps: e.matmul(
                        ps[0:rows, 0:16], lhsT=hT[:, kc, tb * 128:tb * 128 + rows], rhs=wgf[:, kc, :],
                        start=(kc == 0), stop=(kc == 7)), reads=["wgf", "hT"], writes=[pk], inc=(kc == 7))
                V(lambda e, rows=rows, ps=ps, tb=tb: e.tensor_tensor(out=gatT[0:rows, tb, :], in0=ps[0:rows, 0:16],
                                                                     in1=bgt[0:rows, :], op=ALU.add), [pk, "bgt"], ["gatT"])
                A(lambda e, rows=rows, tb=tb: e.activation(out=gex[0:rows, :], in_=gatT[0:rows, tb, :], func=AF.Exp, scale=-1.0),
                  ["gatT"], ["gex"])
                A(lambda e, rows=rows: e.activation(out=gex[0:rows, :], in_=gex[0:rows, :], func=AF.Ln,
                                                    bias=onec[0:rows, 0:1], scale=1.0), ["gex", "onec"], ["gex"])
                V(lambda e, rows=rows, tb=tb: e.tensor_scalar(out=lfT[0:rows, tb, :], in0=gex[0:rows, :], scalar1=-1.0,
                                                              scalar2=None, op0=ALU.mult), ["gex"], ["lfT"])
                if own:
                    store(lf_dst[tok0 + tb * 128:tok0 + tb * 128 + rows, :], lfT[0:rows, tb, 0:8], "lfT", "st_lfT")

            if nseg == 1:
                rv = lambda cb: rawp[:, cb, 3:3 + n]
                pv = lambda ps: ps[:, 0:n]
            else:
                r4 = rawp[:, :, 0:nseg * (L + 3)].rearrange("p c (s l) -> p c s l", l=L + 3)
                rv = lambda cb: r4[:, cb, :, 3:3 + L]
                pv = lambda ps: ps[:, 0:n].rearrange("p (s l) -> p s l", l=L)

            def r_cons(cb, ps, pk):
                A(lambda e: e.copy(out=rv(cb), in_=pv(ps)), [pk], ["rawp"])
            lin_fm(w_in[:, :, C_BQ:C_BQ + 1024], 1024, n, r_cons)
            if nseg == 1:
                win = lambda j: rawp[:, :, j:j + n]
                wj = lambda j: bc_last(convw[:, :, j], n)
                o3 = lambda t: t[:, :, 0:n]
            else:
                win = lambda j: r4[:, :, :, j:j + L]
                wj = lambda j: bc_last(bc_last(convw[:, :, j], nseg), L)
                o3 = lambda t: t[:, :, 0:n].rearrange("p c (s l) -> p c s l", l=L)
            V(lambda e: e.tensor_tensor(out=o3(cacc), in0=win(0), in1=wj(0), op=ALU.mult), ["rawp", "convw"], ["cacc"])
            for j in range(1, 4):
                V(lambda e, j=j: e.tensor_tensor(out=o3(ctmp), in0=win(j), in1=wj(j), op=ALU.mult), ["rawp", "convw"], ["ctmp"])
                V(lambda e: e.tensor_tensor(out=o3(cacc), in0=o3(cacc), in1=o3(ctmp), op=ALU.add), ["cacc", "ctmp"], ["cacc"])
            for cb in range(4):
                A(lambda e, cb=cb: e.activation(out=mqk[:, cb, 0:n], in_=cacc[:, cb, 0:n], func=AF.Silu, bias=convb[:, cb:cb + 1]),
                  ["cacc", "convb"], ["mqk"])
            for cb in range(4, 8):
                A(lambda e, cb=cb: e.activation(out=kf[:, cb - 4, 0:n], in_=cacc[:, cb, 0:n], func=AF.Silu, bias=convb[:, cb:cb + 1]),
                  ["cacc", "convb"], ["kf"])
            V(lambda e: e.tensor_scalar(out=kf[:, :, 0:n], in0=kf[:, :, 0:n], scalar1=float(B_D) ** -0.5, scalar2=None, op0=ALU.mult),
              ["kf"], ["kf"])
            V(lambda e: e.tensor_copy(out=mqk[:, 4:8, 0:n], in_=kf[:, :, 0:n]), ["kf"], ["mqk"])

            def mv_cons(tb, rows, ps, pk):
                A(lambda e: e.copy(out=MV[0:rows, tb, :], in_=ps[0:rows, :]), [pk], ["MV"])
            lin_tm(w_in[:, :, C_BV:C_BV + 512], n, mv_cons)
            if own:
                def bo_cons(cb, ps, pk):
                    A(lambda e: e.activation(out=SBO[:, cb, 0:n], in_=ps[:, 0:n], func=AF.Sigmoid), [pk], ["SBO"])
                lin_fm(w_in[:, :, C_BO:C_BO + 512], 512, n, bo_cons)

                def q_cons(cb, ps, pk):
                    V(lambda e: e.tensor_scalar(out=QTp[:, cb, 0:n], in0=ps[:, 0:n], scalar1=float(A_D) ** -0.5, scalar2=None,
                                                op0=ALU.mult), [pk], ["QTp"])
                if "K" not in KSKIP:
                    lin_fm(w_in[:, :, C_AQ:C_AQ + 512], 512, n, q_cons)

        def fox_block(tb, kb, own):
            if "F" in KSKIP:
                return
            ps, pk = PS.next()
            cumsum_tm(ps, pk, cstb[:, 2, :], lfT[:, tb, 0:8], "lfT", 128, 8)
            A(lambda e: e.copy(out=cs_sb[:], in_=ps[:, 0:8]), [pk], ["cs_sb"])
            V(lambda e: e.scalar_tensor_tensor(out=NFk[:, kb, :], in0=cs_sb[:], scalar=-1.0, in1=carry[:], op0=ALU.mult, op1=ALU.subtract),
              ["cs_sb", "carry"], ["NFk"])
            split3(cs_sb[:], "cs_sb", 128, 8, sq_p, "sq_p")
            for g in range(2):
                pr, prk = PS.next()
                rowbc(pr, prk, sq_p, "sq_p", 128, 4 * g, 4)
                pr3 = pr[:, :].rearrange("p (h t) -> p h t", h=4)
                A(lambda e, g=g, pr3=pr3: e.copy(out=tot_sb[:, 4 * g:4 * g + 4], in_=pr3[:, :, 127]), [prk], ["tot_sb"])
                if own:
                    V(lambda e, g=g, pr3=pr3: e.tensor_tensor(out=Ff[:, 4 * g:4 * g + 4, tb * 128:(tb + 1) * 128], in0=pr3[0:64, :, :],
                                                              in1=bc_last(crel[0:64, 4 * g:4 * g + 4], 128), op=ALU.add),
                      [prk, "crel"], ["cacc"])
            V(lambda e: e.tensor_tensor(out=crel[:], in0=crel[:], in1=tot_sb[:], op=ALU.add), ["crel", "tot_sb"], ["crel"])
            V(lambda e: e.tensor_tensor(out=carry[:], in0=carry[:], in1=tot_sb[:], op=ALU.add), ["carry", "tot_sb"], ["carry"])

        def post_mix(n, nseg, L, s0, xsrc, y_dst, tok0):
            v3 = lambda a: a.rearrange("p (s l) -> p s l", l=L)
            def ga_cons(cb, ps, pk):
                A(lambda e: e.activation(out=SGA[:, cb, 0:n], in_=ps[:, 0:n], func=AF.Sigmoid), [pk], ["SGA"])
            lin_fm(w_in[:, :, C_GA:C_GA + 1024], 1024, n, ga_cons)
            def gb_cons(cb, ps, pk):
                A(lambda e: e.activation(out=SGB[:, cb, 0:n], in_=ps[:, 0:n], func=AF.Sigmoid), [pk], ["SGB"])
            lin_fm(w_in[:, :, C_GB:C_GB + 1024], 1024, n, gb_cons)
            def pa_cons(cb, ps, pk):
                V(lambda e: e.tensor_tensor(out=Yf[:, cb, 0:n], in0=ps[:, 0:n], in1=SGA[:, cb, 0:n], op=ALU.mult), [pk, "SGA"], ["ctmp"])
            lin_gen(w_pa, 4, 1024, n, lambda kc: attT[:, kc, 0:n], ["attT"], pa_cons)
            def pb_cons(cb, ps, pk):
                tt, tk = utmp.next()
                V(lambda e: e.tensor_tensor(out=tt[:, 0:n], in0=ps[:, 0:n], in1=SGB[:, cb, 0:n], op=ALU.mult), [pk, "SGB"], [tk])
                V(lambda e: e.tensor_tensor(out=YPRE[:, cb, 0:n], in0=tt[:, 0:n], in1=Yf[:, cb, 0:n], op=ALU.add), [tk, "ctmp"], ["YPRE"])
            lin_gen(w_pb, 4, 1024, n, lambda kc: hbT[:, kc, 0:n], ["hbT"], pb_cons)
            def wo_cons(cb, ps, pk):
                A(lambda e: e.copy(out=Yf[:, cb, 0:n], in_=ps[:, 0:n]), [pk], ["ctmp"])
            lin_gen(w_o, 8, 1024, n, lambda kc: YPRE[:, kc, 0:n], ["YPRE"], wo_cons)
            rms_rstd(Yf[:, :, 0:n], "ctmp", n)
            ld(xt[:, :, 0:n], xsrc, "xt")
            for kc in range(8):
                tt, tk = utmp.next()
                V(lambda e, kc=kc, tt=tt: e.tensor_tensor(out=tt[:, 0:n], in0=Yf[:, kc, 0:n], in1=rstd[:, 0:n], op=ALU.mult),
                  ["ctmp", "rstd"], [tk])
                V(lambda e, kc=kc, tt=tt: e.tensor_tensor(out=v3(tt[:, 0:n]), in0=v3(tt[:, 0:n]), in1=bc_last(GM[:, kc, s0:s0 + nseg], L),
                                                          op=ALU.mult), [tk, "GM"], [tk])
                V(lambda e, kc=kc, tt=tt: e.tensor_tensor(out=x1[:, kc, 0:n], in0=tt[:, 0:n], in1=xt[:, kc, 0:n], op=ALU.add),
                  [tk, "xt"], ["cacc"])
            rms_rstd(x1[:, :, 0:n], "cacc", n)
            modulate(x1[:, :, 0:n], "cacc", n, nseg, L, s0, AFm, "AFm", 24)
            def up_cons(cb, ps, pk):
                tt, tk = utmp.next()
                A(lambda e: e.activation(out=tt[:, 0:n], in_=ps[:, 0:n], func=AF.Relu), [pk], [tk])
                V(lambda e: e.tensor_tensor(out=UT[:, cb, 0:n], in0=tt[:, 0:n], in1=tt[:, 0:n], op=ALU.mult), [tk] + BIGK, BIGK)
            lin_fm(w_u, 4096, n, up_cons)
            for half in range(2):
                for kg in range(4):
                    wt, wk = wload(w_d[:, half * 4 + kg, :, :], 8, 512)
                    for cb in range(4):
                        for kc in range(8):
                            S.op("tensor", lambda e, kc=kc, cb=cb, wt=wt, kg=kg: e.matmul(
                                ACC[cb][:, 0:n], lhsT=wt[:, kc, cb * 128:(cb + 1) * 128], rhs=UT[:, kg * 8 + kc, 0:n],
                                start=(kg == 0 and kc == 0), stop=(kg == 3 and kc == 7)),
                                reads=[wk] + BIGK, writes=["acc%d" % cb], inc=(kc == 7))
                for cb in range(4):
                    A(lambda e, cb=cb, half=half: e.copy(out=Yf[:, half * 4 + cb, 0:n], in_=ACC[cb][:, 0:n]), ["acc%d" % cb], ["ctmp"])
            rms_rstd(Yf[:, :, 0:n], "ctmp", n)
            for kc in range(8):
                tt, tk = utmp.next()
                V(lambda e, kc=kc, tt=tt: e.tensor_tensor(out=tt[:, 0:n], in0=Yf[:, kc, 0:n], in1=rstd[:, 0:n], op=ALU.mult),
                  ["ctmp", "rstd"], [tk])
                V(lambda e, kc=kc, tt=tt: e.tensor_tensor(out=v3(tt[:, 0:n]), in0=v3(tt[:, 0:n]), in1=bc_last(GFm[:, kc, s0:s0 + nseg], L),
                                                          op=ALU.mult), [tk, "GFm"], [tk])
                V(lambda e, kc=kc, tt=tt: e.tensor_tensor(out=xt[:, kc, 0:n], in0=tt[:, 0:n], in1=x1[:, kc, 0:n], op=ALU.add),
                  [tk, "cacc"], ["xt"])
            store(y_dst[:, :, tok0:tok0 + n], xt[:, :, 0:n], "xt", "st_xt")

        with contextlib.ExitStack() as sst:
            def sbs(name, shape, dt=F32):
                return sst.enter_context(nc.sbuf_tensor(name, list(shape), dt))
            CstS = sbs("CstS", [128, NSEQ, 4, 128])
            nstS = sbs("nstS", [128, NSEQ, 4])
            mstS = sbs("mstS", [128, NSEQ, 4])
            cvst = sbs("cvst", [128, 8, NSEQ, 3])
            KTs = sbs("KTs", [128, 4, TS], BF16)
            vnew = sbs("vnew", [TS, 512])
            qtm = sbs("qtm", [TS, 512])
            ptb = sbs("ptb", [128, NSEQ * 16], I32)
            ptf = sbs("ptf", [128, NSEQ * 16])
            idx = sbs("idx", [128, NSEQ * 16], I32)
            KgR = Ring([sbs("Kg%d" % i, [128, 4, 512]) for i in range(2)], "Kg")
            Vg = sbs("Vg", [128, 4, 512])
            prod = CstS[:, 0:4, :, :].rearrange("p s h e -> p s (h e)")
            QR = Ring([sbs("qrow%d" % i, [128, 512]) for i in range(2)], "qrow")
            lfpg = sbs("lfpg", [128, 16, 8])
            XA = sbs("XA", [128, 17, 8]); XB = sbs("XB", [128, 17, 8])
            Gb = sbs("Gb", [128, 16, 8])
            Sx = sbs("Sx", [128, 512]); Pp = sbs("Pp", [128, 512]); Pr = sbs("Pr", [128, 32])
            Pnew = sbs("Pnew", [TS, 8, TS])
            csn = sbs("csn", [TS, 8])
            ld(CstS[:], sC, "Cst")
            ld(nstS[:], sN, "nst")
            ld(mstS[:], sM.partition_broadcast(128), "mst")
            if "I" not in KSKIP:
                ld(ptb[:], pt.partition_broadcast(128), "ptb")
                V(lambda e: e.tensor_copy(out=ptf[:], in_=ptb[:]), ["ptb"], ["ptf"])
                V(lambda e: e.tensor_scalar(out=ptf[:], in0=ptf[:], scalar1=128.0, scalar2=iotaf, op0=ALU.mult, op1=ALU.add),
                  ["ptf", "cstf"], ["ptf"])
                V(lambda e: e.tensor_copy(out=idx[:], in_=ptf[:]), ["ptf"], ["idx"])
            r4s = rawp[:, :, 0:NSEQ * 7].rearrange("p c (s l) -> p c s l", l=7)
            ld(cvst[:], sCV, "cvst")
            V(lambda e: e.tensor_copy(out=r4s[:, :, :, 0:3], in_=cvst[:]), ["cvst", "rawp"], ["rawp"])
            w_in_phase("smp", xT_smp, TS, NSEQ, 4, 1, kT_s, v_s, lf_s, 0, 0, KTs, None, vnew)
            store(cv_s, rawp[:, :, 0:NSEQ * 7], "rawp", "st_rawp")
            mchunk(TS, NSEQ, 4, 0, 0, cstf[0:TS, 3, 0:TS], cstf[0:TS, 4, 0:NSEQ], lambda b: CstS[:, b, :, :], lambda b: nstS[:, b, :],
                   mstS[:].rearrange("p s h -> p h s"), True, cstb[0:TS, 3, 0:TS])
            store(C_s, CstS[:].rearrange("p s h e -> p (s h) e"), "Cst", "st_Cst")
            store(n_s, nstS[:].rearrange("p s h -> p (s h)"), "nst", "st_nst")
            store(m_s, mstS[0:1, :, :].rearrange("p s h -> p (s h)"), "mst", "st_mst")
            if "S" not in KSKIP:
                def qt_cons(tb, rows, ps, pk):
                    V(lambda e: e.tensor_scalar(out=qtm[0:rows, :], in0=ps[0:rows, :], scalar1=float(A_D) ** -0.5, scalar2=None, op0=ALU.mult),
                      [pk], ["qtm"])
                lin_tm(w_in[:, :, C_AQ:C_AQ + 512], TS, qt_cons)
                S.op("sync", lambda e: e.dma_start(out=qscr, in_=qtm[:]), reads=["qtm"], writes=["qscr"], dma="st_qscr")
                ps, pk = PS.next()
                cumsum_tm(ps, pk, cstb[0:TS, 3, 0:TS], lfT[0:TS, 0, 0:8], "lfT", TS, 8)
                V(lambda e: e.tensor_scalar(out=csn[:], in0=ps[0:TS, 0:8], scalar1=-1.0, scalar2=None, op0=ALU.mult), [pk], ["csn"])
                for h in range(8):
                    pair, r0 = h // 2, (h % 2) * 64
                    ps, pk = PS.next()
                    S.op("tensor", lambda e, ps=ps, pair=pair, r0=r0: e.matmul(ps[0:TS, 0:TS], lhsT=KTs[r0:r0 + 64, pair, :],
                                                                               rhs=QTp[r0:r0 + 64, pair, 0:TS], start=True, stop=False),
                         reads=["KT", "QTp"], writes=[pk], inc=False)
                    S.op("tensor", lambda e, ps=ps: e.matmul(ps[0:TS, 0:TS], lhsT=identb[:, 0:TS], rhs=MASKS, start=False, stop=True),
                         reads=["cstb", "cst2b"], writes=[pk])
                    A(lambda e, ps=ps, h=h: e.activation(out=Pnew[:, h, :], in_=ps[0:TS, 0:TS], func=AF.Exp, bias=csn[:, h:h + 1]),
                      [pk, "csn"], ["Pnew"])
                NUMs, DENs = ACC[0], ACC[1]
                LFK = ["lfpg%d" % i for i in range(16)]
                VGK = ["Vg%d" % i for i in range(4)]
                V(lambda e: e.memset(NUMs[:, :], 0.0), [], ["acc0"])
                V(lambda e: e.memset(DENs[:, :], 0.0), [], ["acc1"])
                Sx4 = Sx[:, :].rearrange("p (t h g) -> p t h g", t=4, h=8)
                Pp4 = Pp[:, :].rearrange("p (t h g) -> p t h g", t=4, h=8)
                for b in range(NSEQ if USE_CACHE else 0):
                    qrs = []
                    for pg in range(16):
                        c = b * 16 + pg
                        S.op("gpsimd", lambda e, pg=pg, c=c: e.indirect_dma_start(
                            out=lfpg[:, pg, :], out_offset=None, in_=cache_lf,
                            in_offset=bass.IndirectOffsetOnAxis(ap=idx[:, c:c + 1], axis=0)), reads=["idx"], writes=["lfpg%d" % pg], dma="lfpg%d" % pg)
                    psg, pkg = PS.next()
                    S.op("tensor", lambda e, psg=psg: e.matmul(psg[:, 0:128], lhsT=cstf[:, 1, :], rhs=lfpg[:].rearrange("p g h -> p (g h)"),
                                                               start=True, stop=True), reads=LFK + ["cstf"], writes=[pkg])
                    V(lambda e: e.memset(XA[:], 0.0), ["XA"], ["XA"])
                    V(lambda e, psg=psg: e.tensor_copy(out=XA[:, 0:16, :], in_=psg[:, 0:128].rearrange("p (g h) -> p g h", h=8)), [pkg, "XA"], ["XA"])
                    src, dst, sk_, dk_ = XA, XB, "XA", "XB"
                    for sh in (1, 2, 4, 8):
                        V(lambda e, src=src, dst=dst: e.tensor_copy(out=dst[:], in_=src[:]), [sk_], [dk_])
                        V(lambda e, src=src, dst=dst, sh=sh: e.tensor_tensor(out=dst[:, 0:16 - sh, :], in0=src[:, 0:16 - sh, :],
                                                                             in1=src[:, sh:16, :], op=ALU.add), [sk_, dk_], [dk_])
                        src, dst, sk_, dk_ = dst, src, dk_, sk_
                    psw, pkw = PS.next()
                    S.op("tensor", lambda e, psw=psw: e.matmul(psw[:, 0:128], lhsT=cstf[:, 5, :], rhs=lfpg[:].rearrange("p g h -> p (g h)"),
                                                               start=True, stop=True), reads=LFK + ["cstf"], writes=[pkw])
                    V(lambda e, psw=psw, src=src: e.tensor_tensor(out=Gb[:], in0=psw[:, 0:128].rearrange("p (g h) -> p g h", h=8),
                                                                  in1=src[:, 1:17, :], op=ALU.add), [pkw, sk_], ["Gb"])
                    for g in range(4):
                        Kg, kgk = KgR.next()
                        KGK = [kgk + "_%d" % j for j in range(4)]
                        for j in range(4):
                            c = b * 16 + g * 4 + j
                            S.op("gpsimd", lambda e, j=j, c=c, Kg=Kg: e.indirect_dma_start(
                                out=Kg[:, j, :], out_offset=None, in_=cache_k,
                                in_offset=bass.IndirectOffsetOnAxis(ap=idx[:, c:c + 1], axis=0)), reads=["idx"], writes=[KGK[j]], dma=KGK[j])
                        for t in range(4):
                            qr, qk_ = QR.next()
                            S.op("sync", lambda e, b=b, t=t, qr=qr: e.dma_start(out=qr[:], in_=qscr[4 * b + t:4 * b + t + 1, :].partition_broadcast(128)),
                                 reads=["qscr"], writes=[qk_], dma=qk_)
                            V(lambda e, qr=qr, Kg=Kg: e.tensor_tensor(out=prod[:], in0=Kg[:], in1=bc_mid(qr[:], 4), op=ALU.mult),
                              KGK + [qk_], ["Cst"])
                            V(lambda e, t=t, g=g: e.tensor_reduce(out=Sx4[:, t, :, g * 4:(g + 1) * 4].rearrange("p h g -> p g h"),
                                                                  in_=prod[:].rearrange("p g (h d) -> p g h d", h=8), axis=AX.X, op=ALU.add),
                              ["Cst"], ["Sx"])
                    V(lambda e: e.tensor_tensor(out=Sx4, in0=Sx4, in1=bc_mid(Gb[:].rearrange("p g h -> p h g"), 4), op=ALU.add),
                      ["Sx", "Gb"], ["Sx"])
                    A(lambda e: e.activation(out=Pp[:], in_=Sx[:], func=AF.Exp), ["Sx"], ["Pp"])
                    V(lambda e: e.tensor_reduce(out=Pr[:, :].rearrange("p (t h) -> p t h", t=4), in_=Pp4, axis=AX.X, op=ALU.add), ["Pp"], ["Pr"])
                    Pr3 = Pr[:, :].rearrange("p (t h) -> p t h", t=4)
                    for g in range(4):
                        for j in range(4):
                            c = b * 16 + g * 4 + j
                            S.op("gpsimd", lambda e, j=j, c=c: e.indirect_dma_start(
                                out=Vg[:, j, :], out_offset=None, in_=cache_v,
                                in_offset=bass.IndirectOffsetOnAxis(ap=idx[:, c:c + 1], axis=0)), reads=["idx"], writes=["Vg%d" % j], dma="Vg%d" % j)
                        for j in range(4):
                            pg = g * 4 + j
                            for h in range(8):
                                col = (h * NSEQ + b) * 4
                                S.op("tensor", lambda e, j=j, pg=pg, h=h, col=col: e.matmul(
                                    NUMs[:, col:col + 4], lhsT=Vg[:, j, (h // 2) * 128:(h // 2 + 1) * 128], rhs=Pp4[:, :, h, pg],
                                    start=False, stop=False, skip_group_check=True), reads=["Vg%d" % j, "Pp"], writes=["acc0"], inc=(h == 7))
                    for h in range(8):
                        col = (h * NSEQ + b) * 4
                        S.op("tensor", lambda e, h=h, col=col: e.matmul(DENs[:, col:col + 4], lhsT=onesf, rhs=Pr3[:, :, h],
                                                                        start=False, stop=False, skip_group_check=True),
                             reads=["Pr", "cstf"], writes=["acc1"], inc=(h == 7))
                for b in range(NSEQ):
                    for h in range(8):
                        col = (h * NSEQ + b) * 4
                        S.op("tensor", lambda e, h=h, col=col, b=b: e.matmul(DENs[:, col:col + 4], lhsT=onesf[0:TS, :], rhs=Pnew[:, h, 4 * b:4 * b + 4],
                                                                             start=False, stop=True, skip_group_check=True),
                             reads=["Pnew", "cstf"], writes=["acc1"], inc=False)
                        S.op("tensor", lambda e, h=h, col=col, b=b: e.matmul(NUMs[:, col:col + 4], lhsT=vnew[:, (h // 2) * 128:(h // 2 + 1) * 128],
                                                                             rhs=Pnew[:, h, 4 * b:4 * b + 4], start=False, stop=True, skip_group_check=True),
                             reads=["Pnew", "vnew"], writes=["acc0"], inc=(h == 7))
                for h in range(8):
                    pair, r0 = h // 2, (h % 2) * 64
                    V(lambda e, h=h, r0=r0: e.reciprocal(out=rec[r0:r0 + 64, 0:TS], in_=DENs[r0:r0 + 64, h * TS:(h + 1) * TS]), ["acc1"], ["rec"])
                    V(lambda e, h=h, r0=r0, pair=pair: e.tensor_tensor(out=attT[r0:r0 + 64, pair, 0:TS], in0=NUMs[r0:r0 + 64, h * TS:(h + 1) * TS],
                                                                       in1=rec[r0:r0 + 64, 0:TS], op=ALU.mult), ["acc0", "rec"], ["attT"])
            if "S" not in KSKIP and "M" not in KSKIP:
                post_mix(TS, NSEQ, 4, 1, xT_smp, yT_s, 0)
            S.barrier()
            S.emit()
            S.reset()

        with contextlib.ExitStack() as pst:
            KTp = pst.enter_context(nc.sbuf_tensor("KTp", [128, 4, 2 * TOK], BF16))
            Vres = pst.enter_context(nc.sbuf_tensor("Vres", [128, 32, 512], BF16))
            NFk = pst.enter_context(nc.sbuf_tensor("NFk", [128, 32, 8], F32))
            psb = lambda name, shape, dt=F32: pst.enter_context(nc.sbuf_tensor(name, list(shape), dt))
            FQa = psb("FQa", [128, 8, T], BF16)
            V(lambda e: e.memset(FQa[:], 0.0), [], ["FQa"])
            WP.tiles.append(psb("wp2", [128, 8, 512], BF16))
            WP.i = 0
            BIAS = psb("BIAS", [128, 32, 8])
            PTR = Ring([psb("PT%d" % i, [128, T], BF16) for i in range(2)], "PT")
            CstP = psb("CstP", [128, 4, 128]); nstP = psb("nstP", [128, 4]); mstP = psb("mstP", [128, 4, 1])
            V(lambda e: e.memset(CstP[:], 0.0), [], ["Cst"])
            V(lambda e: e.memset(nstP[:], 0.0), [], ["nst"])
            V(lambda e: e.memset(mstP[:], 0.0), [], ["mst"])
            V(lambda e: e.memset(rawp[:], 0.0), ["rawp"], ["rawp"])
            V(lambda e: e.memset(carry[:], 0.0), ["carry"], ["carry"])
            V(lambda e: e.memset(crel[:], 0.0), ["crel"], ["crel"])
            onescol = cstf[:, 1, 0:1]
            for ti in range(NT):
                w_in_phase("pre", xT_pre[:, :, ti * T:(ti + 1) * T], T, 1, T, 0, None, None, None, 0, ti * T, KTp, Vres, None)
                for tb in range(2):
                    fox_block(tb, ti * 2 + tb, False)
                    mchunk(128, 1, 128, tb, tb * 128, cstf[:, 2, :], onescol, lambda b: CstP[:], lambda b: nstP[:], mstP[:], False, cstb[:, 2, :])
                V(lambda e: e.tensor_copy(out=rawp[:, :, 0:3], in_=rawp[:, :, T:T + 3]), ["rawp"], ["rawp"])
            for (tl, kk) in ((CstP, "Cst"), (nstP, "nst"), (mstP, "mst"), (carry, "carry")):
                V(lambda e, tl=tl: e.tensor_scalar(out=tl[:], in0=tl[:], scalar1=flg[:, 0:1], scalar2=None, op0=ALU.mult), [kk, "flg"], [kk])
            V(lambda e: e.tensor_scalar(out=rawp[:, :, 0:3], in0=rawp[:, :, 0:3], scalar1=flg[:, 0:1], scalar2=None, op0=ALU.mult),
              ["rawp", "flg"], ["rawp"])
            for ti in range(NT):
                xs_ = xT_own[:, :, ti * T:(ti + 1) * T]
                V(lambda e: e.tensor_copy(out=Rt[:], in_=carry[:]), ["carry"], ["Rt"])
                V(lambda e: e.memset(crel[:], 0.0), ["crel"], ["crel"])
                w_in_phase("own", xs_, T, 1, T, 0, kT_o, v_o, lf_o, ti * T, TOK + ti * T, KTp, Vres, None)
                for tb in range(2):
                    fox_block(tb, 16 + ti * 2 + tb, True)
                    mchunk(128, 1, 128, tb, tb * 128, cstf[:, 2, :], onescol, lambda b: CstP[:], lambda b: nstP[:], mstP[:], True, cstb[:, 2, :])
                if ti == NT - 1:
                    store(cv_o, rawp[:, :, T:T + 3], "rawp", "st_rawp")
                V(lambda e: e.tensor_copy(out=rawp[:, :, 0:3], in_=rawp[:, :, T:T + 3]), ["rawp"], ["rawp"])
                if "F" in KSKIP:
                    continue
                V(lambda e: e.tensor_copy(out=FQa[0:64], in_=Ff[:, :, 0:T]), ["cacc"], ["FQa"])
                V(lambda e: e.tensor_tensor(out=FQa[32:64], in0=Ff[32:64, :, 0:T], in1=FQa[32:64], op=ALU.subtract), ["cacc", "FQa"], ["FQa"])
                nkb = 16 + 2 * (ti + 1)
                V(lambda e, nkb=nkb: e.tensor_tensor(out=BIAS[:, 0:nkb, :], in0=NFk[:, 0:nkb, :], in1=bc_mid(Rt[:], nkb), op=ALU.add),
                  ["NFk", "Rt"], ["BIAS"])
                V(lambda e: e.tensor_scalar(out=BIAS[:, 0:16, :], in0=BIAS[:, 0:16, :], scalar1=FLAGM[:, 0:1], scalar2=None, op0=ALU.add),
                  ["BIAS", "FLAGM"], ["BIAS"])
                for h in range(8 if "P" not in KSKIP else 0):
                    pair, r0 = h // 2, (h % 2) * 64
                    num, den = ACC[2 * (h % 2)], ACC[2 * (h % 2) + 1]
                    nk_, dk_ = "acc%d" % (2 * (h % 2)), "acc%d" % (2 * (h % 2) + 1)
                    def pv_den(kb, ptile, ptk, pair=pair, num=num, den=den, nk_=nk_, dk_=dk_, nkb=nkb):
                        S.op("tensor", lambda e: e.matmul(
                            num[:, 0:T], lhsT=Vres[:, kb, pair * 128:(pair + 1) * 128], rhs=ptile[:], start=(kb == 0), stop=(kb == nkb - 1)),
                            reads=["Vr", ptk], writes=[nk_], inc=False)
                        S.op("tensor", lambda e: e.matmul(
                            den[:, 0:T], lhsT=onesb, rhs=ptile[:], start=(kb == 0), stop=(kb == nkb - 1)),
                            reads=["cstb", ptk], writes=[dk_])
                    prev = None
                    for kb in range(nkb):
                        dj = kb - (nkb - 2)
                        ps, pk = PS.next()
                        S.op("tensor", lambda e, ps=ps, kb=kb, pair=pair, r0=r0: e.matmul(
                            ps[:, 0:T], lhsT=KTp[r0:r0 + 64, pair, kb * 128:(kb + 1) * 128], rhs=QTp[r0:r0 + 64, pair, 0:T],
                            start=True, stop=False), reads=["KT", "QTp"], writes=[pk], inc=False)
                        S.op("tensor", lambda e, ps=ps, h=h, dj=dj: e.matmul(ps[:, 0:T], lhsT=AUGK, rhs=FQa[:, h, :], start=False, stop=(dj < 0)),
                             reads=["cst2b", "FQa"], writes=[pk], inc=(dj < 0))
                        if dj >= 0:
                            S.op("tensor", lambda e, ps=ps, dj=dj: e.matmul(ps[:, 0:T], lhsT=identb, rhs=MASKD(dj), start=False, stop=True),
                                 reads=["cstb", "cst2b"], writes=[pk])
                        ptile, ptk = PTR.next()
                        A(lambda e, ps=ps, ptile=ptile, kb=kb, h=h: e.activation(out=ptile[:], in_=ps[:, 0:T], func=AF.Exp,
                                                                                 bias=BIAS[:, kb, h:h + 1]), [pk, "BIAS"], [ptk])
                        if prev is not None:
                            pv_den(*prev)
                        prev = (kb, ptile, ptk)
                    pv_den(*prev)
                    V(lambda e, r0=r0, den=den: e.reciprocal(out=rec[r0:r0 + 64, 0:T], in_=den[r0:r0 + 64, 0:T]), [dk_], ["rec"])
                    V(lambda e, r0=r0, num=num, pair=pair: e.tensor_tensor(out=attT[r0:r0 + 64, pair, 0:T], in0=num[r0:r0 + 64, 0:T],
                                                                           in1=rec[r0:r0 + 64, 0:T], op=ALU.mult), [nk_, "rec"], ["attT"])
                if "P" not in KSKIP and "M" not in KSKIP:
                    post_mix(T, 1, T, 0, xs_, yT_o, ti * T)
            store(C_o, CstP[:], "Cst", "st_Cst")
            store(n_o, nstP[:], "nst", "st_nst")
            store(m_o, mstP[0:1, :, 0], "mst", "st_mst")
            S.barrier()
            S.emit()
            S.reset()

        S.barrier()
        S.emit()
    return nc


_NC_CACHE = {}


def _rk(w):
    K, N = w.shape
    return np.ascontiguousarray(w.reshape(K // 128, 128, N).transpose(1, 0, 2))


def _bk(w):
    K, N = w.shape
    return np.ascontiguousarray(w.reshape(K // 128, 128, N // 512, 512).transpose(1, 2, 0, 3))


def kernel(x_prompt, x_sample, cache_k, cache_v, cache_logf, page_table, state_C, state_n, state_m, state_conv,
           c_prompt, c_sample, w_ada, b_ada, g_pre_mix, g_post_mix, g_pre_mlp, g_post_mlp, w_in, b_fox_f,
           b_ml_i, b_ml_f, conv_w, conv_b, w_proj_a, w_proj_b, w_out, w_up, w_down):
    f32 = np.float32
    if "nc" not in _NC_CACHE:
        _NC_CACHE["nc"] = build()
    nc = _NC_CACHE["nc"]
    x_prompt = np.asarray(x_prompt, f32)
    x_sample = np.asarray(x_sample, f32)
    w_in0 = np.asarray(w_in, f32)[0]
    w_in_p = np.concatenate([w_in0[:, O_AQ:O_AQ + 512], w_in0[:, O_AK:O_AK + 512], w_in0[:, O_AV:O_AV + 512],
                             w_in0[:, O_BQ:O_BQ + 512], w_in0[:, O_BK:O_BK + 512], w_in0[:, O_BV:O_BV + 512],
                             w_in0[:, O_BO:O_BO + 512], w_in0[:, O_GA:O_GA + 1024], w_in0[:, O_GB:O_GB + 1024]], axis=1)
    w_in_r = _bk(w_in_p)
    w_g = _rk(np.concatenate([w_in0[:, O_AF:O_AF + 8], w_in0[:, O_BI:O_BI + 8]], axis=1))
    w_ada_r = _bk(np.asarray(w_ada, f32)[0])
    b_ada_r = np.ascontiguousarray(np.asarray(b_ada, f32)[0].reshape(48, 128).T)
    g4 = np.stack([np.asarray(g, f32)[0].reshape(8, 128).T for g in (g_pre_mix, g_post_mix, g_pre_mlp, g_post_mlp)], axis=1)
    g4 = np.ascontiguousarray(g4)
    b_g = np.concatenate([np.asarray(b_fox_f, f32)[0], np.asarray(b_ml_i, f32)[0], np.asarray(b_ml_f, f32)[0]])
    b_g = np.ascontiguousarray(np.broadcast_to(b_g[None, :], (128, 16)))
    cst = np.zeros((128, 6, 128), f32)
    cst[:, 5, :] = np.tril(np.ones((128, 128)), -1)
    cst2 = np.zeros((128, 3, 256), f32)
    kk = np.arange(128)[:, None]; qq = np.arange(256)[None, :]
    cst2[:, 0, :] = np.where(qq >= kk, 0.0, -30000.0)
    cst2[:, 1, :] = np.where(qq >= kk + 128, 0.0, -30000.0)
    k6 = np.arange(64)[:, None]; q6 = np.arange(64)[None, :]
    cst2[0:64, 2, 0:64] = np.where((k6 // 4 == q6 // 4) & (k6 <= q6), 0.0, -30000.0)
    cst2[0, 2, 64:192] = 1.0
    cst2[32, 2, 64:192] = 1.0
    cst[:, 4, 127] = np.arange(128)
    w_pa_r = _bk(np.asarray(w_proj_a, f32)[0]); w_pb_r = _bk(np.asarray(w_proj_b, f32)[0])
    w_o_r = _bk(np.asarray(w_out, f32)[0]); w_u_r = _bk(np.asarray(w_up, f32)[0])
    w_d_r = np.ascontiguousarray(np.asarray(w_down, f32)[0].reshape(4, 8, 128, 2, 512).transpose(2, 3, 0, 1, 4)).reshape(128, 8, 8, 512)
    ck = np.asarray(cache_k, f32).reshape(2560 * 128, 512); cv = np.asarray(cache_v, f32).reshape(2560 * 128, 512)
    clf = np.asarray(cache_logf, f32).reshape(2560 * 128, 8)
    ptab = np.asarray(page_table).astype(np.int32)
    cst[:, 0, :] = np.eye(128)
    cst[:, 1, :] = 1.0
    cst[:, 2, :] = np.triu(np.ones((128, 128)))
    for b in range(NSEQ):
        cst[4 * b:4 * b + 4, 3, 4 * b:4 * b + 4] = np.triu(np.ones((4, 4)))
        cst[4 * b:4 * b + 4, 4, b] = 1.0
    conv_w_r = np.ascontiguousarray(np.asarray(conv_w, f32)[0].reshape(4, 8, 128).transpose(2, 1, 0))
    conv_b_r = np.ascontiguousarray(np.asarray(conv_b, f32)[0].reshape(8, 128).T)
    state_C = np.asarray(state_C, f32); state_n = np.asarray(state_n, f32); state_m = np.asarray(state_m, f32)
    state_conv = np.asarray(state_conv, f32)

    def fm(x):
        return np.ascontiguousarray(x.T.reshape(8, 128, x.shape[0]).transpose(1, 0, 2))

    in_maps = []
    for c in range(NCORE):
        s, half = c // 2, c % 2
        xo = x_prompt[s, half * TOK:(half + 1) * TOK]
        xs = x_sample[c * NSEQ:(c + 1) * NSEQ].reshape(TS, D)
        cc = np.concatenate([np.asarray(c_prompt, f32)[s:s + 1], np.asarray(c_sample, f32)[c * NSEQ:(c + 1) * NSEQ]], axis=0)
        in_maps.append({
            "xT_own": fm(xo), "xT_smp": fm(xs), "cT": fm(cc),
            "xT_pre": fm(x_prompt[s, 0:TOK]), "flag": np.full((128, 1), float(half), f32),
            "conv_w": conv_w_r, "conv_b": conv_b_r, "cst2": cst2,
            "w_pa": w_pa_r, "w_pb": w_pb_r, "w_o": w_o_r, "w_u": w_u_r, "w_d": w_d_r,
            **({"cache_k": ck, "cache_v": cv, "cache_lf": clf} if USE_CACHE else {}),
            "pt": np.ascontiguousarray(ptab[c * NSEQ:(c + 1) * NSEQ].reshape(1, NSEQ * 16)),
            "sC": np.ascontiguousarray(state_C[0, c * NSEQ:(c + 1) * NSEQ].transpose(2, 0, 1, 3)),
            "sN": np.ascontiguousarray(state_n[0, c * NSEQ:(c + 1) * NSEQ].transpose(2, 0, 1)),
            "sM": np.ascontiguousarray(state_m[0, c * NSEQ:(c + 1) * NSEQ][None]),
            "sCV": np.ascontiguousarray(state_conv[0, c * NSEQ:(c + 1) * NSEQ].reshape(NSEQ, 3, 8, 128).transpose(3, 2, 0, 1)),
            "w_ada": w_ada_r, "b_ada": b_ada_r, "g4": g4, "w_in": w_in_r, "w_g": w_g, "b_g": b_g, "cst": cst,
        })
    res = run_bass_kernel_spmd(nc, in_maps, core_ids=list(range(NCORE))).results

    B, SEQ = 4, 4096
    y_prompt = np.zeros((B, SEQ, D), f32)
    y_sample = np.zeros((128, 4, D), f32)
    new_k_p = np.zeros((1, B, SEQ, A_H, A_D), f32)
    new_v_p = np.zeros((1, B, SEQ, A_H, A_D), f32)
    new_lf_p = np.zeros((1, B, SEQ, A_H), f32)
    new_C_p = np.zeros((1, B, B_H, B_D, B_D), f32)
    new_n_p = np.zeros((1, B, B_H, B_D), f32)
    new_m_p = np.zeros((1, B, B_H), f32)
    new_cv_p = np.zeros((1, B, 3, D), f32)
    new_k_s = np.zeros((1, 128, 4, A_H, A_D), f32)
    new_v_s = np.zeros((1, 128, 4, A_H, A_D), f32)
    new_lf_s = np.zeros((1, 128, 4, A_H), f32)
    new_C_s = np.zeros((1, 128, B_H, B_D, B_D), f32)
    new_n_s = np.zeros((1, 128, B_H, B_D), f32)
    new_m_s = np.zeros((1, 128, B_H), f32)
    new_cv_s = np.zeros((1, 128, 3, D), f32)
    for c in range(NCORE):
        s, half = c // 2, c % 2
        r = res[c]
        sl = slice(half * TOK, (half + 1) * TOK)
        y_prompt[s, sl] = r["yT_o"].transpose(2, 1, 0).reshape(TOK, D)
        y_sample[c * NSEQ:(c + 1) * NSEQ] = r["yT_s"].transpose(2, 1, 0).reshape(NSEQ, 4, D)
        new_k_p[0, s, sl] = r["kT_o"].T.reshape(TOK, A_H, A_D)
        new_v_p[0, s, sl] = r["v_o"].reshape(TOK, A_H, A_D)
        new_lf_p[0, s, sl] = r["lf_o"]
        if half == 1:
            new_cv_p[0, s] = r["cv_o"].transpose(2, 1, 0).reshape(3, D)
        ss = slice(c * NSEQ, (c + 1) * NSEQ)
        new_k_s[0, ss] = r["kT_s"].T.reshape(NSEQ, 4, A_H, A_D)
        new_v_s[0, ss] = r["v_s"].reshape(NSEQ, 4, A_H, A_D)
        new_lf_s[0, ss] = r["lf_s"].reshape(NSEQ, 4, A_H)
        new_cv_s[0, ss] = r["cv_s"].reshape(128, 8, NSEQ, 7)[:, :, :, 4:7].transpose(2, 3, 1, 0).reshape(NSEQ, 3, D)
        new_C_s[0, ss] = r["C_s"].reshape(128, NSEQ, 4, 128).transpose(1, 2, 0, 3)
        new_n_s[0, ss] = r["n_s"].reshape(128, NSEQ, 4).transpose(1, 2, 0)
        new_m_s[0, ss] = r["m_s"].reshape(NSEQ, 4)
        if half == 1:
            new_C_p[0, s] = r["C_o"].transpose(1, 0, 2)
            new_n_p[0, s] = r["n_o"].T
            new_m_p[0, s] = r["m_o"][0]
    return (y_prompt, y_sample, new_k_p, new_v_p, new_lf_p, new_C_p, new_n_p, new_m_p, new_cv_p,
            new_k_s, new_v_s, new_lf_s, new_C_s, new_n_s, new_m_s, new_cv_s)
```

```python
import contextlib
import numpy as np
import concourse.bass as bass
import concourse.mybir as mybir
from concourse.bass_utils import run_bass_kernel_spmd

F32 = mybir.dt.float32
BF16 = mybir.dt.bfloat16
I32 = mybir.dt.int32
ALU = mybir.AluOpType
AF = mybir.ActivationFunctionType
AX = mybir.AxisListType
ENGS = ("tensor", "vector", "scalar", "gpsimd", "sync")

D = 1024
NCORE = 8
TOK = 2048
T = 256
NT = TOK // T
TS = 64
NSEQ = 16
A_H, A_D = 8, 64
B_H, B_D = 4, 128
NIN = 5648
USE_CACHE = True
import os
KSKIP = os.environ.get("KSKIP", "")
O_AQ, O_AK, O_AV, O_AF, O_BQ, O_BK, O_BV, O_BI, O_BF, O_BO, O_GA, O_GB = (
    0, 512, 1024, 1536, 1544, 2056, 2568, 3080, 3084, 3088, 3600, 4624)
C_AQ, C_AK, C_AV, C_BQ, C_BK, C_BV, C_BO, C_GA, C_GB = (0, 512, 1024, 1536, 2048, 2560, 3072, 3584, 4608)
NINP = 5632


class WB:
    def __init__(self, ap4, base=0, width=None):
        self.ap4, self.base, self.width = ap4, base, width

    def __getitem__(self, key):
        _, ks, cs = key
        c0 = self.base + (cs.start or 0)
        c1 = self.base + (cs.stop if cs.stop is not None else self.width)
        if c1 - c0 <= 512:
            assert c0 % 512 == 0 and ks == slice(None)
            return self.ap4[:, c0 // 512, :, 0:c1 - c0]
        return WB(self.ap4, c0, c1 - c0)


class Sched:
    def __init__(self, nc):
        self.nc = nc
        self.phase = 0
        self.reset()

    def reset(self):
        self.phase += 1
        self.ops = {e: [] for e in ENGS}
        self.cnt = {}
        self.known = {e: {} for e in ENGS}
        self.keys = {}
        self.pseudo = {}
        self.pending = {e: ([], []) for e in ENGS}
        self.semnames = ["E_" + e for e in ENGS]

    def _need(self, eng, ev, waits, kind="raw"):
        if ev is None:
            return
        s, v = ev
        if s == "E_" + eng:
            if eng == "tensor":
                return
            if eng in ("vector", "scalar") and kind != "raw":
                return
        if self.known[eng].get(s, 0) >= v:
            return
        self.known[eng][s] = v
        waits.append((s, v))

    def op(self, eng, fn, reads=(), writes=(), inc=True, dma=None):
        pacc = [k for k in reads if k.startswith(("ps", "acc"))]
        if pacc:
            reads = [k for k in reads if k not in pacc]
            writes = list(writes) + pacc
        waits = []
        for k in reads:
            st = self.keys.get(k)
            if st:
                self._need(eng, st[0], waits, "raw")
        for k in writes:
            st = self.keys.get(k)
            if st:
                if k in pacc:
                    self._need(eng, st[0], waits, "war" if self.pseudo.get(k) else "raw")
                else:
                    self._need(eng, st[0], waits, "waw")
                for ev in st[1]:
                    self._need(eng, ev, waits, "war")
        for k in writes:
            self.pseudo[k] = (k in pacc)
        if dma is not None:
            s = "D_" + dma
            if s not in self.cnt:
                self.semnames.append(s)
            self.cnt[s] = self.cnt.get(s, 0) + 16
            self._apply((s, self.cnt[s]), reads, writes)
            self.ops[eng].append((waits, fn, (s, 16)))
            return
        pr, pw = self.pending[eng]
        if not inc:
            pr.extend(reads)
            pw.extend(writes)
            self.ops[eng].append((waits, fn, None))
            return
        s = "E_" + eng
        self.cnt[s] = self.cnt.get(s, 0) + 1
        self._apply((s, self.cnt[s]), list(reads) + pr, list(writes) + pw)
        self.pending[eng] = ([], [])
        self.ops[eng].append((waits, fn, (s, 1)))

    def _apply(self, ev, reads, writes):
        for k in reads:
            self.keys.setdefault(k, [None, []])[1].append(ev)
        for k in writes:
            self.keys[k] = [ev, []]

    def barrier(self):
        for e in ENGS:
            waits = []
            for s, v in self.cnt.items():
                self._need(e, (s, v), waits)
            self.ops[e].append((waits, None, None))

    def emit(self):
        nc = self.nc
        with contextlib.ExitStack() as st:
            sems = {n: st.enter_context(nc.semaphore("%s_p%d" % (n, self.phase))) for n in self.semnames}
            block = st.enter_context(nc.Block())

            def run(engname):
                def body(e):
                    for waits, fn, inc in self.ops[engname]:
                        for s, v in waits:
                            e.wait_ge(sems[s], v)
                        if fn is None:
                            continue
                        ins = fn(e)
                        if inc is not None:
                            ins.then_inc(sems[inc[0]], inc[1])
                return body
            block.tensor(run("tensor"))
            block.vector(run("vector"))
            block.scalar(run("scalar"))
            block.gpsimd(run("gpsimd"))
            block.sync(run("sync"))


def bc_last(ap, n):
    return bass.AP(ap.tensor, ap.offset, [list(x) for x in ap.ap] + [[0, n]])


class Ring:
    def __init__(self, tiles, name):
        self.tiles = tiles
        self.name = name
        self.i = 0

    def next(self):
        j = self.i % len(self.tiles)
        self.i += 1
        return self.tiles[j], "%s%d" % (self.name, j)


def build():
    nc = bass.Bass("TRN2", target_bir_lowering=False)

    def din(name, shape, dt=F32):
        return nc.dram_tensor(name, list(shape), dt, kind="ExternalInput").ap()

    def dout(name, shape, dt=F32):
        return nc.dram_tensor(name, list(shape), dt, kind="ExternalOutput").ap()

    xT_own = din("xT_own", [128, 8, TOK])
    xT_smp = din("xT_smp", [128, 8, TS])
    cT = din("cT", [128, 8, 17])
    w_ada = WB(din("w_ada", [128, 12, 8, 512]))
    b_ada = din("b_ada", [128, 48])
    g4 = din("g4", [128, 4, 8])
    w_in = WB(din("w_in", [128, NINP // 512, 8, 512]))
    w_g = din("w_g", [128, 8, 16])
    b_g = din("b_g", [128, 16])
    cst = din("cst", [128, 6, 128])
    xT_pre = din("xT_pre", [128, 8, TOK])
    flag = din("flag", [128, 1])
    conv_w = din("conv_w", [128, 8, 4])
    conv_b = din("conv_b", [128, 8])
    sC = din("sC", [128, NSEQ, 4, 128])
    sN = din("sN", [128, NSEQ, 4])
    sM = din("sM", [1, NSEQ, 4])
    sCV = din("sCV", [128, 8, NSEQ, 3])
    cst2 = din("cst2", [128, 3, 256])
    w_pa = WB(din("w_pa", [128, 2, 4, 512]))
    w_pb = WB(din("w_pb", [128, 2, 4, 512]))
    w_o = WB(din("w_o", [128, 2, 8, 512]))
    w_u = WB(din("w_u", [128, 8, 8, 512]))
    w_d = din("w_d", [128, 8, 8, 512])
    if USE_CACHE:
        cache_k = din("cache_k", [2560 * 128, 512])
        cache_v = din("cache_v", [2560 * 128, 512])
        cache_lf = din("cache_lf", [2560 * 128, 8])
    pt = din("pt", [1, NSEQ * 16], I32)
    qscr = dout("qscr", [TS, 512])

    kT_o = dout("kT_o", [512, TOK])
    v_o = dout("v_o", [TOK, 512])
    lf_o = dout("lf_o", [TOK, 8])
    cv_o = dout("cv_o", [128, 8, 3])
    kT_s = dout("kT_s", [512, TS])
    v_s = dout("v_s", [TS, 512])
    lf_s = dout("lf_s", [TS, 8])
    cv_s = dout("cv_s", [128, 8, NSEQ * 7])
    yT_o = dout("yT_o", [128, 8, TOK])
    yT_s = dout("yT_s", [128, 8, TS])
    C_o = dout("C_o", [128, 4, 128])
    n_o = dout("n_o", [128, 4])
    m_o = dout("m_o", [1, 4])
    C_s = dout("C_s", [128, NSEQ * 4, 128])
    n_s = dout("n_s", [128, NSEQ * 4])
    m_s = dout("m_s", [1, NSEQ * 4])

    S = Sched(nc)
    with contextlib.ExitStack() as st:
        def sb(name, shape, dt=F32):
            return st.enter_context(nc.sbuf_tensor(name, list(shape), dt))

        PS = Ring([st.enter_context(nc.psum_tensor("ps%d" % i, [128, 512], F32)) for i in range(8)], "ps")
        WP = Ring([sb("wp%d" % i, [128, 8, 512], BF16) for i in range(2)], "wp")
        STG = Ring([sb("stg%d" % i, [128, 512], F32) for i in range(2)], "stg")

        cstf = sb("cstf", [128, 6, 128])
        cstb = sb("cstb", [128, 6, 128], BF16)
        g4t = sb("g4t", [128, 4, 8])
        badat = sb("badat", [128, 48])
        bgt = sb("bgt", [128, 16])
        ctt = sb("ctt", [128, 8, 17])
        csil = sb("csil", [128, 8, 17], BF16)
        MOD = sb("MOD", [128, 48, 17])
        AM = sb("AM", [128, 8, 17])
        GM = sb("GM", [128, 8, 17])
        AFm = sb("AFm", [128, 8, 17])
        GFm = sb("GFm", [128, 8, 17])
        wgf = sb("wgf", [128, 8, 16], BF16)
        onec = sb("onec", [128, 1])
        epsc = sb("epsc", [128, 1])

        ld = lambda dst, src, key: S.op("sync", lambda e: e.dma_start(out=dst, in_=src), writes=[key], dma=key)
        ld(cstf[:], cst, "cstf")
        ld(g4t[:], g4, "g4t")
        ld(badat[:], b_ada, "badat")
        ld(bgt[:], b_g, "bgt")
        ld(ctt[:], cT, "ctt")
        S.op("gpsimd", lambda e: e.dma_start(out=wgf[:], in_=w_g), writes=["wgf"], dma="wgf")
        S.op("vector", lambda e: e.tensor_copy(out=cstb[:], in_=cstf[:]), reads=["cstf"], writes=["cstb"])
        S.op("vector", lambda e: e.memset(onec[:], 1.0), writes=["onec"])
        S.op("vector", lambda e: e.memset(epsc[:], 1e-6), writes=["epsc"])
        identb = cstb[:, 0, :]
        onesb = cstb[:, 1, :]
        trif = cstf[:, 2, :]

        def wload(src_ap, kc, ncols):
            tile, key = WP.next()
            if ncols == 512:
                S.op("gpsimd", lambda e: e.dma_start(out=tile[:, 0:kc, :].rearrange("p k c -> p (k c)"),
                                                     in_=src_ap.rearrange("p k c -> p (k c)"), max_dma_last_dim=8192),
                     writes=[key], dma=key)
            else:
                S.op("gpsimd", lambda e: e.dma_start(out=tile[:, 0:kc, 0:ncols], in_=src_ap), writes=[key], dma=key)
            return tile, key

        S.op("scalar", lambda e: e.activation(out=csil[:], in_=ctt[:], func=AF.Silu), reads=["ctt"], writes=["csil"])
        for blk in range(12):
            wt, wk = wload(w_ada[:, :, blk * 512:(blk + 1) * 512], 8, 512)
            ps, pk = PS.next()
            for jj in range(4):
                for kc in range(8):
                    S.op("tensor", lambda e, jj=jj, kc=kc, wt=wt, ps=ps: e.matmul(
                        ps[:, jj * 17:(jj + 1) * 17], lhsT=wt[:, kc, jj * 128:(jj + 1) * 128], rhs=csil[:, kc, :],
                        start=(kc == 0), stop=(kc == 7)), reads=[wk, "csil"], writes=[pk], inc=(kc == 7 and jj == 3))
            for jj in range(4):
                j = blk * 4 + jj
                S.op("vector", lambda e, jj=jj, j=j, ps=ps: e.tensor_scalar(
                    out=MOD[:, j, :], in0=ps[:, jj * 17:(jj + 1) * 17], scalar1=badat[:, j:j + 1], scalar2=None,
                    op0=ALU.add), reads=[pk, "badat"], writes=["MOD"])
        for (dst, dk, gi, mj, plus1) in ((AM, "AM", 0, 8, True), (GM, "GM", 1, 16, False),
                                         (AFm, "AFm", 2, 32, True), (GFm, "GFm", 3, 40, False)):
            gb = bc_last(g4t[:, gi, :], 17)
            if plus1:
                S.op("vector", lambda e, dst=dst, mj=mj, gb=gb: e.scalar_tensor_tensor(
                    out=dst[:], in0=MOD[:, mj:mj + 8, :], scalar=1.0, in1=gb, op0=ALU.add, op1=ALU.mult),
                    reads=["MOD", "g4t"], writes=[dk])
            else:
                S.op("vector", lambda e, dst=dst, mj=mj, gb=gb: e.tensor_tensor(
                    out=dst[:], in0=MOD[:, mj:mj + 8, :], in1=gb, op=ALU.mult), reads=["MOD", "g4t"], writes=[dk])

        xt = sb("xt", [128, 8, T])
        sq = sb("sq", [128, 8, T], BF16)
        hT = sb("hT", [128, 8, T], BF16)
        rstd = sb("rstd", [128, T])
        tmpA = Ring([sb("tmpA%d" % i, [128, T]) for i in range(2)], "tmpA")
        gatT = sb("gatT", [128, 2, 16])
        gex = sb("gex", [128, 16])
        lfT = sb("lfT", [128, 2, 16])
        rawp = sb("rawp", [128, 8, T + 3])
        cacc = sb("cacc", [128, 8, T])
        ctmp = sb("ctmp", [128, 8, T])
        mqk = sb("mqk", [128, 8, T], BF16)
        kf = sb("kf", [128, 4, T])
        MV = sb("MV", [128, 2, 512], BF16)
        SBO = sb("SBO", [128, 4, T], BF16)
        hbT = sb("hbT", [128, 4, T], BF16)
        convw = sb("convw", [128, 8, 4])
        convb = sb("convb", [128, 8])
        flg = sb("flg", [128, 1])
        ld(convw[:], conv_w, "convw")
        ld(convb[:], conv_b, "convb")
        ld(flg[:], flag, "flg")
        a_sb = sb("a_sb", [128, 4]); b_sb = sb("b_sb", [128, 4]); w_sb = sb("w_sb", [128, 4]); wa_sb = sb("wa_sb", [128, 4])
        Ra = sb("Ra", [128, 4, 128])
        brow = sb("brow", [128, 512]); thr = sb("thr", [128, 512]); hd = brow; hd2 = thr
        amax = sb("amax", [128, 4, 16]); Mt = sb("Mt", [128, 4, 16]); dsi = sb("dsi", [128, 4, 16]); si = sb("si", [128, 4, 16])
        tmm = sb("tmm", [128, 4, 16]); Mtok = sb("Mtok", [128, 4])
        mkT = sb("mkT", [128, 4, 128], BF16)
        vp = sb("vp", [128, 4, 129], BF16); vpb = sb("vpb", [128, 4, 129], BF16)
        WM = Ra; AT = sb("AT", [128, 4, 128], BF16)
        CsR = Ring([sb("Cs%d" % i, [128, 4, 128], BF16) for i in range(2)], "Cs")
        nrR = Ring([sb("nr%d" % i, [128, 4, 128], BF16) for i in range(2)], "nr")
        S.op("vector", lambda e: e.memset(rawp[:], 0.0), writes=["rawp"])

        identf = cstf[:, 0, :]
        onesf = cstf[:, 1, :]
        ACC = [PS.tiles[4 + i] for i in range(4)]
        PS.tiles = PS.tiles[0:4]

        sp_r = sb("sp_r", [128, 8])
        sp_p = [sb("sp_p%d" % i, [128, 8], BF16) for i in range(3)]
        sq_p = [sb("sq_p%d" % i, [128, 8], BF16) for i in range(3)]
        RP = Ring([sb("RP%d" % i, [128, 4, 128], BF16) for i in range(2)], "RP")

        def split3(src, skey, rows, ncol, dst, dkey):
            V(lambda e: e.tensor_copy(out=dst[0][0:rows, 0:ncol], in_=src), [skey], [dkey + "0"])
            V(lambda e: e.tensor_tensor(out=sp_r[0:rows, 0:ncol], in0=src, in1=dst[0][0:rows, 0:ncol], op=ALU.subtract),
              [skey, dkey + "0"], ["sp_r"])
            V(lambda e: e.tensor_copy(out=dst[1][0:rows, 0:ncol], in_=sp_r[0:rows, 0:ncol]), ["sp_r"], [dkey + "1"])
            V(lambda e: e.tensor_tensor(out=sp_r[0:rows, 0:ncol], in0=sp_r[0:rows, 0:ncol], in1=dst[1][0:rows, 0:ncol], op=ALU.subtract),
              ["sp_r", dkey + "1"], ["sp_r"])
            V(lambda e: e.tensor_copy(out=dst[2][0:rows, 0:ncol], in_=sp_r[0:rows, 0:ncol]), ["sp_r"], [dkey + "2"])

        def cumsum_tm(ps, pk, TRIb, src, skey, rows, ncol):
            split3(src, skey, rows, ncol, sp_p, "sp_p")
            for j in range(3):
                S.op("tensor", lambda e, j=j: e.matmul(ps[0:rows, 0:ncol], lhsT=TRIb, rhs=sp_p[j][0:rows, 0:ncol],
                                                       start=(j == 0), stop=(j == 2)), reads=["sp_p%d" % j, "cstb"], writes=[pk], inc=(j == 2))

        def rowbc(ps, pk, src, skey, rows, h0, nh):
            for j in range(3):
                rp, rk = RP.next()
                V(lambda e, j=j, rp=rp: e.tensor_tensor(out=rp[0:rows, 0:nh, 0:rows], in0=bc_mid(identb[0:rows, 0:rows], nh),
                                                        in1=bc_last(src[j][0:rows, h0:h0 + nh], rows), op=ALU.mult),
                  [skey + str(j), "cstb"], [rk])
                S.op("tensor", lambda e, j=j, rp=rp: e.matmul(ps[:, 0:nh * rows].rearrange("p (h t) -> p h t", h=nh), lhsT=onesb[0:rows, :],
                                                             rhs=rp[0:rows, 0:nh, 0:rows], start=(j == 0), stop=(j == 2)),
                     reads=[rk, "cstb"], writes=[pk], inc=(j == 2))

        def bc_mid(ap, n):
            a = [list(x) for x in ap.ap]
            return bass.AP(ap.tensor, ap.offset, [a[0], [0, n]] + a[1:])

        def rms_rstd(src, skey, n):
            S.op("scalar", lambda e: e.activation(out=sq[:, :, 0:n], in_=src, func=AF.Square), reads=[skey], writes=["sq"])
            ps, pk = PS.next()
            for kc in range(8):
                S.op("tensor", lambda e, kc=kc, ps=ps: e.matmul(ps[:, 0:n], lhsT=onesb, rhs=sq[:, kc, 0:n],
                                                                start=(kc == 0), stop=(kc == 7)),
                     reads=["sq", "cstb"], writes=[pk], inc=(kc == 7))
            S.op("scalar", lambda e, ps=ps: e.activation(out=rstd[:, 0:n], in_=ps[:, 0:n], func=AF.Sqrt,
                                                         bias=epsc[:, 0:1], scale=1.0 / D),
                 reads=[pk, "epsc"], writes=["rstd"])
            S.op("vector", lambda e: e.reciprocal(out=rstd[:, 0:n], in_=rstd[:, 0:n]), reads=["rstd"], writes=["rstd"])

        def modulate(src, skey, n, nseg, L, s0, Amat, akey, sh_j):
            v3 = lambda a: a.rearrange("p (s l) -> p s l", l=L)
            for kc in range(8):
                tt, tk = tmpA.next()
                S.op("vector", lambda e, kc=kc, tt=tt: e.tensor_tensor(out=tt[:, 0:n], in0=src[:, kc, :], in1=rstd[:, 0:n],
                                                                       op=ALU.mult), reads=[skey, "rstd"], writes=[tk])
                if nseg == 1:
                    S.op("vector", lambda e, kc=kc, tt=tt: e.tensor_scalar(
                        out=hT[:, kc, 0:n], in0=tt[:, 0:n], scalar1=Amat[:, kc, s0:s0 + 1], scalar2=MOD[:, sh_j + kc, s0:s0 + 1],
                        op0=ALU.mult, op1=ALU.add), reads=[tk, akey, "MOD"], writes=["hT"])
                    continue
                S.op("vector", lambda e, kc=kc, tt=tt: e.tensor_tensor(
                    out=v3(tt[:, 0:n]), in0=v3(tt[:, 0:n]), in1=bc_last(Amat[:, kc, s0:s0 + nseg], L), op=ALU.mult),
                    reads=[tk, akey], writes=[tk])
                S.op("vector", lambda e, kc=kc, tt=tt: e.tensor_tensor(
                    out=v3(hT[:, kc, 0:n]), in0=v3(tt[:, 0:n]), in1=bc_last(MOD[:, sh_j + kc, s0:s0 + nseg], L), op=ALU.add),
                    reads=[tk, "MOD"], writes=["hT"])

        def lin_fm(wsrc, ncols, n, consume):
            for c0 in range(0, ncols, 512):
                w = min(512, ncols - c0)
                wt, wk = wload(wsrc[:, :, c0:c0 + w], 8, w)
                for cb in range(w // 128):
                    ps, pk = PS.next()
                    for kc in range(8):
                        S.op("tensor", lambda e, kc=kc, cb=cb, wt=wt, ps=ps: e.matmul(
                            ps[:, 0:n], lhsT=wt[:, kc, cb * 128:(cb + 1) * 128], rhs=hT[:, kc, 0:n],
                            start=(kc == 0), stop=(kc == 7)), reads=[wk, "hT"], writes=[pk], inc=(kc == 7))
                    consume(c0 // 128 + cb, ps, pk)

        def lin_tm(wsrc, n, consume):
            wt, wk = wload(wsrc, 8, 512)
            for tb in range((n + 127) // 128):
                rows = min(128, n - tb * 128)
                ps, pk = PS.next()
                for kc in range(8):
                    S.op("tensor", lambda e, kc=kc, tb=tb, rows=rows, wt=wt, ps=ps: e.matmul(
                        ps[0:rows, :], lhsT=hT[:, kc, tb * 128:tb * 128 + rows], rhs=wt[:, kc, :],
                        start=(kc == 0), stop=(kc == 7)), reads=[wk, "hT"], writes=[pk], inc=(kc == 7))
                consume(tb, rows, ps, pk)

        def store(dst, src, skey, chan):
            S.op("sync", lambda e: e.dma_start(out=dst, in_=src), reads=[skey], dma=chan)

        V = lambda fn, r, w: S.op("vector", fn, reads=r, writes=w)
        A = lambda fn, r, w: S.op("scalar", fn, reads=r, writes=w)

        def mchunk(Tc, nseg, L, tb, c0, TRI, IND, Cseg, nseg_ap, mst, full, TRIb):
            W4 = 4 * Tc
            i_ap = gatT[0:Tc, tb, 8:12]
            lf_ap = lfT[0:Tc, tb, 12:16]
            v4 = lambda a: a.rearrange("p (h s l) -> p h s l", h=4, l=L)
            ps, pk = PS.next()
            cumsum_tm(ps, pk, TRIb, lf_ap, "lfT", Tc, 4)
            V(lambda e: e.tensor_tensor(out=a_sb[0:Tc, :], in0=i_ap, in1=ps[0:Tc, 0:4], op=ALU.subtract), ["gatT", pk], ["a_sb"])
            A(lambda e: e.copy(out=b_sb[0:Tc, :], in_=ps[0:Tc, 0:4]), [pk], ["b_sb"])
            psA, pkA = PS.next()
            split3(a_sb[0:Tc, :], "a_sb", Tc, 4, sq_p, "sq_p")
            rowbc(psA, pkA, sq_p, "sq_p", Tc, 0, 4)
            V(lambda e: e.tensor_reduce(out=amax[:, :, 0:nseg], in_=v4(psA[:, 0:W4]), axis=AX.X, op=ALU.max), [pkA], ["amax"])
            psB, pkB = PS.next()
            split3(b_sb[0:Tc, :], "b_sb", Tc, 4, sq_p, "sq_p")
            rowbc(psB, pkB, sq_p, "sq_p", Tc, 0, 4)
            A(lambda e: e.copy(out=brow[:, 0:W4], in_=psB[:, 0:W4]), [pkB], ["brow"])
            blv = v4(brow[:, 0:W4])[:, :, :, L - 1]
            V(lambda e: e.tensor_tensor(out=Mt[:, :, 0:nseg], in0=mst, in1=amax[:, :, 0:nseg], op=ALU.max), ["mst", "amax"], ["Mt"])
            V(lambda e: e.tensor_tensor(out=dsi[:, :, 0:nseg], in0=mst, in1=Mt[:, :, 0:nseg], op=ALU.subtract), ["mst", "Mt"], ["dsi"])
            A(lambda e: e.activation(out=si[:, :, 0:nseg], in_=dsi[:, :, 0:nseg], func=AF.Exp), ["dsi"], ["si"])
            V(lambda e: e.tensor_tensor(out=mst, in0=blv, in1=Mt[:, :, 0:nseg], op=ALU.add), ["brow", "Mt"], ["mst"])
            V(lambda e: e.tensor_tensor(out=tmm[0:Tc, :, 0:nseg], in0=Mt[0:Tc, :, 0:nseg], in1=bc_mid(IND, 4), op=ALU.mult),
              ["Mt", "cstf"], ["tmm"])
            V(lambda e: e.tensor_reduce(out=Mtok[0:Tc, :], in_=tmm[0:Tc, :, 0:nseg], axis=AX.X, op=ALU.add), ["tmm"], ["Mtok"])
            V(lambda e: e.tensor_tensor(out=wa_sb[0:Tc, :], in0=a_sb[0:Tc, :], in1=Mtok[0:Tc, :], op=ALU.subtract),
              ["a_sb", "Mtok"], ["wa_sb"])
            A(lambda e: e.activation(out=w_sb[0:Tc, :], in_=wa_sb[0:Tc, :], func=AF.Exp), ["wa_sb"], ["w_sb"])
            psT, pkT = PS.next()
            for h in range(4):
                S.op("tensor", lambda e, h=h: e.transpose(out=psT[0:Tc, h * 128:(h + 1) * 128], in_=kf[:, h, c0:c0 + Tc], identity=identf),
                     reads=["kf", "cstf"], writes=[pkT], inc=(h == 3))
            A(lambda e: e.copy(out=mkT[0:Tc, :, :], in_=psT[0:Tc, :].rearrange("p (h d) -> p h d", h=4)), [pkT], ["mkT"])
            V(lambda e: e.tensor_tensor(out=vp[0:Tc, :, 0:128], in0=MV[0:Tc, tb, :].rearrange("p (h e) -> p h e", h=4),
                                        in1=bc_last(w_sb[0:Tc, :], 128), op=ALU.mult), ["MV", "w_sb"], ["vp"])
            V(lambda e: e.tensor_copy(out=vp[0:Tc, :, 128], in_=w_sb[0:Tc, :]), ["w_sb"], ["vp"])
            if full:
                V(lambda e: e.tensor_tensor(out=v4(thr[:, 0:W4]), in0=v4(brow[:, 0:W4]), in1=bc_last(Mt[:, :, 0:nseg], L), op=ALU.add),
                  ["brow", "Mt"], ["thr"])
                A(lambda e: e.activation(out=thr[:, 0:W4], in_=thr[:, 0:W4], func=AF.Exp, scale=-1.0), ["thr"], ["thr"])
                psQ, pkQ = PS.next()
                for h in range(4):
                    S.op("tensor", lambda e, h=h: e.matmul(psQ[0:Tc, h * Tc:(h + 1) * Tc], lhsT=mqk[:, 4 + h, c0:c0 + Tc],
                                                           rhs=mqk[:, h, c0:c0 + Tc], start=True, stop=True),
                         reads=["mqk"], writes=[pkQ], inc=(h == 3))
                V(lambda e: e.tensor_tensor(out=WM[0:Tc, :, 0:Tc], in0=bc_mid(TRI, 4), in1=bc_last(w_sb[0:Tc, :], Tc), op=ALU.mult),
                  ["w_sb", "cstf"], ["Ra"])
                V(lambda e: e.tensor_tensor(out=AT[0:Tc, :, 0:Tc], in0=psQ[0:Tc, 0:W4].rearrange("p (h t) -> p h t", h=4),
                                            in1=WM[0:Tc, :, 0:Tc], op=ALU.mult), [pkQ, "Ra"], ["AT"])
                num, den = ACC[0], ACC[1]
                V(lambda e: e.memset(num[:, :], 0.0), [], ["acc0"])
                V(lambda e: e.memset(den[:, :], 0.0), [], ["acc1"])
                for h in range(4):
                    S.op("tensor", lambda e, h=h: e.matmul(num[:, h * Tc:(h + 1) * Tc], lhsT=MV[0:Tc, tb, h * 128:(h + 1) * 128],
                                                           rhs=AT[0:Tc, h, 0:Tc], start=False, stop=False, skip_group_check=True),
                         reads=["MV", "AT"], writes=["acc0"], inc=False)
                    S.op("tensor", lambda e, h=h: e.matmul(den[:, h * Tc:(h + 1) * Tc], lhsT=onesb[0:Tc, :],
                                                           rhs=AT[0:Tc, h, 0:Tc], start=False, stop=False, skip_group_check=True),
                         reads=["cstb", "AT"], writes=["acc1"], inc=False)
                for b in range(nseg):
                    Cs, ck = CsR.next()
                    nr, nk = nrR.next()
                    V(lambda e, b=b, Cs=Cs: e.tensor_tensor(out=Cs[:], in0=Cseg(b), in1=bc_last(si[:, :, b], 128), op=ALU.mult),
                      ["Cst", "si"], [ck])
                    V(lambda e, b=b, nr=nr: e.tensor_tensor(out=nr[:], in0=bc_last(nseg_ap(b), 128), in1=bc_last(si[:, :, b], 128),
                                                            op=ALU.mult), ["nst", "si"], [nk])
                    for h in range(4):
                        last = (b == nseg - 1 and h == 3)
                        cs0 = h * Tc + b * L
                        S.op("tensor", lambda e, h=h, b=b, Cs=Cs, cs0=cs0: e.matmul(
                            num[:, cs0:cs0 + L], lhsT=Cs[:, h, :], rhs=mqk[:, h, c0 + b * L:c0 + (b + 1) * L], start=False, stop=True,
                            skip_group_check=True),
                            reads=[ck, "mqk"], writes=["acc0"], inc=False)
                        S.op("tensor", lambda e, h=h, b=b, nr=nr, cs0=cs0: e.matmul(
                            den[:, cs0:cs0 + L], lhsT=nr[:, h, :], rhs=mqk[:, h, c0 + b * L:c0 + (b + 1) * L], start=False, stop=True,
                            skip_group_check=True),
                            reads=[nk, "mqk"], writes=["acc1"], inc=(h == 3))
                A(lambda e: e.activation(out=hd[:, 0:W4], in_=den[:, 0:W4], func=AF.Abs), ["acc1"], ["brow"])
                V(lambda e: e.tensor_tensor(out=hd[:, 0:W4], in0=hd[:, 0:W4], in1=thr[:, 0:W4], op=ALU.max), ["brow", "thr"], ["brow"])
                V(lambda e: e.reciprocal(out=hd[:, 0:W4], in_=hd[:, 0:W4]), ["brow"], ["brow"])
                V(lambda e: e.tensor_tensor(out=hd2[:, 0:W4], in0=num[:, 0:W4], in1=hd[:, 0:W4], op=ALU.mult), ["acc0", "brow"], ["thr"])
                V(lambda e: e.tensor_tensor(out=hbT[:, :, c0:c0 + Tc], in0=hd2[:, 0:W4].rearrange("p (h t) -> p h t", h=4),
                                            in1=SBO[:, :, c0:c0 + Tc], op=ALU.mult), ["thr", "SBO"], ["hbT"])
            for b in range(nseg):
                if nseg > 1:
                    V(lambda e, b=b: e.tensor_scalar(out=vpb[0:Tc, :, :], in0=vp[0:Tc, :, :], scalar1=IND[:, b:b + 1], scalar2=None,
                                                     op0=ALU.mult), ["vp", "cstf"], ["vpb"])
                    vv, vk = vpb, "vpb"
                else:
                    vv, vk = vp, "vp"
                psC, pkC = PS.next()
                for h in range(4):
                    S.op("tensor", lambda e, h=h, vv=vv: e.matmul(psC[:, h * 128:(h + 1) * 128], lhsT=mkT[0:Tc, h, :], rhs=vv[0:Tc, h, 0:128],
                                                                  start=True, stop=True), reads=["mkT", vk], writes=[pkC], inc=(h == 3))
                psN, pkN = PS.next()
                for h in range(4):
                    S.op("tensor", lambda e, h=h, vv=vv: e.matmul(psN[:, h:h + 1], lhsT=mkT[0:Tc, h, :], rhs=vv[0:Tc, h, 128:129],
                                                                  start=True, stop=True), reads=["mkT", vk], writes=[pkN], inc=(h == 3))
                V(lambda e, b=b: e.tensor_tensor(out=Cseg(b), in0=Cseg(b), in1=bc_last(si[:, :, b], 128), op=ALU.mult), ["Cst", "si"], ["Cst"])
                V(lambda e, b=b: e.tensor_tensor(out=Cseg(b), in0=Cseg(b), in1=psC[:, :].rearrange("p (h e) -> p h e", h=4), op=ALU.add),
                  ["Cst", pkC], ["Cst"])
                V(lambda e, b=b: e.tensor_tensor(out=nseg_ap(b), in0=nseg_ap(b), in1=si[:, :, b], op=ALU.mult), ["nst", "si"], ["nst"])
                V(lambda e, b=b: e.tensor_tensor(out=nseg_ap(b), in0=nseg_ap(b), in1=psN[:, 0:4], op=ALU.add), ["nst", pkN], ["nst"])

        cst2b = sb("cst2b", [128, 3, 256], BF16)
        S.op("gpsimd", lambda e: e.dma_start(out=cst2b[:], in_=cst2), writes=["cst2b"], dma="cst2b")
        MASKD = lambda j: cst2b[:, j, 0:T]
        MASKS = cst2b[:, 2, 0:TS]
        AUGK = cst2b[:, 2, 64:192]
        iotaf = cstf[:, 4, 127:128]
        FLAGM = sb("FLAGM", [128, 1])
        V(lambda e: e.tensor_scalar(out=FLAGM[:], in0=flg[:], scalar1=-1.0, scalar2=30000.0, op0=ALU.add, op1=ALU.mult), ["flg"], ["FLAGM"])
        BIG = sb("BIG", [128, 32 * T], BF16)
        SGA = BIG[:, 0:8 * T].rearrange("p (c t) -> p c t", c=8)
        SGB = BIG[:, 8 * T:16 * T].rearrange("p (c t) -> p c t", c=8)
        YPRE = BIG[:, 16 * T:24 * T].rearrange("p (c t) -> p c t", c=8)
        attT = BIG[:, 24 * T:28 * T].rearrange("p (c t) -> p c t", c=4)
        QTp = BIG[:, 28 * T:32 * T].rearrange("p (c t) -> p c t", c=4)
        UT = BIG[:, :].rearrange("p (c t) -> p c t", c=32)
        BIGK = ["SGA", "SGB", "YPRE", "attT", "QTp"]
        Yf = cacc_alias = None
        Yf = ctmp
        x1 = cacc
        Ff = cacc[0:64, :, :]
        cs_sb = sb("cs_sb", [128, 8]); tot_sb = sb("tot_sb", [128, 8])
        carry = sb("carry", [128, 8]); crel = sb("crel", [128, 8]); Rt = sb("Rt", [128, 8])
        rec = sb("rec", [128, T])
        utmp = tmpA

        def lin_gen(wsrc, nk, ncols, n, rhs_fn, rkeys, consume):
            for c0_ in range(0, ncols, 512):
                w = min(512, ncols - c0_)
                wt, wk = wload(wsrc[:, :, c0_:c0_ + w], nk, w)
                for cb in range(w // 128):
                    ps, pk = PS.next()
                    for kc in range(nk):
                        S.op("tensor", lambda e, kc=kc, cb=cb, wt=wt, ps=ps: e.matmul(
                            ps[:, 0:n], lhsT=wt[:, kc, cb * 128:(cb + 1) * 128], rhs=rhs_fn(kc),
                            start=(kc == 0), stop=(kc == nk - 1)), reads=[wk] + rkeys, writes=[pk], inc=(kc == nk - 1))
                    consume(c0_ // 128 + cb, ps, pk)

        def w_in_phase(mode, xsrc, n, nseg, L, s0, kT_dst, v_dst, lf_dst, tok0, kcol0, KT, Vr, vnew):
            own = mode != "pre"
            smp = mode == "smp"
            ld(xt[:, :, 0:n], xsrc, "xt")
            rms_rstd(xt[:, :, 0:n], "xt", n)
            modulate(xt[:, :, 0:n], "xt", n, nseg, L, s0, AM, "AM", 0)

            def k_cons(cb, ps, pk):
                if "K" not in KSKIP:
                    V(lambda e: e.tensor_copy(out=KT[:, cb, kcol0:kcol0 + n], in_=ps[:, 0:n]), [pk], ["KT"])
                if own:
                    sg, sk = STG.next()
                    A(lambda e: e.copy(out=sg[:, 0:n], in_=ps[:, 0:n]), [pk], [sk])
                    store(kT_dst[cb * 128:(cb + 1) * 128, tok0:tok0 + n], sg[:, 0:n], sk, "st_" + sk)
            lin_fm(w_in[:, :, C_AK:C_AK + 512], 512, n, k_cons)

            def v_cons(tb, rows, ps, pk):
                if smp:
                    A(lambda e: e.copy(out=vnew[0:rows, :], in_=ps[0:rows, :]), [pk], ["vnew"])
                    store(v_dst[tok0 + tb * 128:tok0 + tb * 128 + rows, :], vnew[0:rows, :], "vnew", "st_vnew")
                    return
                if "K" not in KSKIP:
                    V(lambda e: e.tensor_copy(out=Vr[0:rows, kcol0 // 128 + tb, :], in_=ps[0:rows, :]), [pk], ["Vr"])
                if own:
                    sg, sk = STG.next()
                    A(lambda e: e.copy(out=sg[0:rows, :], in_=ps[0:rows, :]), [pk], [sk])
                    store(v_dst[tok0 + tb * 128:tok0 + tb * 128 + rows, :], sg[0:rows, :], sk, "st_" + sk)
            lin_tm(w_in[:, :, C_AV:C_AV + 512], n, v_cons)

            for tb in range((n + 127) // 128):
                rows = min(128, n - tb * 128)
                ps, pk = PS.next()
                for kc in range(8):
                    S.op("tensor", lambda e, kc=kc, tb=tb, rows=rows, ps=ps: e.matmul(
                        ps[0:rows, 0:16], lhsT=hT[:, kc, tb * 128:tb * 128 + rows], rhs=wgf[:, kc, :],
                        start=(kc == 0), stop=(kc == 7)), reads=["wgf", "hT"], writes=[pk], inc=(kc == 7))
                V(lambda e, rows=rows, ps=ps, tb=tb: e.tensor_tensor(out=gatT[0:rows, tb, :], in0=ps[0:rows, 0:16],
                                                                     in1=bgt[0:rows, :], op=ALU.add), [pk, "bgt"], ["gatT"])
                A(lambda e, rows=rows, tb=tb: e.activation(out=gex[0:rows, :], in_=gatT[0:rows, tb, :], func=AF.Exp, scale=-1.0),
                  ["gatT"], ["gex"])
                A(lambda e, rows=rows: e.activation(out=gex[0:rows, :], in_=gex[0:rows, :], func=AF.Ln,
                                                    bias=onec[0:rows, 0:1], scale=1.0), ["gex", "onec"], ["gex"])
                V(lambda e, rows=rows, tb=tb: e.tensor_scalar(out=lfT[0:rows, tb, :], in0=gex[0:rows, :], scalar1=-1.0,
                                                              scalar2=None, op0=ALU.mult), ["gex"], ["lfT"])
                if own:
                    store(lf_dst[tok0 + tb * 128:tok0 + tb * 128 + rows, :], lfT[0:rows, tb, 0:8], "lfT", "st_lfT")

            if nseg == 1:
                rv = lambda cb: rawp[:, cb, 3:3 + n]
                pv = lambda ps: ps[:, 0:n]
            else:
                r4 = rawp[:, :, 0:nseg * (L + 3)].rearrange("p c (s l) -> p c s l", l=L + 3)
                rv = lambda cb: r4[:, cb, :, 3:3 + L]
                pv = lambda ps: ps[:, 0:n].rearrange("p (s l) -> p s l", l=L)

            def r_cons(cb, ps, pk):
                A(lambda e: e.copy(out=rv(cb), in_=pv(ps)), [pk], ["rawp"])
            lin_fm(w_in[:, :, C_BQ:C_BQ + 1024], 1024, n, r_cons)
            if nseg == 1:
                win = lambda j: rawp[:, :, j:j + n]
                wj = lambda j: bc_last(convw[:, :, j], n)
                o3 = lambda t: t[:, :, 0:n]
            else:
                win = lambda j: r4[:, :, :, j:j + L]
                wj = lambda j: bc_last(bc_last(convw[:, :, j], nseg), L)
                o3 = lambda t: t[:, :, 0:n].rearrange("p c (s l) -> p c s l", l=L)
            V(lambda e: e.tensor_tensor(out=o3(cacc), in0=win(0), in1=wj(0), op=ALU.mult), ["rawp", "convw"], ["cacc"])
            for j in range(1, 4):
                V(lambda e, j=j: e.tensor_tensor(out=o3(ctmp), in0=win(j), in1=wj(j), op=ALU.mult), ["rawp", "convw"], ["ctmp"])
                V(lambda e: e.tensor_tensor(out=o3(cacc), in0=o3(cacc), in1=o3(ctmp), op=ALU.add), ["cacc", "ctmp"], ["cacc"])
            for cb in range(4):
                A(lambda e, cb=cb: e.activation(out=mqk[:, cb, 0:n], in_=cacc[:, cb, 0:n], func=AF.Silu, bias=convb[:, cb:cb + 1]),
                  ["cacc", "convb"], ["mqk"])
            for cb in range(4, 8):
                A(lambda e, cb=cb: e.activation(out=kf[:, cb - 4, 0:n], in_=cacc[:, cb, 0:n], func=AF.Silu, bias=convb[:, cb:cb + 1]),
                  ["cacc", "convb"], ["kf"])
            V(lambda e: e.tensor_scalar(out=kf[:, :, 0:n], in0=kf[:, :, 0:n], scalar1=float(B_D) ** -0.5, scalar2=None, op0=ALU.mult),
              ["kf"], ["kf"])
            V(lambda e: e.tensor_copy(out=mqk[:, 4:8, 0:n], in_=kf[:, :, 0:n]), ["kf"], ["mqk"])

            def mv_cons(tb, rows, ps, pk):
                A(lambda e: e.copy(out=MV[0:rows, tb, :], in_=ps[0:rows, :]), [pk], ["MV"])
            lin_tm(w_in[:, :, C_BV:C_BV + 512], n, mv_cons)
            if own:
                def bo_cons(cb, ps, pk):
                    A(lambda e: e.activation(out=SBO[:, cb, 0:n], in_=ps[:, 0:n], func=AF.Sigmoid), [pk], ["SBO"])
                lin_fm(w_in[:, :, C_BO:C_BO + 512], 512, n, bo_cons)

                def q_cons(cb, ps, pk):
                    V(lambda e: e.tensor_scalar(out=QTp[:, cb, 0:n], in0=ps[:, 0:n], scalar1=float(A_D) ** -0.5, scalar2=None,
                                                op0=ALU.mult), [pk], ["QTp"])
                if "K" not in KSKIP:
                    lin_fm(w_in[:, :, C_AQ:C_AQ + 512], 512, n, q_cons)

        def fox_block(tb, kb, own):
            if "F" in KSKIP:
                return
            ps, pk = PS.next()
            cumsum_tm(ps, pk, cstb[:, 2, :], lfT[:, tb, 0:8], "lfT", 128, 8)
            A(lambda e: e.copy(out=cs_sb[:], in_=ps[:, 0:8]), [pk], ["cs_sb"])
            V(lambda e: e.scalar_tensor_tensor(out=NFk[:, kb, :], in0=cs_sb[:], scalar=-1.0, in1=carry[:], op0=ALU.mult, op1=ALU.subtract),
              ["cs_sb", "carry"], ["NFk"])
            split3(cs_sb[:], "cs_sb", 128, 8, sq_p, "sq_p")
            for g in range(2):
                pr, prk = PS.next()
                rowbc(pr, prk, sq_p, "sq_p", 128, 4 * g, 4)
                pr3 = pr[:, :].rearrange("p (h t) -> p h t", h=4)
                A(lambda e, g=g, pr3=pr3: e.copy(out=tot_sb[:, 4 * g:4 * g + 4], in_=pr3[:, :, 127]), [prk], ["tot_sb"])
                if own:
                    V(lambda e, g=g, pr3=pr3: e.tensor_tensor(out=Ff[:, 4 * g:4 * g + 4, tb * 128:(tb + 1) * 128], in0=pr3[0:64, :, :],
                                                              in1=bc_last(crel[0:64, 4 * g:4 * g + 4], 128), op=ALU.add),
                      [prk, "crel"], ["cacc"])
            V(lambda e: e.tensor_tensor(out=crel[:], in0=crel[:], in1=tot_sb[:], op=ALU.add), ["crel", "tot_sb"], ["crel"])
            V(lambda e: e.tensor_tensor(out=carry[:], in0=carry[:], in1=tot_sb[:], op=ALU.add), ["carry", "tot_sb"], ["carry"])

        def post_mix(n, nseg, L, s0, xsrc, y_dst, tok0):
            v3 = lambda a: a.rearrange("p (s l) -> p s l", l=L)
            def ga_cons(cb, ps, pk):
                A(lambda e: e.activation(out=SGA[:, cb, 0:n], in_=ps[:, 0:n], func=AF.Sigmoid), [pk], ["SGA"])
            lin_fm(w_in[:, :, C_GA:C_GA + 1024], 1024, n, ga_cons)
            def gb_cons(cb, ps, pk):
                A(lambda e: e.activation(out=SGB[:, cb, 0:n], in_=ps[:, 0:n], func=AF.Sigmoid), [pk], ["SGB"])
            lin_fm(w_in[:, :, C_GB:C_GB + 1024], 1024, n, gb_cons)
            def pa_cons(cb, ps, pk):
                V(lambda e: e.tensor_tensor(out=Yf[:, cb, 0:n], in0=ps[:, 0:n], in1=SGA[:, cb, 0:n], op=ALU.mult), [pk, "SGA"], ["ctmp"])
            lin_gen(w_pa, 4, 1024, n, lambda kc: attT[:, kc, 0:n], ["attT"], pa_cons)
            def pb_cons(cb, ps, pk):
                tt, tk = utmp.next()
                V(lambda e: e.tensor_tensor(out=tt[:, 0:n], in0=ps[:, 0:n], in1=SGB[:, cb, 0:n], op=ALU.mult), [pk, "SGB"], [tk])
                V(lambda e: e.tensor_tensor(out=YPRE[:, cb, 0:n], in0=tt[:, 0:n], in1=Yf[:, cb, 0:n], op=ALU.add), [tk, "ctmp"], ["YPRE"])
            lin_gen(w_pb, 4, 1024, n, lambda kc: hbT[:, kc, 0:n], ["hbT"], pb_cons)
            def wo_cons(cb, ps, pk):
                A(lambda e: e.copy(out=Yf[:, cb, 0:n], in_=ps[:, 0:n]), [pk], ["ctmp"])
            lin_gen(w_o, 8, 1024, n, lambda kc: YPRE[:, kc, 0:n], ["YPRE"], wo_cons)
            rms_rstd(Yf[:, :, 0:n], "ctmp", n)
            ld(xt[:, :, 0:n], xsrc, "xt")
            for kc in range(8):
                tt, tk = utmp.next()
                V(lambda e, kc=kc, tt=tt: e.tensor_tensor(out=tt[:, 0:n], in0=Yf[:, kc, 0:n], in1=rstd[:, 0:n], op=ALU.mult),
                  ["ctmp", "rstd"], [tk])
                V(lambda e, kc=kc, tt=tt: e.tensor_tensor(out=v3(tt[:, 0:n]), in0=v3(tt[:, 0:n]), in1=bc_last(GM[:, kc, s0:s0 + nseg], L),
                                                          op=ALU.mult), [tk, "GM"], [tk])
                V(lambda e, kc=kc, tt=tt: e.tensor_tensor(out=x1[:, kc, 0:n], in0=tt[:, 0:n], in1=xt[:, kc, 0:n], op=ALU.add),
                  [tk, "xt"], ["cacc"])
            rms_rstd(x1[:, :, 0:n], "cacc", n)
            modulate(x1[:, :, 0:n], "cacc", n, nseg, L, s0, AFm, "AFm", 24)
            def up_cons(cb, ps, pk):
                tt, tk = utmp.next()
                A(lambda e: e.activation(out=tt[:, 0:n], in_=ps[:, 0:n], func=AF.Relu), [pk], [tk])
                V(lambda e: e.tensor_tensor(out=UT[:, cb, 0:n], in0=tt[:, 0:n], in1=tt[:, 0:n], op=ALU.mult), [tk] + BIGK, BIGK)
            lin_fm(w_u, 4096, n, up_cons)
            for half in range(2):
                for kg in range(4):
                    wt, wk = wload(w_d[:, half * 4 + kg, :, :], 8, 512)
                    for cb in range(4):
                        for kc in range(8):
                            S.op("tensor", lambda e, kc=kc, cb=cb, wt=wt, kg=kg: e.matmul(
                                ACC[cb][:, 0:n], lhsT=wt[:, kc, cb * 128:(cb + 1) * 128], rhs=UT[:, kg * 8 + kc, 0:n],
                                start=(kg == 0 and kc == 0), stop=(kg == 3 and kc == 7)),
                                reads=[wk] + BIGK, writes=["acc%d" % cb], inc=(kc == 7))
                for cb in range(4):
                    A(lambda e, cb=cb, half=half: e.copy(out=Yf[:, half * 4 + cb, 0:n], in_=ACC[cb][:, 0:n]), ["acc%d" % cb], ["ctmp"])
            rms_rstd(Yf[:, :, 0:n], "ctmp", n)
            for kc in range(8):
                tt, tk = utmp.next()
                V(lambda e, kc=kc, tt=tt: e.tensor_tensor(out=tt[:, 0:n], in0=Yf[:, kc, 0:n], in1=rstd[:, 0:n], op=ALU.mult),
                  ["ctmp", "rstd"], [tk])
                V(lambda e, kc=kc, tt=tt: e.tensor_tensor(out=v3(tt[:, 0:n]), in0=v3(tt[:, 0:n]), in1=bc_last(GFm[:, kc, s0:s0 + nseg], L),
                                                          op=ALU.mult), [tk, "GFm"], [tk])
                V(lambda e, kc=kc, tt=tt: e.tensor_tensor(out=xt[:, kc, 0:n], in0=tt[:, 0:n], in1=x1[:, kc, 0:n], op=ALU.add),
                  [tk, "cacc"], ["xt"])
            store(y_dst[:, :, tok0:tok0 + n], xt[:, :, 0:n], "xt", "st_xt")

        with contextlib.ExitStack() as sst:
            def sbs(name, shape, dt=F32):
                return sst.enter_context(nc.sbuf_tensor(name, list(shape), dt))
            CstS = sbs("CstS", [128, NSEQ, 4, 128])
            nstS = sbs("nstS", [128, NSEQ, 4])
            mstS = sbs("mstS", [128, NSEQ, 4])
            cvst = sbs("cvst", [128, 8, NSEQ, 3])
            KTs = sbs("KTs", [128, 4, TS], BF16)
            vnew = sbs("vnew", [TS, 512])
            qtm = sbs("qtm", [TS, 512])
            ptb = sbs("ptb", [128, NSEQ * 16], I32)
            ptf = sbs("ptf", [128, NSEQ * 16])
            idx = sbs("idx", [128, NSEQ * 16], I32)
            KgR = Ring([sbs("Kg%d" % i, [128, 4, 512]) for i in range(2)], "Kg")
            Vg = sbs("Vg", [128, 4, 512])
            prod = Vg
            QR = Ring([sbs("qrow%d" % i, [128, 512]) for i in range(2)], "qrow")
            lfpg = sbs("lfpg", [128, 16, 8])
            XA = sbs("XA", [128, 17, 8]); XB = sbs("XB", [128, 17, 8])
            Gb = sbs("Gb", [128, 16, 8])
            Sx = sbs("Sx", [128, 512]); Pp = sbs("Pp", [128, 512]); Pr = sbs("Pr", [128, 32])
            Pnew = sbs("Pnew", [TS, 8, TS])
            csn = sbs("csn", [TS, 8])
            ld(CstS[:], sC, "Cst")
            ld(nstS[:], sN, "nst")
            ld(mstS[:], sM.partition_broadcast(128), "mst")
            if "I" not in KSKIP:
                ld(ptb[:], pt.partition_broadcast(128), "ptb")
                V(lambda e: e.tensor_copy(out=ptf[:], in_=ptb[:]), ["ptb"], ["ptf"])
                V(lambda e: e.tensor_scalar(out=ptf[:], in0=ptf[:], scalar1=128.0, scalar2=iotaf, op0=ALU.mult, op1=ALU.add),
                  ["ptf", "cstf"], ["ptf"])
                V(lambda e: e.tensor_copy(out=idx[:], in_=ptf[:]), ["ptf"], ["idx"])
            r4s = rawp[:, :, 0:NSEQ * 7].rearrange("p c (s l) -> p c s l", l=7)
            ld(cvst[:], sCV, "cvst")
            V(lambda e: e.tensor_copy(out=r4s[:, :, :, 0:3], in_=cvst[:]), ["cvst", "rawp"], ["rawp"])
            w_in_phase("smp", xT_smp, TS, NSEQ, 4, 1, kT_s, v_s, lf_s, 0, 0, KTs, None, vnew)
            store(cv_s, rawp[:, :, 0:NSEQ * 7], "rawp", "st_rawp")
            mchunk(TS, NSEQ, 4, 0, 0, cstf[0:TS, 3, 0:TS], cstf[0:TS, 4, 0:NSEQ], lambda b: CstS[:, b, :, :], lambda b: nstS[:, b, :],
                   mstS[:].rearrange("p s h -> p h s"), True, cstb[0:TS, 3, 0:TS])
            store(C_s, CstS[:].rearrange("p s h e -> p (s h) e"), "Cst", "st_Cst")
            store(n_s, nstS[:].rearrange("p s h -> p (s h)"), "nst", "st_nst")
            store(m_s, mstS[0:1, :, :].rearrange("p s h -> p (s h)"), "mst", "st_mst")
            if "S" not in KSKIP:
                def qt_cons(tb, rows, ps, pk):
                    V(lambda e: e.tensor_scalar(out=qtm[0:rows, :], in0=ps[0:rows, :], scalar1=float(A_D) ** -0.5, scalar2=None, op0=ALU.mult),
                      [pk], ["qtm"])
                lin_tm(w_in[:, :, C_AQ:C_AQ + 512], TS, qt_cons)
                S.op("sync", lambda e: e.dma_start(out=qscr, in_=qtm[:]), reads=["qtm"], writes=["qscr"], dma="st_qscr")
                ps, pk = PS.next()
                cumsum_tm(ps, pk, cstb[0:TS, 3, 0:TS], lfT[0:TS, 0, 0:8], "lfT", TS, 8)
                V(lambda e: e.tensor_scalar(out=csn[:], in0=ps[0:TS, 0:8], scalar1=-1.0, scalar2=None, op0=ALU.mult), [pk], ["csn"])
                for h in range(8):
                    pair, r0 = h // 2, (h % 2) * 64
                    ps, pk = PS.next()
                    S.op("tensor", lambda e, ps=ps, pair=pair, r0=r0: e.matmul(ps[0:TS, 0:TS], lhsT=KTs[r0:r0 + 64, pair, :],
                                                                               rhs=QTp[r0:r0 + 64, pair, 0:TS], start=True, stop=False),
                         reads=["KT", "QTp"], writes=[pk], inc=False)
                    S.op("tensor", lambda e, ps=ps: e.matmul(ps[0:TS, 0:TS], lhsT=identb[:, 0:TS], rhs=MASKS, start=False, stop=True),
                         reads=["cstb", "cst2b"], writes=[pk])
                    A(lambda e, ps=ps, h=h: e.activation(out=Pnew[:, h, :], in_=ps[0:TS, 0:TS], func=AF.Exp, bias=csn[:, h:h + 1]),
                      [pk, "csn"], ["Pnew"])
                NUMs, DENs = ACC[0], ACC[1]
                LFK = ["lfpg%d" % i for i in range(16)]
                VGK = ["Vg%d" % i for i in range(4)]
                V(lambda e: e.memset(NUMs[:, :], 0.0), [], ["acc0"])
                V(lambda e: e.memset(DENs[:, :], 0.0), [], ["acc1"])
                Sx4 = Sx[:, :].rearrange("p (t h g) -> p t h g", t=4, h=8)
                Pp4 = Pp[:, :].rearrange("p (t h g) -> p t h g", t=4, h=8)
                for b in range(NSEQ if USE_CACHE else 0):
                    qrs = []
                    for pg in range(16):
                        c = b * 16 + pg
                        S.op("gpsimd", lambda e, pg=pg, c=c: e.indirect_dma_start(
                            out=lfpg[:, pg, :], out_offset=None, in_=cache_lf,
                            in_offset=bass.IndirectOffsetOnAxis(ap=idx[:, c:c + 1], axis=0)), reads=["idx"], writes=["lfpg%d" % pg], dma="lfpg%d" % pg)
                    psg, pkg = PS.next()
                    S.op("tensor", lambda e, psg=psg: e.matmul(psg[:, 0:128], lhsT=cstf[:, 1, :], rhs=lfpg[:].rearrange("p g h -> p (g h)"),
                                                               start=True, stop=True), reads=LFK + ["cstf"], writes=[pkg])
                    V(lambda e: e.memset(XA[:], 0.0), ["XA"], ["XA"])
                    V(lambda e, psg=psg: e.tensor_copy(out=XA[:, 0:16, :], in_=psg[:, 0:128].rearrange("p (g h) -> p g h", h=8)), [pkg, "XA"], ["XA"])
                    src, dst, sk_, dk_ = XA, XB, "XA", "XB"
                    for sh in (1, 2, 4, 8):
                        V(lambda e, src=src, dst=dst: e.tensor_copy(out=dst[:], in_=src[:]), [sk_], [dk_])
                        V(lambda e, src=src, dst=dst, sh=sh: e.tensor_tensor(out=dst[:, 0:16 - sh, :], in0=src[:, 0:16 - sh, :],
                                                                             in1=src[:, sh:16, :], op=ALU.add), [sk_, dk_], [dk_])
                        src, dst, sk_, dk_ = dst, src, dk_, sk_
                    psw, pkw = PS.next()
                    S.op("tensor", lambda e, psw=psw: e.matmul(psw[:, 0:128], lhsT=cstf[:, 5, :], rhs=lfpg[:].rearrange("p g h -> p (g h)"),
                                                               start=True, stop=True), reads=LFK + ["cstf"], writes=[pkw])
                    V(lambda e, psw=psw, src=src: e.tensor_tensor(out=Gb[:], in0=psw[:, 0:128].rearrange("p (g h) -> p g h", h=8),
                                                                  in1=src[:, 1:17, :], op=ALU.add), [pkw, sk_], ["Gb"])
                    for g in range(4):
                        Kg, kgk = KgR.next()
                        KGK = [kgk + "_%d" % j for j in range(4)]
                        for j in range(4):
                            c = b * 16 + g * 4 + j
                            S.op("gpsimd", lambda e, j=j, c=c, Kg=Kg: e.indirect_dma_start(
                                out=Kg[:, j, :], out_offset=None, in_=cache_k,
                                in_offset=bass.IndirectOffsetOnAxis(ap=idx[:, c:c + 1], axis=0)), reads=["idx"], writes=[KGK[j]], dma=KGK[j])
                        for t in range(4):
                            qr, qk_ = QR.next()
                            S.op("sync", lambda e, b=b, t=t, qr=qr: e.dma_start(out=qr[:], in_=qscr[4 * b + t:4 * b + t + 1, :].partition_broadcast(128)),
                                 reads=["qscr"], writes=[qk_], dma=qk_)
                            V(lambda e, qr=qr, Kg=Kg: e.tensor_tensor(out=prod[:], in0=Kg[:], in1=bc_mid(qr[:], 4), op=ALU.mult),
                              KGK + [qk_], VGK)
                            V(lambda e, t=t, g=g: e.tensor_reduce(out=Sx4[:, t, :, g * 4:(g + 1) * 4].rearrange("p h g -> p g h"),
                                                                  in_=prod[:].rearrange("p g (h d) -> p g h d", h=8), axis=AX.X, op=ALU.add),
                              VGK, ["Sx"])
                    V(lambda e: e.tensor_tensor(out=Sx4, in0=Sx4, in1=bc_mid(Gb[:].rearrange("p g h -> p h g"), 4), op=ALU.add),
                      ["Sx", "Gb"], ["Sx"])
                    A(lambda e: e.activation(out=Pp[:], in_=Sx[:], func=AF.Exp), ["Sx"], ["Pp"])
                    V(lambda e: e.tensor_reduce(out=Pr[:, :].rearrange("p (t h) -> p t h", t=4), in_=Pp4, axis=AX.X, op=ALU.add), ["Pp"], ["Pr"])
                    Pr3 = Pr[:, :].rearrange("p (t h) -> p t h", t=4)
                    for g in range(4):
                        for j in range(4):
                            c = b * 16 + g * 4 + j
                            S.op("gpsimd", lambda e, j=j, c=c: e.indirect_dma_start(
                                out=Vg[:, j, :], out_offset=None, in_=cache_v,
                                in_offset=bass.IndirectOffsetOnAxis(ap=idx[:, c:c + 1], axis=0)), reads=["idx"], writes=["Vg%d" % j], dma="Vg%d" % j)
                        for j in range(4):
                            pg = g * 4 + j
                            for h in range(8):
                                col = (h * NSEQ + b) * 4
                                S.op("tensor", lambda e, j=j, pg=pg, h=h, col=col: e.matmul(
                                    NUMs[:, col:col + 4], lhsT=Vg[:, j, (h // 2) * 128:(h // 2 + 1) * 128], rhs=Pp4[:, :, h, pg],
                                    start=False, stop=False, skip_group_check=True), reads=["Vg%d" % j, "Pp"], writes=["acc0"], inc=(h == 7))
                    for h in range(8):
                        col = (h * NSEQ + b) * 4
                        S.op("tensor", lambda e, h=h, col=col: e.matmul(DENs[:, col:col + 4], lhsT=onesf, rhs=Pr3[:, :, h],
                                                                        start=False, stop=False, skip_group_check=True),
                             reads=["Pr", "cstf"], writes=["acc1"], inc=(h == 7))
                for b in range(NSEQ):
                    for h in range(8):
                        col = (h * NSEQ + b) * 4
                        S.op("tensor", lambda e, h=h, col=col, b=b: e.matmul(DENs[:, col:col + 4], lhsT=onesf[0:TS, :], rhs=Pnew[:, h, 4 * b:4 * b + 4],
                                                                             start=False, stop=True, skip_group_check=True),
                             reads=["Pnew", "cstf"], writes=["acc1"], inc=False)
                        S.op("tensor", lambda e, h=h, col=col, b=b: e.matmul(NUMs[:, col:col + 4], lhsT=vnew[:, (h // 2) * 128:(h // 2 + 1) * 128],
                                                                             rhs=Pnew[:, h, 4 * b:4 * b + 4], start=False, stop=True, skip_group_check=True),
                             reads=["Pnew", "vnew"], writes=["acc0"], inc=(h == 7))
                for h in range(8):
                    pair, r0 = h // 2, (h % 2) * 64
                    V(lambda e, h=h, r0=r0: e.reciprocal(out=rec[r0:r0 + 64, 0:TS], in_=DENs[r0:r0 + 64, h * TS:(h + 1) * TS]), ["acc1"], ["rec"])
                    V(lambda e, h=h, r0=r0, pair=pair: e.tensor_tensor(out=attT[r0:r0 + 64, pair, 0:TS], in0=NUMs[r0:r0 + 64, h * TS:(h + 1) * TS],
                                                                       in1=rec[r0:r0 + 64, 0:TS], op=ALU.mult), ["acc0", "rec"], ["attT"])
            if "S" not in KSKIP and "M" not in KSKIP:
                post_mix(TS, NSEQ, 4, 1, xT_smp, yT_s, 0)
            S.barrier()
            S.emit()
            S.reset()

        with contextlib.ExitStack() as pst:
            KTp = pst.enter_context(nc.sbuf_tensor("KTp", [128, 4, 2 * TOK], BF16))
            Vres = pst.enter_context(nc.sbuf_tensor("Vres", [128, 32, 512], BF16))
            NFk = pst.enter_context(nc.sbuf_tensor("NFk", [128, 32, 8], F32))
            psb = lambda name, shape, dt=F32: pst.enter_context(nc.sbuf_tensor(name, list(shape), dt))
            FQa = psb("FQa", [128, 8, T], BF16)
            V(lambda e: e.memset(FQa[:], 0.0), [], ["FQa"])
            WP.tiles.append(psb("wp2", [128, 8, 512], BF16))
            WP.i = 0
            BIAS = psb("BIAS", [128, 32, 8])
            PTR = Ring([psb("PT%d" % i, [128, T], BF16) for i in range(2)], "PT")
            CstP = psb("CstP", [128, 4, 128]); nstP = psb("nstP", [128, 4]); mstP = psb("mstP", [128, 4, 1])
            V(lambda e: e.memset(CstP[:], 0.0), [], ["Cst"])
            V(lambda e: e.memset(nstP[:], 0.0), [], ["nst"])
            V(lambda e: e.memset(mstP[:], 0.0), [], ["mst"])
            V(lambda e: e.memset(rawp[:], 0.0), ["rawp"], ["rawp"])
            V(lambda e: e.memset(carry[:], 0.0), ["carry"], ["carry"])
            V(lambda e: e.memset(crel[:], 0.0), ["crel"], ["crel"])
            onescol = cstf[:, 1, 0:1]
            for ti in range(NT):
                w_in_phase("pre", xT_pre[:, :, ti * T:(ti + 1) * T], T, 1, T, 0, None, None, None, 0, ti * T, KTp, Vres, None)
                for tb in range(2):
                    fox_block(tb, ti * 2 + tb, False)
                    mchunk(128, 1, 128, tb, tb * 128, cstf[:, 2, :], onescol, lambda b: CstP[:], lambda b: nstP[:], mstP[:], False, cstb[:, 2, :])
                V(lambda e: e.tensor_copy(out=rawp[:, :, 0:3], in_=rawp[:, :, T:T + 3]), ["rawp"], ["rawp"])
            for (tl, kk) in ((CstP, "Cst"), (nstP, "nst"), (mstP, "mst"), (carry, "carry")):
                V(lambda e, tl=tl: e.tensor_scalar(out=tl[:], in0=tl[:], scalar1=flg[:, 0:1], scalar2=None, op0=ALU.mult), [kk, "flg"], [kk])
            V(lambda e: e.tensor_scalar(out=rawp[:, :, 0:3], in0=rawp[:, :, 0:3], scalar1=flg[:, 0:1], scalar2=None, op0=ALU.mult),
              ["rawp", "flg"], ["rawp"])
            for ti in range(NT):
                xs_ = xT_own[:, :, ti * T:(ti + 1) * T]
                V(lambda e: e.tensor_copy(out=Rt[:], in_=carry[:]), ["carry"], ["Rt"])
                V(lambda e: e.memset(crel[:], 0.0), ["crel"], ["crel"])
                w_in_phase("own", xs_, T, 1, T, 0, kT_o, v_o, lf_o, ti * T, TOK + ti * T, KTp, Vres, None)
                for tb in range(2):
                    fox_block(tb, 16 + ti * 2 + tb, True)
                    mchunk(128, 1, 128, tb, tb * 128, cstf[:, 2, :], onescol, lambda b: CstP[:], lambda b: nstP[:], mstP[:], True, cstb[:, 2, :])
                if ti == NT - 1:
                    store(cv_o, rawp[:, :, T:T + 3], "rawp", "st_rawp")
                V(lambda e: e.tensor_copy(out=rawp[:, :, 0:3], in_=rawp[:, :, T:T + 3]), ["rawp"], ["rawp"])
                if "F" in KSKIP:
                    continue
                V(lambda e: e.tensor_copy(out=FQa[0:64], in_=Ff[:, :, 0:T]), ["cacc"], ["FQa"])
                V(lambda e: e.tensor_tensor(out=FQa[32:64], in0=Ff[32:64, :, 0:T], in1=FQa[32:64], op=ALU.subtract), ["cacc", "FQa"], ["FQa"])
                nkb = 16 + 2 * (ti + 1)
                V(lambda e, nkb=nkb: e.tensor_tensor(out=BIAS[:, 0:nkb, :], in0=NFk[:, 0:nkb, :], in1=bc_mid(Rt[:], nkb), op=ALU.add),
                  ["NFk", "Rt"], ["BIAS"])
                V(lambda e: e.tensor_scalar(out=BIAS[:, 0:16, :], in0=BIAS[:, 0:16, :], scalar1=FLAGM[:, 0:1], scalar2=None, op0=ALU.add),
                  ["BIAS", "FLAGM"], ["BIAS"])
                for h in range(8 if "P" not in KSKIP else 0):
                    pair, r0 = h // 2, (h % 2) * 64
                    num, den = ACC[2 * (h % 2)], ACC[2 * (h % 2) + 1]
                    nk_, dk_ = "acc%d" % (2 * (h % 2)), "acc%d" % (2 * (h % 2) + 1)
                    def pv_den(kb, ptile, ptk, pair=pair, num=num, den=den, nk_=nk_, dk_=dk_, nkb=nkb):
                        S.op("tensor", lambda e: e.matmul(
                            num[:, 0:T], lhsT=Vres[:, kb, pair * 128:(pair + 1) * 128], rhs=ptile[:], start=(kb == 0), stop=(kb == nkb - 1)),
                            reads=["Vr", ptk], writes=[nk_], inc=False)
                        S.op("tensor", lambda e: e.matmul(
                            den[:, 0:T], lhsT=onesb, rhs=ptile[:], start=(kb == 0), stop=(kb == nkb - 1)),
                            reads=["cstb", ptk], writes=[dk_])
                    prev = None
                    for kb in range(nkb):
                        dj = kb - (nkb - 2)
                        ps, pk = PS.next()
                        S.op("tensor", lambda e, ps=ps, kb=kb, pair=pair, r0=r0: e.matmul(
                            ps[:, 0:T], lhsT=KTp[r0:r0 + 64, pair, kb * 128:(kb + 1) * 128], rhs=QTp[r0:r0 + 64, pair, 0:T],
                            start=True, stop=False), reads=["KT", "QTp"], writes=[pk], inc=False)
                        S.op("tensor", lambda e, ps=ps, h=h, dj=dj: e.matmul(ps[:, 0:T], lhsT=AUGK, rhs=FQa[:, h, :], start=False, stop=(dj < 0)),
                             reads=["cst2b", "FQa"], writes=[pk], inc=(dj < 0))
                        if dj >= 0:
                            S.op("tensor", lambda e, ps=ps, dj=dj: e.matmul(ps[:, 0:T], lhsT=identb, rhs=MASKD(dj), start=False, stop=True),
                                 reads=["cstb", "cst2b"], writes=[pk])
                        ptile, ptk = PTR.next()
                        A(lambda e, ps=ps, ptile=ptile, kb=kb, h=h: e.activation(out=ptile[:], in_=ps[:, 0:T], func=AF.Exp,
                                                                                 bias=BIAS[:, kb, h:h + 1]), [pk, "BIAS"], [ptk])
                        if prev is not None:
                            pv_den(*prev)
                        prev = (kb, ptile, ptk)
                    pv_den(*prev)
                    V(lambda e, r0=r0, den=den: e.reciprocal(out=rec[r0:r0 + 64, 0:T], in_=den[r0:r0 + 64, 0:T]), [dk_], ["rec"])
                    V(lambda e, r0=r0, num=num, pair=pair: e.tensor_tensor(out=attT[r0:r0 + 64, pair, 0:T], in0=num[r0:r0 + 64, 0:T],
                                                                           in1=rec[r0:r0 + 64, 0:T], op=ALU.mult), [nk_, "rec"], ["attT"])
                if "P" not in KSKIP and "M" not in KSKIP:
                    post_mix(T, 1, T, 0, xs_, yT_o, ti * T)
            store(C_o, CstP[:], "Cst", "st_Cst")
            store(n_o, nstP[:], "nst", "st_nst")
            store(m_o, mstP[0:1, :, 0], "mst", "st_mst")
            S.barrier()
            S.emit()
            S.reset()

        S.barrier()
        S.emit()
    return nc


_NC_CACHE = {}


def _rk(w):
    K, N = w.shape
    return np.ascontiguousarray(w.reshape(K // 128, 128, N).transpose(1, 0, 2))


def _bk(w):
    K, N = w.shape
    return np.ascontiguousarray(w.reshape(K // 128, 128, N // 512, 512).transpose(1, 2, 0, 3))


def kernel(x_prompt, x_sample, cache_k, cache_v, cache_logf, page_table, state_C, state_n, state_m, state_conv,
           c_prompt, c_sample, w_ada, b_ada, g_pre_mix, g_post_mix, g_pre_mlp, g_post_mlp, w_in, b_fox_f,
           b_ml_i, b_ml_f, conv_w, conv_b, w_proj_a, w_proj_b, w_out, w_up, w_down):
    f32 = np.float32
    if "nc" not in _NC_CACHE:
        _NC_CACHE["nc"] = build()
    nc = _NC_CACHE["nc"]
    x_prompt = np.asarray(x_prompt, f32)
    x_sample = np.asarray(x_sample, f32)
    w_in0 = np.asarray(w_in, f32)[0]
    w_in_p = np.concatenate([w_in0[:, O_AQ:O_AQ + 512], w_in0[:, O_AK:O_AK + 512], w_in0[:, O_AV:O_AV + 512],
                             w_in0[:, O_BQ:O_BQ + 512], w_in0[:, O_BK:O_BK + 512], w_in0[:, O_BV:O_BV + 512],
                             w_in0[:, O_BO:O_BO + 512], w_in0[:, O_GA:O_GA + 1024], w_in0[:, O_GB:O_GB + 1024]], axis=1)
    w_in_r = _bk(w_in_p)
    w_g = _rk(np.concatenate([w_in0[:, O_AF:O_AF + 8], w_in0[:, O_BI:O_BI + 8]], axis=1))
    w_ada_r = _bk(np.asarray(w_ada, f32)[0])
    b_ada_r = np.ascontiguousarray(np.asarray(b_ada, f32)[0].reshape(48, 128).T)
    g4 = np.stack([np.asarray(g, f32)[0].reshape(8, 128).T for g in (g_pre_mix, g_post_mix, g_pre_mlp, g_post_mlp)], axis=1)
    g4 = np.ascontiguousarray(g4)
    b_g = np.concatenate([np.asarray(b_fox_f, f32)[0], np.asarray(b_ml_i, f32)[0], np.asarray(b_ml_f, f32)[0]])
    b_g = np.ascontiguousarray(np.broadcast_to(b_g[None, :], (128, 16)))
    cst = np.zeros((128, 6, 128), f32)
    cst[:, 5, :] = np.tril(np.ones((128, 128)), -1)
    cst2 = np.zeros((128, 3, 256), f32)
    kk = np.arange(128)[:, None]; qq = np.arange(256)[None, :]
    cst2[:, 0, :] = np.where(qq >= kk, 0.0, -30000.0)
    cst2[:, 1, :] = np.where(qq >= kk + 128, 0.0, -30000.0)
    k6 = np.arange(64)[:, None]; q6 = np.arange(64)[None, :]
    cst2[0:64, 2, 0:64] = np.where((k6 // 4 == q6 // 4) & (k6 <= q6), 0.0, -30000.0)
    cst2[0, 2, 64:192] = 1.0
    cst2[32, 2, 64:192] = 1.0
    cst[:, 4, 127] = np.arange(128)
    w_pa_r = _bk(np.asarray(w_proj_a, f32)[0]); w_pb_r = _bk(np.asarray(w_proj_b, f32)[0])
    w_o_r = _bk(np.asarray(w_out, f32)[0]); w_u_r = _bk(np.asarray(w_up, f32)[0])
    w_d_r = np.ascontiguousarray(np.asarray(w_down, f32)[0].reshape(4, 8, 128, 2, 512).transpose(2, 3, 0, 1, 4)).reshape(128, 8, 8, 512)
    ck = np.asarray(cache_k, f32).reshape(2560 * 128, 512); cv = np.asarray(cache_v, f32).reshape(2560 * 128, 512)
    clf = np.asarray(cache_logf, f32).reshape(2560 * 128, 8)
    ptab = np.asarray(page_table).astype(np.int32)
    cst[:, 0, :] = np.eye(128)
    cst[:, 1, :] = 1.0
    cst[:, 2, :] = np.triu(np.ones((128, 128)))
    for b in range(NSEQ):
        cst[4 * b:4 * b + 4, 3, 4 * b:4 * b + 4] = np.triu(np.ones((4, 4)))
        cst[4 * b:4 * b + 4, 4, b] = 1.0
    conv_w_r = np.ascontiguousarray(np.asarray(conv_w, f32)[0].reshape(4, 8, 128).transpose(2, 1, 0))
    conv_b_r = np.ascontiguousarray(np.asarray(conv_b, f32)[0].reshape(8, 128).T)
    state_C = np.asarray(state_C, f32); state_n = np.asarray(state_n, f32); state_m = np.asarray(state_m, f32)
    state_conv = np.asarray(state_conv, f32)

    def fm(x):
        return np.ascontiguousarray(x.T.reshape(8, 128, x.shape[0]).transpose(1, 0, 2))

    in_maps = []
    for c in range(NCORE):
        s, half = c // 2, c % 2
        xo = x_prompt[s, half * TOK:(half + 1) * TOK]
        xs = x_sample[c * NSEQ:(c + 1) * NSEQ].reshape(TS, D)
        cc = np.concatenate([np.asarray(c_prompt, f32)[s:s + 1], np.asarray(c_sample, f32)[c * NSEQ:(c + 1) * NSEQ]], axis=0)
        in_maps.append({
            "xT_own": fm(xo), "xT_smp": fm(xs), "cT": fm(cc),
            "xT_pre": fm(x_prompt[s, 0:TOK]), "flag": np.full((128, 1), float(half), f32),
            "conv_w": conv_w_r, "conv_b": conv_b_r, "cst2": cst2,
            "w_pa": w_pa_r, "w_pb": w_pb_r, "w_o": w_o_r, "w_u": w_u_r, "w_d": w_d_r,
            **({"cache_k": ck, "cache_v": cv, "cache_lf": clf} if USE_CACHE else {}),
            "pt": np.ascontiguousarray(ptab[c * NSEQ:(c + 1) * NSEQ].reshape(1, NSEQ * 16)),
            "sC": np.ascontiguousarray(state_C[0, c * NSEQ:(c + 1) * NSEQ].transpose(2, 0, 1, 3)),
            "sN": np.ascontiguousarray(state_n[0, c * NSEQ:(c + 1) * NSEQ].transpose(2, 0, 1)),
            "sM": np.ascontiguousarray(state_m[0, c * NSEQ:(c + 1) * NSEQ][None]),
            "sCV": np.ascontiguousarray(state_conv[0, c * NSEQ:(c + 1) * NSEQ].reshape(NSEQ, 3, 8, 128).transpose(3, 2, 0, 1)),
            "w_ada": w_ada_r, "b_ada": b_ada_r, "g4": g4, "w_in": w_in_r, "w_g": w_g, "b_g": b_g, "cst": cst,
        })
    res = run_bass_kernel_spmd(nc, in_maps, core_ids=list(range(NCORE))).results

    B, SEQ = 4, 4096
    y_prompt = np.zeros((B, SEQ, D), f32)
    y_sample = np.zeros((128, 4, D), f32)
    new_k_p = np.zeros((1, B, SEQ, A_H, A_D), f32)
    new_v_p = np.zeros((1, B, SEQ, A_H, A_D), f32)
    new_lf_p = np.zeros((1, B, SEQ, A_H), f32)
    new_C_p = np.zeros((1, B, B_H, B_D, B_D), f32)
    new_n_p = np.zeros((1, B, B_H, B_D), f32)
    new_m_p = np.zeros((1, B, B_H), f32)
    new_cv_p = np.zeros((1, B, 3, D), f32)
    new_k_s = np.zeros((1, 128, 4, A_H, A_D), f32)
    new_v_s = np.zeros((1, 128, 4, A_H, A_D), f32)
    new_lf_s = np.zeros((1, 128, 4, A_H), f32)
    new_C_s = np.zeros((1, 128, B_H, B_D, B_D), f32)
    new_n_s = np.zeros((1, 128, B_H, B_D), f32)
    new_m_s = np.zeros((1, 128, B_H), f32)
    new_cv_s = np.zeros((1, 128, 3, D), f32)
    for c in range(NCORE):
        s, half = c // 2, c % 2
        r = res[c]
        sl = slice(half * TOK, (half + 1) * TOK)
        y_prompt[s, sl] = r["yT_o"].transpose(2, 1, 0).reshape(TOK, D)
        y_sample[c * NSEQ:(c + 1) * NSEQ] = r["yT_s"].transpose(2, 1, 0).reshape(NSEQ, 4, D)
        new_k_p[0, s, sl] = r["kT_o"].T.reshape(TOK, A_H, A_D)
        new_v_p[0, s, sl] = r["v_o"].reshape(TOK, A_H, A_D)
        new_lf_p[0, s, sl] = r["lf_o"]
        if half == 1:
            new_cv_p[0, s] = r["cv_o"].transpose(2, 1, 0).reshape(3, D)
        ss = slice(c * NSEQ, (c + 1) * NSEQ)
        new_k_s[0, ss] = r["kT_s"].T.reshape(NSEQ, 4, A_H, A_D)
        new_v_s[0, ss] = r["v_s"].reshape(NSEQ, 4, A_H, A_D)
        new_lf_s[0, ss] = r["lf_s"].reshape(NSEQ, 4, A_H)
        new_cv_s[0, ss] = r["cv_s"].reshape(128, 8, NSEQ, 7)[:, :, :, 4:7].transpose(2, 3, 1, 0).reshape(NSEQ, 3, D)
        new_C_s[0, ss] = r["C_s"].reshape(128, NSEQ, 4, 128).transpose(1, 2, 0, 3)
        new_n_s[0, ss] = r["n_s"].reshape(128, NSEQ, 4).transpose(1, 2, 0)
        new_m_s[0, ss] = r["m_s"].reshape(NSEQ, 4)
        if half == 1:
            new_C_p[0, s] = r["C_o"].transpose(1, 0, 2)
            new_n_p[0, s] = r["n_o"].T
            new_m_p[0, s] = r["m_o"][0]
    return (y_prompt, y_sample, new_k_p, new_v_p, new_lf_p, new_C_p, new_n_p, new_m_p, new_cv_p,
            new_k_s, new_v_s, new_lf_s, new_C_s, new_n_s, new_m_s, new_cv_s)
```

```python
import contextlib
import numpy as np
import concourse.bass as bass
import concourse.mybir as mybir
from concourse.bass_utils import run_bass_kernel_spmd

F32 = mybir.dt.float32
BF16 = mybir.dt.bfloat16
I32 = mybir.dt.int32
ALU = mybir.AluOpType
AF = mybir.ActivationFunctionType
AX = mybir.AxisListType
ENGS = ("tensor", "vector", "scalar", "gpsimd", "sync")

D = 1024
NCORE = 8
TOK = 2048
T = 256
NT = TOK // T
TS = 64
NSEQ = 16
A_H, A_D = 8, 64
B_H, B_D = 4, 128
NIN = 5648
USE_CACHE = True
import os
KSKIP = os.environ.get("KSKIP", "")
O_AQ, O_AK, O_AV, O_AF, O_BQ, O_BK, O_BV, O_BI, O_BF, O_BO, O_GA, O_GB = (
    0, 512, 1024, 1536, 1544, 2056, 2568, 3080, 3084, 3088, 3600, 4624)
C_AQ, C_AK, C_AV, C_BQ, C_BK, C_BV, C_BO, C_GA, C_GB = (0, 512, 1024, 1536, 2048, 2560, 3072, 3584, 4608)
NINP = 5632


class WB:
    def __init__(self, ap4, base=0, width=None):
        self.ap4, self.base, self.width = ap4, base, width

    def __getitem__(self, key):
        _, ks, cs = key
        c0 = self.base + (cs.start or 0)
        c1 = self.base + (cs.stop if cs.stop is not None else self.width)
        if c1 - c0 <= 512:
            assert c0 % 512 == 0 and ks == slice(None)
            return self.ap4[:, c0 // 512, :, 0:c1 - c0]
        return WB(self.ap4, c0, c1 - c0)


class Sched:
    def __init__(self, nc):
        self.nc = nc
        self.phase = 0
        self.reset()

    def reset(self):
        self.phase += 1
        self.ops = {e: [] for e in ENGS}
        self.cnt = {}
        self.known = {e: {} for e in ENGS}
        self.keys = {}
        self.pseudo = {}
        self.pending = {e: ([], []) for e in ENGS}
        self.semnames = ["E_" + e for e in ENGS]

    def _need(self, eng, ev, waits, kind="raw"):
        if ev is None:
            return
        s, v = ev
        if s == "E_" + eng:
            if eng == "tensor":
                return
            if eng in ("vector", "scalar") and kind != "raw":
                return
        if self.known[eng].get(s, 0) >= v:
            return
        self.known[eng][s] = v
        waits.append((s, v))

    def op(self, eng, fn, reads=(), writes=(), inc=True, dma=None):
        pacc = [k for k in reads if k.startswith(("ps", "acc"))]
        if pacc:
            reads = [k for k in reads if k not in pacc]
            writes = list(writes) + pacc
        waits = []
        for k in reads:
            st = self.keys.get(k)
            if st:
                self._need(eng, st[0], waits, "raw")
        for k in writes:
            st = self.keys.get(k)
            if st:
                if k in pacc:
                    self._need(eng, st[0], waits, "war" if self.pseudo.get(k) else "raw")
                else:
                    self._need(eng, st[0], waits, "waw")
                for ev in st[1]:
                    self._need(eng, ev, waits, "war")
        for k in writes:
            self.pseudo[k] = (k in pacc)
        if dma is not None:
            s = "D_" + dma
            if s not in self.cnt:
                self.semnames.append(s)
            self.cnt[s] = self.cnt.get(s, 0) + 16
            self._apply((s, self.cnt[s]), reads, writes)
            self.ops[eng].append((waits, fn, (s, 16)))
            return
        pr, pw = self.pending[eng]
        if not inc:
            pr.extend(reads)
            pw.extend(writes)
            self.ops[eng].append((waits, fn, None))
            return
        s = "E_" + eng
        self.cnt[s] = self.cnt.get(s, 0) + 1
        self._apply((s, self.cnt[s]), list(reads) + pr, list(writes) + pw)
        self.pending[eng] = ([], [])
        self.ops[eng].append((waits, fn, (s, 1)))

    def _apply(self, ev, reads, writes):
        for k in reads:
            self.keys.setdefault(k, [None, []])[1].append(ev)
        for k in writes:
            self.keys[k] = [ev, []]

    def barrier(self):
        for e in ENGS:
            waits = []
            for s, v in self.cnt.items():
                self._need(e, (s, v), waits)
            self.ops[e].append((waits, None, None))

    def emit(self):
        nc = self.nc
        with contextlib.ExitStack() as st:
            sems = {n: st.enter_context(nc.semaphore("%s_p%d" % (n, self.phase))) for n in self.semnames}
            block = st.enter_context(nc.Block())

            def run(engname):
                def body(e):
                    for waits, fn, inc in self.ops[engname]:
                        for s, v in waits:
                            e.wait_ge(sems[s], v)
                        if fn is None:
                            continue
                        ins = fn(e)
                        if inc is not None:
                            ins.then_inc(sems[inc[0]], inc[1])
                return body
            block.tensor(run("tensor"))
            block.vector(run("vector"))
            block.scalar(run("scalar"))
            block.gpsimd(run("gpsimd"))
            block.sync(run("sync"))


def bc_last(ap, n):
    return bass.AP(ap.tensor, ap.offset, [list(x) for x in ap.ap] + [[0, n]])


class Ring:
    def __init__(self, tiles, name):
        self.tiles = tiles
        self.name = name
        self.i = 0

    def next(self):
        j = self.i % len(self.tiles)
        self.i += 1
        return self.tiles[j], "%s%d" % (self.name, j)


def build():
    nc = bass.Bass("TRN2", target_bir_lowering=False)

    def din(name, shape, dt=F32):
        return nc.dram_tensor(name, list(shape), dt, kind="ExternalInput").ap()

    def dout(name, shape, dt=F32):
        return nc.dram_tensor(name, list(shape), dt, kind="ExternalOutput").ap()

    xT_own = din("xT_own", [128, 8, TOK])
    xT_smp = din("xT_smp", [128, 8, TS])
    cT = din("cT", [128, 8, 17])
    w_ada = WB(din("w_ada", [128, 12, 8, 512]))
    b_ada = din("b_ada", [128, 48])
    g4 = din("g4", [128, 4, 8])
    w_in = WB(din("w_in", [128, NINP // 512, 8, 512]))
    w_g = din("w_g", [128, 8, 16])
    b_g = din("b_g", [128, 16])
    cst = din("cst", [128, 6, 128])
    xT_pre = din("xT_pre", [128, 8, TOK])
    flag = din("flag", [128, 1])
    conv_w = din("conv_w", [128, 8, 4])
    conv_b = din("conv_b", [128, 8])
    sC = din("sC", [128, NSEQ, 4, 128])
    sN = din("sN", [128, NSEQ, 4])
    sM = din("sM", [1, NSEQ, 4])
    sCV = din("sCV", [128, 8, NSEQ, 3])
    cst2 = din("cst2", [128, 3, 256])
    w_pa = WB(din("w_pa", [128, 2, 4, 512]))
    w_pb = WB(din("w_pb", [128, 2, 4, 512]))
    w_o = WB(din("w_o", [128, 2, 8, 512]))
    w_u = WB(din("w_u", [128, 8, 8, 512]))
    w_d = din("w_d", [128, 8, 8, 512])
    if USE_CACHE:
        cache_k = din("cache_k", [2560 * 128, 512])
        cache_v = din("cache_v", [2560 * 128, 512])
        cache_lf = din("cache_lf", [2560 * 128, 8])
    pt = din("pt", [1, NSEQ * 16], I32)
    qscr = dout("qscr", [TS, 512])

    kT_o = dout("kT_o", [512, TOK])
    v_o = dout("v_o", [TOK, 512])
    lf_o = dout("lf_o", [TOK, 8])
    cv_o = dout("cv_o", [128, 8, 3])
    kT_s = dout("kT_s", [512, TS])
    v_s = dout("v_s", [TS, 512])
    lf_s = dout("lf_s", [TS, 8])
    cv_s = dout("cv_s", [128, 8, NSEQ * 7])
    yT_o = dout("yT_o", [128, 8, TOK])
    yT_s = dout("yT_s", [128, 8, TS])
    C_o = dout("C_o", [128, 4, 128])
    n_o = dout("n_o", [128, 4])
    m_o = dout("m_o", [1, 4])
    C_s = dout("C_s", [128, NSEQ * 4, 128])
    n_s = dout("n_s", [128, NSEQ * 4])
    m_s = dout("m_s", [1, NSEQ * 4])

    S = Sched(nc)
    with contextlib.ExitStack() as st:
        def sb(name, shape, dt=F32):
            return st.enter_context(nc.sbuf_tensor(name, list(shape), dt))

        PS = Ring([st.enter_context(nc.psum_tensor("ps%d" % i, [128, 512], F32)) for i in range(8)], "ps")
        WP = Ring([sb("wp%d" % i, [128, 8, 512], BF16) for i in range(3)], "wp")
        STG = Ring([sb("stg%d" % i, [128, 512], F32) for i in range(2)], "stg")

        cstf = sb("cstf", [128, 6, 128])
        cstb = sb("cstb", [128, 6, 128], BF16)
        g4t = sb("g4t", [128, 4, 8])
        badat = sb("badat", [128, 48])
        bgt = sb("bgt", [128, 16])
        ctt = sb("ctt", [128, 8, 17])
        csil = sb("csil", [128, 8, 17], BF16)
        MOD = sb("MOD", [128, 48, 17])
        AM = sb("AM", [128, 8, 17])
        GM = sb("GM", [128, 8, 17])
        AFm = sb("AFm", [128, 8, 17])
        GFm = sb("GFm", [128, 8, 17])
        wgf = sb("wgf", [128, 8, 16], BF16)
        onec = sb("onec", [128, 1])
        epsc = sb("epsc", [128, 1])

        ld = lambda dst, src, key: S.op("sync", lambda e: e.dma_start(out=dst, in_=src), writes=[key], dma=key)
        ld(cstf[:], cst, "cstf")
        ld(g4t[:], g4, "g4t")
        ld(badat[:], b_ada, "badat")
        ld(bgt[:], b_g, "bgt")
        ld(ctt[:], cT, "ctt")
        S.op("gpsimd", lambda e: e.dma_start(out=wgf[:], in_=w_g), writes=["wgf"], dma="wgf")
        S.op("vector", lambda e: e.tensor_copy(out=cstb[:], in_=cstf[:]), reads=["cstf"], writes=["cstb"])
        S.op("vector", lambda e: e.memset(onec[:], 1.0), writes=["onec"])
        S.op("vector", lambda e: e.memset(epsc[:], 1e-6), writes=["epsc"])
        identb = cstb[:, 0, :]
        onesb = cstb[:, 1, :]
        trif = cstf[:, 2, :]

        def wload(src_ap, kc, ncols):
            tile, key = WP.next()
            if ncols == 512:
                S.op("gpsimd", lambda e: e.dma_start(out=tile[:, 0:kc, :].rearrange("p k c -> p (k c)"),
                                                     in_=src_ap.rearrange("p k c -> p (k c)"), max_dma_last_dim=8192),
                     writes=[key], dma=key)
            else:
                S.op("gpsimd", lambda e: e.dma_start(out=tile[:, 0:kc, 0:ncols], in_=src_ap), writes=[key], dma=key)
            return tile, key

        S.op("scalar", lambda e: e.activation(out=csil[:], in_=ctt[:], func=AF.Silu), reads=["ctt"], writes=["csil"])
        for blk in range(12):
            wt, wk = wload(w_ada[:, :, blk * 512:(blk + 1) * 512], 8, 512)
            ps, pk = PS.next()
            for jj in range(4):
                for kc in range(8):
                    S.op("tensor", lambda e, jj=jj, kc=kc, wt=wt, ps=ps: e.matmul(
                        ps[:, jj * 17:(jj + 1) * 17], lhsT=wt[:, kc, jj * 128:(jj + 1) * 128], rhs=csil[:, kc, :],
                        start=(kc == 0), stop=(kc == 7)), reads=[wk, "csil"], writes=[pk], inc=(kc == 7 and jj == 3))
            for jj in range(4):
                j = blk * 4 + jj
                S.op("vector", lambda e, jj=jj, j=j, ps=ps: e.tensor_scalar(
                    out=MOD[:, j, :], in0=ps[:, jj * 17:(jj + 1) * 17], scalar1=badat[:, j:j + 1], scalar2=None,
                    op0=ALU.add), reads=[pk, "badat"], writes=["MOD"])
        for (dst, dk, gi, mj, plus1) in ((AM, "AM", 0, 8, True), (GM, "GM", 1, 16, False),
                                         (AFm, "AFm", 2, 32, True), (GFm, "GFm", 3, 40, False)):
            gb = bc_last(g4t[:, gi, :], 17)
            if plus1:
                S.op("vector", lambda e, dst=dst, mj=mj, gb=gb: e.scalar_tensor_tensor(
                    out=dst[:], in0=MOD[:, mj:mj + 8, :], scalar=1.0, in1=gb, op0=ALU.add, op1=ALU.mult),
                    reads=["MOD", "g4t"], writes=[dk])
            else:
                S.op("vector", lambda e, dst=dst, mj=mj, gb=gb: e.tensor_tensor(
                    out=dst[:], in0=MOD[:, mj:mj + 8, :], in1=gb, op=ALU.mult), reads=["MOD", "g4t"], writes=[dk])

        xt = sb("xt", [128, 8, T])
        sq = sb("sq", [128, 8, T], BF16)
        hT = sb("hT", [128, 8, T], BF16)
        rstd = sb("rstd", [128, T])
        tmpA = Ring([sb("tmpA%d" % i, [128, T]) for i in range(2)], "tmpA")
        gatT = sb("gatT", [128, 2, 16])
        gex = sb("gex", [128, 16])
        lfT = sb("lfT", [128, 2, 16])
        rawp = sb("rawp", [128, 8, T + 3])
        cacc = sb("cacc", [128, 8, T])
        ctmp = sb("ctmp", [128, 8, T])
        mqk = sb("mqk", [128, 8, T], BF16)
        kf = sb("kf", [128, 4, T])
        MV = sb("MV", [128, 2, 512], BF16)
        SBO = sb("SBO", [128, 4, T], BF16)
        hbT = sb("hbT", [128, 4, T], BF16)
        convw = sb("convw", [128, 8, 4])
        convb = sb("convb", [128, 8])
        flg = sb("flg", [128, 1])
        ld(convw[:], conv_w, "convw")
        ld(convb[:], conv_b, "convb")
        ld(flg[:], flag, "flg")
        a_sb = sb("a_sb", [128, 4]); b_sb = sb("b_sb", [128, 4]); w_sb = sb("w_sb", [128, 4]); wa_sb = sb("wa_sb", [128, 4])
        Ra = sb("Ra", [128, 4, 128])
        brow = sb("brow", [128, 512]); thr = sb("thr", [128, 512]); hd = brow; hd2 = thr
        amax = sb("amax", [128, 4, 16]); Mt = sb("Mt", [128, 4, 16]); dsi = sb("dsi", [128, 4, 16]); si = sb("si", [128, 4, 16])
        tmm = sb("tmm", [128, 4, 16]); Mtok = sb("Mtok", [128, 4])
        mkT = sb("mkT", [128, 4, 128], BF16)
        vp = sb("vp", [128, 4, 129], BF16); vpb = sb("vpb", [128, 4, 129], BF16)
        WM = Ra; AT = sb("AT", [128, 4, 128], BF16)
        CsR = Ring([sb("Cs%d" % i, [128, 4, 128], BF16) for i in range(2)], "Cs")
        nrR = Ring([sb("nr%d" % i, [128, 4, 128], BF16) for i in range(2)], "nr")
        S.op("vector", lambda e: e.memset(rawp[:], 0.0), writes=["rawp"])

        identf = cstf[:, 0, :]
        onesf = cstf[:, 1, :]
        ACC = [PS.tiles[4 + i] for i in range(4)]
        PS.tiles = PS.tiles[0:4]

        sp_r = sb("sp_r", [128, 8])
        sp_p = [sb("sp_p%d" % i, [128, 8], BF16) for i in range(3)]
        sq_p = [sb("sq_p%d" % i, [128, 8], BF16) for i in range(3)]
        RP = Ring([sb("RP%d" % i, [128, 4, 128], BF16) for i in range(2)], "RP")

        def split3(src, skey, rows, ncol, dst, dkey):
            V(lambda e: e.tensor_copy(out=dst[0][0:rows, 0:ncol], in_=src), [skey], [dkey + "0"])
            V(lambda e: e.tensor_tensor(out=sp_r[0:rows, 0:ncol], in0=src, in1=dst[0][0:rows, 0:ncol], op=ALU.subtract),
              [skey, dkey + "0"], ["sp_r"])
            V(lambda e: e.tensor_copy(out=dst[1][0:rows, 0:ncol], in_=sp_r[0:rows, 0:ncol]), ["sp_r"], [dkey + "1"])
            V(lambda e: e.tensor_tensor(out=sp_r[0:rows, 0:ncol], in0=sp_r[0:rows, 0:ncol], in1=dst[1][0:rows, 0:ncol], op=ALU.subtract),
              ["sp_r", dkey + "1"], ["sp_r"])
            V(lambda e: e.tensor_copy(out=dst[2][0:rows, 0:ncol], in_=sp_r[0:rows, 0:ncol]), ["sp_r"], [dkey + "2"])

        def cumsum_tm(ps, pk, TRIb, src, skey, rows, ncol):
            split3(src, skey, rows, ncol, sp_p, "sp_p")
            for j in range(3):
                S.op("tensor", lambda e, j=j: e.matmul(ps[0:rows, 0:ncol], lhsT=TRIb, rhs=sp_p[j][0:rows, 0:ncol],
                                                       start=(j == 0), stop=(j == 2)), reads=["sp_p%d" % j, "cstb"], writes=[pk], inc=(j == 2))

        def rowbc(ps, pk, src, skey, rows, h0, nh):
            for j in range(3):
                rp, rk = RP.next()
                V(lambda e, j=j, rp=rp: e.tensor_tensor(out=rp[0:rows, 0:nh, 0:rows], in0=bc_mid(identb[0:rows, 0:rows], nh),
                                                        in1=bc_last(src[j][0:rows, h0:h0 + nh], rows), op=ALU.mult),
                  [skey + str(j), "cstb"], [rk])
                S.op("tensor", lambda e, j=j, rp=rp: e.matmul(ps[:, 0:nh * rows].rearrange("p (h t) -> p h t", h=nh), lhsT=onesb[0:rows, :],
                                                             rhs=rp[0:rows, 0:nh, 0:rows], start=(j == 0), stop=(j == 2)),
                     reads=[rk, "cstb"], writes=[pk], inc=(j == 2))

        def bc_mid(ap, n):
            a = [list(x) for x in ap.ap]
            return bass.AP(ap.tensor, ap.offset, [a[0], [0, n]] + a[1:])

        def rms_rstd(src, skey, n):
            S.op("scalar", lambda e: e.activation(out=sq[:, :, 0:n], in_=src, func=AF.Square), reads=[skey], writes=["sq"])
            ps, pk = PS.next()
            for kc in range(8):
                S.op("tensor", lambda e, kc=kc, ps=ps: e.matmul(ps[:, 0:n], lhsT=onesb, rhs=sq[:, kc, 0:n],
                                                                start=(kc == 0), stop=(kc == 7)),
                     reads=["sq", "cstb"], writes=[pk], inc=(kc == 7))
            S.op("scalar", lambda e, ps=ps: e.activation(out=rstd[:, 0:n], in_=ps[:, 0:n], func=AF.Sqrt,
                                                         bias=epsc[:, 0:1], scale=1.0 / D),
                 reads=[pk, "epsc"], writes=["rstd"])
            S.op("vector", lambda e: e.reciprocal(out=rstd[:, 0:n], in_=rstd[:, 0:n]), reads=["rstd"], writes=["rstd"])

        def modulate(src, skey, n, nseg, L, s0, Amat, akey, sh_j):
            v3 = lambda a: a.rearrange("p (s l) -> p s l", l=L)
            for kc in range(8):
                tt, tk = tmpA.next()
                S.op("vector", lambda e, kc=kc, tt=tt: e.tensor_tensor(out=tt[:, 0:n], in0=src[:, kc, :], in1=rstd[:, 0:n],
                                                                       op=ALU.mult), reads=[skey, "rstd"], writes=[tk])
                if nseg == 1:
                    S.op("vector", lambda e, kc=kc, tt=tt: e.tensor_scalar(
                        out=hT[:, kc, 0:n], in0=tt[:, 0:n], scalar1=Amat[:, kc, s0:s0 + 1], scalar2=MOD[:, sh_j + kc, s0:s0 + 1],
                        op0=ALU.mult, op1=ALU.add), reads=[tk, akey, "MOD"], writes=["hT"])
                    continue
                S.op("vector", lambda e, kc=kc, tt=tt: e.tensor_tensor(
                    out=v3(tt[:, 0:n]), in0=v3(tt[:, 0:n]), in1=bc_last(Amat[:, kc, s0:s0 + nseg], L), op=ALU.mult),
                    reads=[tk, akey], writes=[tk])
                S.op("vector", lambda e, kc=kc, tt=tt: e.tensor_tensor(
                    out=v3(hT[:, kc, 0:n]), in0=v3(tt[:, 0:n]), in1=bc_last(MOD[:, sh_j + kc, s0:s0 + nseg], L), op=ALU.add),
                    reads=[tk, "MOD"], writes=["hT"])

        def lin_fm(wsrc, ncols, n, consume):
            for c0 in range(0, ncols, 512):
                w = min(512, ncols - c0)
                wt, wk = wload(wsrc[:, :, c0:c0 + w], 8, w)
                for cb in range(w // 128):
                    ps, pk = PS.next()
                    for kc in range(8):
                        S.op("tensor", lambda e, kc=kc, cb=cb, wt=wt, ps=ps: e.matmul(
                            ps[:, 0:n], lhsT=wt[:, kc, cb * 128:(cb + 1) * 128], rhs=hT[:, kc, 0:n],
                            start=(kc == 0), stop=(kc == 7)), reads=[wk, "hT"], writes=[pk], inc=(kc == 7))
                    consume(c0 // 128 + cb, ps, pk)

        def lin_tm(wsrc, n, consume):
            wt, wk = wload(wsrc, 8, 512)
            for tb in range((n + 127) // 128):
                rows = min(128, n - tb * 128)
                ps, pk = PS.next()
                for kc in range(8):
                    S.op("tensor", lambda e, kc=kc, tb=tb, rows=rows, wt=wt, ps=ps: e.matmul(
                        ps[0:rows, :], lhsT=hT[:, kc, tb * 128:tb * 128 + rows], rhs=wt[:, kc, :],
                        start=(kc == 0), stop=(kc == 7)), reads=[wk, "hT"], writes=[pk], inc=(kc == 7))
                consume(tb, rows, ps, pk)

        def store(dst, src, skey, chan):
            S.op("sync", lambda e: e.dma_start(out=dst, in_=src), reads=[skey], dma=chan)

        V = lambda fn, r, w: S.op("vector", fn, reads=r, writes=w)
        A = lambda fn, r, w: S.op("scalar", fn, reads=r, writes=w)

        def mchunk(Tc, nseg, L, tb, c0, TRI, IND, Cseg, nseg_ap, mst, full, TRIb):
            W4 = 4 * Tc
            i_ap = gatT[0:Tc, tb, 8:12]
            lf_ap = lfT[0:Tc, tb, 12:16]
            v4 = lambda a: a.rearrange("p (h s l) -> p h s l", h=4, l=L)
            ps, pk = PS.next()
            cumsum_tm(ps, pk, TRIb, lf_ap, "lfT", Tc, 4)
            V(lambda e: e.tensor_tensor(out=a_sb[0:Tc, :], in0=i_ap, in1=ps[0:Tc, 0:4], op=ALU.subtract), ["gatT", pk], ["a_sb"])
            A(lambda e: e.copy(out=b_sb[0:Tc, :], in_=ps[0:Tc, 0:4]), [pk], ["b_sb"])
            psA, pkA = PS.next()
            split3(a_sb[0:Tc, :], "a_sb", Tc, 4, sq_p, "sq_p")
            rowbc(psA, pkA, sq_p, "sq_p", Tc, 0, 4)
            V(lambda e: e.tensor_reduce(out=amax[:, :, 0:nseg], in_=v4(psA[:, 0:W4]), axis=AX.X, op=ALU.max), [pkA], ["amax"])
            psB, pkB = PS.next()
            split3(b_sb[0:Tc, :], "b_sb", Tc, 4, sq_p, "sq_p")
            rowbc(psB, pkB, sq_p, "sq_p", Tc, 0, 4)
            A(lambda e: e.copy(out=brow[:, 0:W4], in_=psB[:, 0:W4]), [pkB], ["brow"])
            blv = v4(brow[:, 0:W4])[:, :, :, L - 1]
            V(lambda e: e.tensor_tensor(out=Mt[:, :, 0:nseg], in0=mst, in1=amax[:, :, 0:nseg], op=ALU.max), ["mst", "amax"], ["Mt"])
            V(lambda e: e.tensor_tensor(out=dsi[:, :, 0:nseg], in0=mst, in1=Mt[:, :, 0:nseg], op=ALU.subtract), ["mst", "Mt"], ["dsi"])
            A(lambda e: e.activation(out=si[:, :, 0:nseg], in_=dsi[:, :, 0:nseg], func=AF.Exp), ["dsi"], ["si"])
            V(lambda e: e.tensor_tensor(out=mst, in0=blv, in1=Mt[:, :, 0:nseg], op=ALU.add), ["brow", "Mt"], ["mst"])
            V(lambda e: e.tensor_tensor(out=tmm[0:Tc, :, 0:nseg], in0=Mt[0:Tc, :, 0:nseg], in1=bc_mid(IND, 4), op=ALU.mult),
              ["Mt", "cstf"], ["tmm"])
            V(lambda e: e.tensor_reduce(out=Mtok[0:Tc, :], in_=tmm[0:Tc, :, 0:nseg], axis=AX.X, op=ALU.add), ["tmm"], ["Mtok"])
            V(lambda e: e.tensor_tensor(out=wa_sb[0:Tc, :], in0=a_sb[0:Tc, :], in1=Mtok[0:Tc, :], op=ALU.subtract),
              ["a_sb", "Mtok"], ["wa_sb"])
            A(lambda e: e.activation(out=w_sb[0:Tc, :], in_=wa_sb[0:Tc, :], func=AF.Exp), ["wa_sb"], ["w_sb"])
            psT, pkT = PS.next()
            for h in range(4):
                S.op("tensor", lambda e, h=h: e.transpose(out=psT[0:Tc, h * 128:(h + 1) * 128], in_=kf[:, h, c0:c0 + Tc], identity=identf),
                     reads=["kf", "cstf"], writes=[pkT], inc=(h == 3))
            A(lambda e: e.copy(out=mkT[0:Tc, :, :], in_=psT[0:Tc, :].rearrange("p (h d) -> p h d", h=4)), [pkT], ["mkT"])
            V(lambda e: e.tensor_tensor(out=vp[0:Tc, :, 0:128], in0=MV[0:Tc, tb, :].rearrange("p (h e) -> p h e", h=4),
                                        in1=bc_last(w_sb[0:Tc, :], 128), op=ALU.mult), ["MV", "w_sb"], ["vp"])
            V(lambda e: e.tensor_copy(out=vp[0:Tc, :, 128], in_=w_sb[0:Tc, :]), ["w_sb"], ["vp"])
            if full:
                V(lambda e: e.tensor_tensor(out=v4(thr[:, 0:W4]), in0=v4(brow[:, 0:W4]), in1=bc_last(Mt[:, :, 0:nseg], L), op=ALU.add),
                  ["brow", "Mt"], ["thr"])
                A(lambda e: e.activation(out=thr[:, 0:W4], in_=thr[:, 0:W4], func=AF.Exp, scale=-1.0), ["thr"], ["thr"])
                psQ, pkQ = PS.next()
                for h in range(4):
                    S.op("tensor", lambda e, h=h: e.matmul(psQ[0:Tc, h * Tc:(h + 1) * Tc], lhsT=mqk[:, 4 + h, c0:c0 + Tc],
                                                           rhs=mqk[:, h, c0:c0 + Tc], start=True, stop=True),
                         reads=["mqk"], writes=[pkQ], inc=(h == 3))
                V(lambda e: e.tensor_tensor(out=WM[0:Tc, :, 0:Tc], in0=bc_mid(TRI, 4), in1=bc_last(w_sb[0:Tc, :], Tc), op=ALU.mult),
                  ["w_sb", "cstf"], ["Ra"])
                V(lambda e: e.tensor_tensor(out=AT[0:Tc, :, 0:Tc], in0=psQ[0:Tc, 0:W4].rearrange("p (h t) -> p h t", h=4),
                                            in1=WM[0:Tc, :, 0:Tc], op=ALU.mult), [pkQ, "Ra"], ["AT"])
                num, den = ACC[0], ACC[1]
                V(lambda e: e.memset(num[:, :], 0.0), [], ["acc0"])
                V(lambda e: e.memset(den[:, :], 0.0), [], ["acc1"])
                for h in range(4):
                    S.op("tensor", lambda e, h=h: e.matmul(num[:, h * Tc:(h + 1) * Tc], lhsT=MV[0:Tc, tb, h * 128:(h + 1) * 128],
                                                           rhs=AT[0:Tc, h, 0:Tc], start=False, stop=False, skip_group_check=True),
                         reads=["MV", "AT"], writes=["acc0"], inc=False)
                    S.op("tensor", lambda e, h=h: e.matmul(den[:, h * Tc:(h + 1) * Tc], lhsT=onesb[0:Tc, :],
                                                           rhs=AT[0:Tc, h, 0:Tc], start=False, stop=False, skip_group_check=True),
                         reads=["cstb", "AT"], writes=["acc1"], inc=False)
                for b in range(nseg):
                    Cs, ck = CsR.next()
                    nr, nk = nrR.next()
                    V(lambda e, b=b, Cs=Cs: e.tensor_tensor(out=Cs[:], in0=Cseg(b), in1=bc_last(si[:, :, b], 128), op=ALU.mult),
                      ["Cst", "si"], [ck])
                    V(lambda e, b=b, nr=nr: e.tensor_tensor(out=nr[:], in0=bc_last(nseg_ap(b), 128), in1=bc_last(si[:, :, b], 128),
                                                            op=ALU.mult), ["nst", "si"], [nk])
                    for h in range(4):
                        last = (b == nseg - 1 and h == 3)
                        cs0 = h * Tc + b * L
                        S.op("tensor", lambda e, h=h, b=b, Cs=Cs, cs0=cs0: e.matmul(
                            num[:, cs0:cs0 + L], lhsT=Cs[:, h, :], rhs=mqk[:, h, c0 + b * L:c0 + (b + 1) * L], start=False, stop=True,
                            skip_group_check=True),
                            reads=[ck, "mqk"], writes=["acc0"], inc=False)
                        S.op("tensor", lambda e, h=h, b=b, nr=nr, cs0=cs0: e.matmul(
                            den[:, cs0:cs0 + L], lhsT=nr[:, h, :], rhs=mqk[:, h, c0 + b * L:c0 + (b + 1) * L], start=False, stop=True,
                            skip_group_check=True),
                            reads=[nk, "mqk"], writes=["acc1"], inc=(h == 3))
                A(lambda e: e.activation(out=hd[:, 0:W4], in_=den[:, 0:W4], func=AF.Abs), ["acc1"], ["brow"])
                V(lambda e: e.tensor_tensor(out=hd[:, 0:W4], in0=hd[:, 0:W4], in1=thr[:, 0:W4], op=ALU.max), ["brow", "thr"], ["brow"])
                V(lambda e: e.reciprocal(out=hd[:, 0:W4], in_=hd[:, 0:W4]), ["brow"], ["brow"])
                V(lambda e: e.tensor_tensor(out=hd2[:, 0:W4], in0=num[:, 0:W4], in1=hd[:, 0:W4], op=ALU.mult), ["acc0", "brow"], ["thr"])
                V(lambda e: e.tensor_tensor(out=hbT[:, :, c0:c0 + Tc], in0=hd2[:, 0:W4].rearrange("p (h t) -> p h t", h=4),
                                            in1=SBO[:, :, c0:c0 + Tc], op=ALU.mult), ["thr", "SBO"], ["hbT"])
            for b in range(nseg):
                if nseg > 1:
                    V(lambda e, b=b: e.tensor_scalar(out=vpb[0:Tc, :, :], in0=vp[0:Tc, :, :], scalar1=IND[:, b:b + 1], scalar2=None,
                                                     op0=ALU.mult), ["vp", "cstf"], ["vpb"])
                    vv, vk = vpb, "vpb"
                else:
                    vv, vk = vp, "vp"
                psC, pkC = PS.next()
                for h in range(4):
                    S.op("tensor", lambda e, h=h, vv=vv: e.matmul(psC[:, h * 128:(h + 1) * 128], lhsT=mkT[0:Tc, h, :], rhs=vv[0:Tc, h, 0:128],
                                                                  start=True, stop=True), reads=["mkT", vk], writes=[pkC], inc=(h == 3))
                psN, pkN = PS.next()
                for h in range(4):
                    S.op("tensor", lambda e, h=h, vv=vv: e.matmul(psN[:, h:h + 1], lhsT=mkT[0:Tc, h, :], rhs=vv[0:Tc, h, 128:129],
                                                                  start=True, stop=True), reads=["mkT", vk], writes=[pkN], inc=(h == 3))
                V(lambda e, b=b: e.tensor_tensor(out=Cseg(b), in0=Cseg(b), in1=bc_last(si[:, :, b], 128), op=ALU.mult), ["Cst", "si"], ["Cst"])
                V(lambda e, b=b: e.tensor_tensor(out=Cseg(b), in0=Cseg(b), in1=psC[:, :].rearrange("p (h e) -> p h e", h=4), op=ALU.add),
                  ["Cst", pkC], ["Cst"])
                V(lambda e, b=b: e.tensor_tensor(out=nseg_ap(b), in0=nseg_ap(b), in1=si[:, :, b], op=ALU.mult), ["nst", "si"], ["nst"])
                V(lambda e, b=b: e.tensor_tensor(out=nseg_ap(b), in0=nseg_ap(b), in1=psN[:, 0:4], op=ALU.add), ["nst", pkN], ["nst"])

        cst2b = sb("cst2b", [128, 3, 256], BF16)
        S.op("gpsimd", lambda e: e.dma_start(out=cst2b[:], in_=cst2), writes=["cst2b"], dma="cst2b")
        MASKD = lambda j: cst2b[:, j, 0:T]
        MASKS = cst2b[:, 2, 0:TS]
        AUGK = cst2b[:, 2, 64:192]
        iotaf = cstf[:, 4, 127:128]
        FLAGM = sb("FLAGM", [128, 1])
        V(lambda e: e.tensor_scalar(out=FLAGM[:], in0=flg[:], scalar1=-1.0, scalar2=30000.0, op0=ALU.add, op1=ALU.mult), ["flg"], ["FLAGM"])
        BIG = sb("BIG", [128, 32 * T], BF16)
        SGA = BIG[:, 0:8 * T].rearrange("p (c t) -> p c t", c=8)
        SGB = BIG[:, 8 * T:16 * T].rearrange("p (c t) -> p c t", c=8)
        YPRE = BIG[:, 16 * T:24 * T].rearrange("p (c t) -> p c t", c=8)
        attT = BIG[:, 24 * T:28 * T].rearrange("p (c t) -> p c t", c=4)
        QTp = BIG[:, 28 * T:32 * T].rearrange("p (c t) -> p c t", c=4)
        UT = BIG[:, :].rearrange("p (c t) -> p c t", c=32)
        BIGK = ["SGA", "SGB", "YPRE", "attT", "QTp"]
        Yf = cacc_alias = None
        Yf = ctmp
        x1 = cacc
        Ff = cacc[0:64, :, :]
        cs_sb = sb("cs_sb", [128, 8]); tot_sb = sb("tot_sb", [128, 8])
        carry = sb("carry", [128, 8]); crel = sb("crel", [128, 8]); Rt = sb("Rt", [128, 8])
        rec = sb("rec", [128, T])
        utmp = tmpA

        def lin_gen(wsrc, nk, ncols, n, rhs_fn, rkeys, consume):
            for c0_ in range(0, ncols, 512):
                w = min(512, ncols - c0_)
                wt, wk = wload(wsrc[:, :, c0_:c0_ + w], nk, w)
                for cb in range(w // 128):
                    ps, pk = PS.next()
                    for kc in range(nk):
                        S.op("tensor", lambda e, kc=kc, cb=cb, wt=wt, ps=ps: e.matmul(
                            ps[:, 0:n], lhsT=wt[:, kc, cb * 128:(cb + 1) * 128], rhs=rhs_fn(kc),
                            start=(kc == 0), stop=(kc == nk - 1)), reads=[wk] + rkeys, writes=[pk], inc=(kc == nk - 1))
                    consume(c0_ // 128 + cb, ps, pk)

        def w_in_phase(mode, xsrc, n, nseg, L, s0, kT_dst, v_dst, lf_dst, tok0, kcol0, KT, Vr, vnew):
            own = mode != "pre"
            smp = mode == "smp"
            ld(xt[:, :, 0:n], xsrc, "xt")
            rms_rstd(xt[:, :, 0:n], "xt", n)
            modulate(xt[:, :, 0:n], "xt", n, nseg, L, s0, AM, "AM", 0)

            def k_cons(cb, ps, pk):
                if "K" not in KSKIP:
                    V(lambda e: e.tensor_copy(out=KT[:, cb, kcol0:kcol0 + n], in_=ps[:, 0:n]), [pk], ["KT"])
                if own:
                    sg, sk = STG.next()
                    A(lambda e: e.copy(out=sg[:, 0:n], in_=ps[:, 0:n]), [pk], [sk])
                    store(kT_dst[cb * 128:(cb + 1) * 128, tok0:tok0 + n], sg[:, 0:n], sk, "st_" + sk)
            lin_fm(w_in[:, :, C_AK:C_AK + 512], 512, n, k_cons)

            def v_cons(tb, rows, ps, pk):
                if smp:
                    A(lambda e: e.copy(out=vnew[0:rows, :], in_=ps[0:rows, :]), [pk], ["vnew"])
                    store(v_dst[tok0 + tb * 128:tok0 + tb * 128 + rows, :], vnew[0:rows, :], "vnew", "st_vnew")
                    return
                if "K" not in KSKIP:
                    V(lambda e: e.tensor_copy(out=Vr[0:rows, kcol0 // 128 + tb, :], in_=ps[0:rows, :]), [pk], ["Vr"])
                if own:
                    sg, sk = STG.next()
                    A(lambda e: e.copy(out=sg[0:rows, :], in_=ps[0:rows, :]), [pk], [sk])
                    store(v_dst[tok0 + tb * 128:tok0 + tb * 128 + rows, :], sg[0:rows, :], sk, "st_" + sk)
            lin_tm(w_in[:, :, C_AV:C_AV + 512], n, v_cons)

            for tb in range((n + 127) // 128):
                rows = min(128, n - tb * 128)
                ps, pk = PS.next()
                for kc in range(8):
                    S.op("tensor", lambda e, kc=kc, tb=tb, rows=rows, ps=ps: e.matmul(
                        ps[0:rows, 0:16], lhsT=hT[:, kc, tb * 128:tb * 128 + rows], rhs=wgf[:, kc, :],
                        start=(kc == 0), stop=(kc == 7)), reads=["wgf", "hT"], writes=[pk], inc=(kc == 7))
                V(lambda e, rows=rows, ps=ps, tb=tb: e.tensor_tensor(out=gatT[0:rows, tb, :], in0=ps[0:rows, 0:16],
                                                                     in1=bgt[0:rows, :], op=ALU.add), [pk, "bgt"], ["gatT"])
                A(lambda e, rows=rows, tb=tb: e.activation(out=gex[0:rows, :], in_=gatT[0:rows, tb, :], func=AF.Exp, scale=-1.0),
                  ["gatT"], ["gex"])
                A(lambda e, rows=rows: e.activation(out=gex[0:rows, :], in_=gex[0:rows, :], func=AF.Ln,
                                                    bias=onec[0:rows, 0:1], scale=1.0), ["gex", "onec"], ["gex"])
                V(lambda e, rows=rows, tb=tb: e.tensor_scalar(out=lfT[0:rows, tb, :], in0=gex[0:rows, :], scalar1=-1.0,
                                                              scalar2=None, op0=ALU.mult), ["gex"], ["lfT"])
                if own:
                    store(lf_dst[tok0 + tb * 128:tok0 + tb * 128 + rows, :], lfT[0:rows, tb, 0:8], "lfT", "st_lfT")

            if nseg == 1:
                rv = lambda cb: rawp[:, cb, 3:3 + n]
                pv = lambda ps: ps[:, 0:n]
            else:
                r4 = rawp[:, :, 0:nseg * (L + 3)].rearrange("p c (s l) -> p c s l", l=L + 3)
                rv = lambda cb: r4[:, cb, :, 3:3 + L]
                pv = lambda ps: ps[:, 0:n].rearrange("p (s l) -> p s l", l=L)

            def r_cons(cb, ps, pk):
                A(lambda e: e.copy(out=rv(cb), in_=pv(ps)), [pk], ["rawp"])
            lin_fm(w_in[:, :, C_BQ:C_BQ + 1024], 1024, n, r_cons)
            if nseg == 1:
                win = lambda j: rawp[:, :, j:j + n]
                wj = lambda j: bc_last(convw[:, :, j], n)
                o3 = lambda t: t[:, :, 0:n]
            else:
                win = lambda j: r4[:, :, :, j:j + L]
                wj = lambda j: bc_last(bc_last(convw[:, :, j], nseg), L)
                o3 = lambda t: t[:, :, 0:n].rearrange("p c (s l) -> p c s l", l=L)
            V(lambda e: e.tensor_tensor(out=o3(cacc), in0=win(0), in1=wj(0), op=ALU.mult), ["rawp", "convw"], ["cacc"])
            for j in range(1, 4):
                V(lambda e, j=j: e.tensor_tensor(out=o3(ctmp), in0=win(j), in1=wj(j), op=ALU.mult), ["rawp", "convw"], ["ctmp"])
                V(lambda e: e.tensor_tensor(out=o3(cacc), in0=o3(cacc), in1=o3(ctmp), op=ALU.add), ["cacc", "ctmp"], ["cacc"])
            for cb in range(4):
                A(lambda e, cb=cb: e.activation(out=mqk[:, cb, 0:n], in_=cacc[:, cb, 0:n], func=AF.Silu, bias=convb[:, cb:cb + 1]),
                  ["cacc", "convb"], ["mqk"])
            for cb in range(4, 8):
                A(lambda e, cb=cb: e.activation(out=kf[:, cb - 4, 0:n], in_=cacc[:, cb, 0:n], func=AF.Silu, bias=convb[:, cb:cb + 1]),
                  ["cacc", "convb"], ["kf"])
            V(lambda e: e.tensor_scalar(out=kf[:, :, 0:n], in0=kf[:, :, 0:n], scalar1=float(B_D) ** -0.5, scalar2=None, op0=ALU.mult),
              ["kf"], ["kf"])
            V(lambda e: e.tensor_copy(out=mqk[:, 4:8, 0:n], in_=kf[:, :, 0:n]), ["kf"], ["mqk"])

            def mv_cons(tb, rows, ps, pk):
                A(lambda e: e.copy(out=MV[0:rows, tb, :], in_=ps[0:rows, :]), [pk], ["MV"])
            lin_tm(w_in[:, :, C_BV:C_BV + 512], n, mv_cons)
            if own:
                def bo_cons(cb, ps, pk):
                    A(lambda e: e.activation(out=SBO[:, cb, 0:n], in_=ps[:, 0:n], func=AF.Sigmoid), [pk], ["SBO"])
                lin_fm(w_in[:, :, C_BO:C_BO + 512], 512, n, bo_cons)

                def q_cons(cb, ps, pk):
                    V(lambda e: e.tensor_scalar(out=QTp[:, cb, 0:n], in0=ps[:, 0:n], scalar1=float(A_D) ** -0.5, scalar2=None,
                                                op0=ALU.mult), [pk], ["QTp"])
                if "K" not in KSKIP:
                    lin_fm(w_in[:, :, C_AQ:C_AQ + 512], 512, n, q_cons)

        def fox_block(tb, kb, own):
            if "F" in KSKIP:
                return
            ps, pk = PS.next()
            cumsum_tm(ps, pk, cstb[:, 2, :], lfT[:, tb, 0:8], "lfT", 128, 8)
            A(lambda e: e.copy(out=cs_sb[:], in_=ps[:, 0:8]), [pk], ["cs_sb"])
            V(lambda e: e.scalar_tensor_tensor(out=NFk[:, kb, :], in0=cs_sb[:], scalar=-1.0, in1=carry[:], op0=ALU.mult, op1=ALU.subtract),
              ["cs_sb", "carry"], ["NFk"])
            split3(cs_sb[:], "cs_sb", 128, 8, sq_p, "sq_p")
            for g in range(2):
                pr, prk = PS.next()
                rowbc(pr, prk, sq_p, "sq_p", 128, 4 * g, 4)
                pr3 = pr[:, :].rearrange("p (h t) -> p h t", h=4)
                A(lambda e, g=g, pr3=pr3: e.copy(out=tot_sb[:, 4 * g:4 * g + 4], in_=pr3[:, :, 127]), [prk], ["tot_sb"])
                if own:
                    V(lambda e, g=g, pr3=pr3: e.tensor_tensor(out=Ff[:, 4 * g:4 * g + 4, tb * 128:(tb + 1) * 128], in0=pr3[0:64, :, :],
                                                              in1=bc_last(crel[0:64, 4 * g:4 * g + 4], 128), op=ALU.add),
                      [prk, "crel"], ["cacc"])
            V(lambda e: e.tensor_tensor(out=crel[:], in0=crel[:], in1=tot_sb[:], op=ALU.add), ["crel", "tot_sb"], ["crel"])
            V(lambda e: e.tensor_tensor(out=carry[:], in0=carry[:], in1=tot_sb[:], op=ALU.add), ["carry", "tot_sb"], ["carry"])

        def post_mix(n, nseg, L, s0, xsrc, y_dst, tok0):
            v3 = lambda a: a.rearrange("p (s l) -> p s l", l=L)
            def ga_cons(cb, ps, pk):
                A(lambda e: e.activation(out=SGA[:, cb, 0:n], in_=ps[:, 0:n], func=AF.Sigmoid), [pk], ["SGA"])
            lin_fm(w_in[:, :, C_GA:C_GA + 1024], 1024, n, ga_cons)
            def gb_cons(cb, ps, pk):
                A(lambda e: e.activation(out=SGB[:, cb, 0:n], in_=ps[:, 0:n], func=AF.Sigmoid), [pk], ["SGB"])
            lin_fm(w_in[:, :, C_GB:C_GB + 1024], 1024, n, gb_cons)
            def pa_cons(cb, ps, pk):
                V(lambda e: e.tensor_tensor(out=Yf[:, cb, 0:n], in0=ps[:, 0:n], in1=SGA[:, cb, 0:n], op=ALU.mult), [pk, "SGA"], ["ctmp"])
            lin_gen(w_pa, 4, 1024, n, lambda kc: attT[:, kc, 0:n], ["attT"], pa_cons)
            def pb_cons(cb, ps, pk):
                tt, tk = utmp.next()
                V(lambda e: e.tensor_tensor(out=tt[:, 0:n], in0=ps[:, 0:n], in1=SGB[:, cb, 0:n], op=ALU.mult), [pk, "SGB"], [tk])
                V(lambda e: e.tensor_tensor(out=YPRE[:, cb, 0:n], in0=tt[:, 0:n], in1=Yf[:, cb, 0:n], op=ALU.add), [tk, "ctmp"], ["YPRE"])
            lin_gen(w_pb, 4, 1024, n, lambda kc: hbT[:, kc, 0:n], ["hbT"], pb_cons)
            def wo_cons(cb, ps, pk):
                A(lambda e: e.copy(out=Yf[:, cb, 0:n], in_=ps[:, 0:n]), [pk], ["ctmp"])
            lin_gen(w_o, 8, 1024, n, lambda kc: YPRE[:, kc, 0:n], ["YPRE"], wo_cons)
            rms_rstd(Yf[:, :, 0:n], "ctmp", n)
            ld(xt[:, :, 0:n], xsrc, "xt")
            for kc in range(8):
                tt, tk = utmp.next()
                V(lambda e, kc=kc, tt=tt: e.tensor_tensor(out=tt[:, 0:n], in0=Yf[:, kc, 0:n], in1=rstd[:, 0:n], op=ALU.mult),
                  ["ctmp", "rstd"], [tk])
                V(lambda e, kc=kc, tt=tt: e.tensor_tensor(out=v3(tt[:, 0:n]), in0=v3(tt[:, 0:n]), in1=bc_last(GM[:, kc, s0:s0 + nseg], L),
                                                          op=ALU.mult), [tk, "GM"], [tk])
                V(lambda e, kc=kc, tt=tt: e.tensor_tensor(out=x1[:, kc, 0:n], in0=tt[:, 0:n], in1=xt[:, kc, 0:n], op=ALU.add),
                  [tk, "xt"], ["cacc"])
            rms_rstd(x1[:, :, 0:n], "cacc", n)
            modulate(x1[:, :, 0:n], "cacc", n, nseg, L, s0, AFm, "AFm", 24)
            def up_cons(cb, ps, pk):
                tt, tk = utmp.next()
                A(lambda e: e.activation(out=tt[:, 0:n], in_=ps[:, 0:n], func=AF.Relu), [pk], [tk])
                V(lambda e: e.tensor_tensor(out=UT[:, cb, 0:n], in0=tt[:, 0:n], in1=tt[:, 0:n], op=ALU.mult), [tk] + BIGK, BIGK)
            lin_fm(w_u, 4096, n, up_cons)
            for half in range(2):
                for kg in range(4):
                    wt, wk = wload(w_d[:, half * 4 + kg, :, :], 8, 512)
                    for cb in range(4):
                        for kc in range(8):
                            S.op("tensor", lambda e, kc=kc, cb=cb, wt=wt, kg=kg: e.matmul(
                                ACC[cb][:, 0:n], lhsT=wt[:, kc, cb * 128:(cb + 1) * 128], rhs=UT[:, kg * 8 + kc, 0:n],
                                start=(kg == 0 and kc == 0), stop=(kg == 3 and kc == 7)),
                                reads=[wk] + BIGK, writes=["acc%d" % cb], inc=(kc == 7))
                for cb in range(4):
                    A(lambda e, cb=cb, half=half: e.copy(out=Yf[:, half * 4 + cb, 0:n], in_=ACC[cb][:, 0:n]), ["acc%d" % cb], ["ctmp"])
            rms_rstd(Yf[:, :, 0:n], "ctmp", n)
            for kc in range(8):
                tt, tk = utmp.next()
                V(lambda e, kc=kc, tt=tt: e.tensor_tensor(out=tt[:, 0:n], in0=Yf[:, kc, 0:n], in1=rstd[:, 0:n], op=ALU.mult),
                  ["ctmp", "rstd"], [tk])
                V(lambda e, kc=kc, tt=tt: e.tensor_tensor(out=v3(tt[:, 0:n]), in0=v3(tt[:, 0:n]), in1=bc_last(GFm[:, kc, s0:s0 + nseg], L),
                                                          op=ALU.mult), [tk, "GFm"], [tk])
                V(lambda e, kc=kc, tt=tt: e.tensor_tensor(out=xt[:, kc, 0:n], in0=tt[:, 0:n], in1=x1[:, kc, 0:n], op=ALU.add),
                  [tk, "cacc"], ["xt"])
            store(y_dst[:, :, tok0:tok0 + n], xt[:, :, 0:n], "xt", "st_xt")

        with contextlib.ExitStack() as sst:
            def sbs(name, shape, dt=F32):
                return sst.enter_context(nc.sbuf_tensor(name, list(shape), dt))
            CstS = sbs("CstS", [128, NSEQ, 4, 128])
            nstS = sbs("nstS", [128, NSEQ, 4])
            mstS = sbs("mstS", [128, NSEQ, 4])
            cvst = sbs("cvst", [128, 8, NSEQ, 3])
            KTs = sbs("KTs", [128, 4, TS], BF16)
            vnew = sbs("vnew", [TS, 512])
            qtm = sbs("qtm", [TS, 512])
            ptb = sbs("ptb", [128, NSEQ * 16], I32)
            ptf = sbs("ptf", [128, NSEQ * 16])
            idx = sbs("idx", [128, NSEQ * 16], I32)
            CSf = CstS[:].rearrange("p s h e -> p (s h e)")
            KgR = Ring([CSf[:, i * 2048:(i + 1) * 2048].rearrange("p (g c) -> p g c", g=4) for i in range(2)], "Kg")
            Vg = CSf[:, 4096:6144].rearrange("p (g c) -> p g c", g=4)
            prod = Vg
            QR = Ring([sbs("qrow%d" % i, [128, 512]) for i in range(2)], "qrow")
            lfpg = sbs("lfpg", [128, 16, 8])
            XA = sbs("XA", [128, 17, 8]); XB = sbs("XB", [128, 17, 8])
            Gb = sbs("Gb", [128, 16, 8])
            Sx = sbs("Sx", [128, 512]); Pp = sbs("Pp", [128, 512]); Pr = sbs("Pr", [128, 32])
            Pnew = sbs("Pnew", [TS, 8, TS])
            csn = sbs("csn", [TS, 8])
            ld(CstS[:], sC, "Cst")
            ld(nstS[:], sN, "nst")
            ld(mstS[:], sM.partition_broadcast(128), "mst")
            if "I" not in KSKIP:
                ld(ptb[:], pt.partition_broadcast(128), "ptb")
                V(lambda e: e.tensor_copy(out=ptf[:], in_=ptb[:]), ["ptb"], ["ptf"])
                V(lambda e: e.tensor_scalar(out=ptf[:], in0=ptf[:], scalar1=128.0, scalar2=iotaf, op0=ALU.mult, op1=ALU.add),
                  ["ptf", "cstf"], ["ptf"])
                V(lambda e: e.tensor_copy(out=idx[:], in_=ptf[:]), ["ptf"], ["idx"])
            r4s = rawp[:, :, 0:NSEQ * 7].rearrange("p c (s l) -> p c s l", l=7)
            ld(cvst[:], sCV, "cvst")
            V(lambda e: e.tensor_copy(out=r4s[:, :, :, 0:3], in_=cvst[:]), ["cvst", "rawp"], ["rawp"])
            w_in_phase("smp", xT_smp, TS, NSEQ, 4, 1, kT_s, v_s, lf_s, 0, 0, KTs, None, vnew)
            store(cv_s, rawp[:, :, 0:NSEQ * 7], "rawp", "st_rawp")
            mchunk(TS, NSEQ, 4, 0, 0, cstf[0:TS, 3, 0:TS], cstf[0:TS, 4, 0:NSEQ], lambda b: CstS[:, b, :, :], lambda b: nstS[:, b, :],
                   mstS[:].rearrange("p s h -> p h s"), True, cstb[0:TS, 3, 0:TS])
            store(C_s, CstS[:].rearrange("p s h e -> p (s h) e"), "Cst", "st_Cst")
            store(n_s, nstS[:].rearrange("p s h -> p (s h)"), "nst", "st_nst")
            store(m_s, mstS[0:1, :, :].rearrange("p s h -> p (s h)"), "mst", "st_mst")
            S.barrier()
            if "S" not in KSKIP:
                def qt_cons(tb, rows, ps, pk):
                    V(lambda e: e.tensor_scalar(out=qtm[0:rows, :], in0=ps[0:rows, :], scalar1=float(A_D) ** -0.5, scalar2=None, op0=ALU.mult),
                      [pk], ["qtm"])
                lin_tm(w_in[:, :, C_AQ:C_AQ + 512], TS, qt_cons)
                S.op("sync", lambda e: e.dma_start(out=qscr, in_=qtm[:]), reads=["qtm"], writes=["qscr"], dma="st_qscr")
                ps, pk = PS.next()
                cumsum_tm(ps, pk, cstb[0:TS, 3, 0:TS], lfT[0:TS, 0, 0:8], "lfT", TS, 8)
                V(lambda e: e.tensor_scalar(out=csn[:], in0=ps[0:TS, 0:8], scalar1=-1.0, scalar2=None, op0=ALU.mult), [pk], ["csn"])
                for h in range(8):
                    pair, r0 = h // 2, (h % 2) * 64
                    ps, pk = PS.next()
                    S.op("tensor", lambda e, ps=ps, pair=pair, r0=r0: e.matmul(ps[0:TS, 0:TS], lhsT=KTs[r0:r0 + 64, pair, :],
                                                                               rhs=QTp[r0:r0 + 64, pair, 0:TS], start=True, stop=False),
                         reads=["KT", "QTp"], writes=[pk], inc=False)
                    S.op("tensor", lambda e, ps=ps: e.matmul(ps[0:TS, 0:TS], lhsT=identb[:, 0:TS], rhs=MASKS, start=False, stop=True),
                         reads=["cstb", "cst2b"], writes=[pk])
                    A(lambda e, ps=ps, h=h: e.activation(out=Pnew[:, h, :], in_=ps[0:TS, 0:TS], func=AF.Exp, bias=csn[:, h:h + 1]),
                      [pk, "csn"], ["Pnew"])
                NUMs, DENs = ACC[0], ACC[1]
                LFK = ["lfpg%d" % i for i in range(16)]
                VGK = ["Vg%d" % i for i in range(4)]
                V(lambda e: e.memset(NUMs[:, :], 0.0), [], ["acc0"])
                V(lambda e: e.memset(DENs[:, :], 0.0), [], ["acc1"])
                Sx4 = Sx[:, :].rearrange("p (t h g) -> p t h g", t=4, h=8)
                Pp4 = Pp[:, :].rearrange("p (t h g) -> p t h g", t=4, h=8)
                for b in range(NSEQ if USE_CACHE else 0):
                    qrs = []
                    for pg in range(16):
                        c = b * 16 + pg
                        S.op("gpsimd", lambda e, pg=pg, c=c: e.indirect_dma_start(
                            out=lfpg[:, pg, :], out_offset=None, in_=cache_lf,
                            in_offset=bass.IndirectOffsetOnAxis(ap=idx[:, c:c + 1], axis=0)), reads=["idx"], writes=["lfpg%d" % pg], dma="lfpg%d" % pg)
                    psg, pkg = PS.next()
                    S.op("tensor", lambda e, psg=psg: e.matmul(psg[:, 0:128], lhsT=cstf[:, 1, :], rhs=lfpg[:].rearrange("p g h -> p (g h)"),
                                                               start=True, stop=True), reads=LFK + ["cstf"], writes=[pkg])
                    V(lambda e: e.memset(XA[:], 0.0), ["XA"], ["XA"])
                    V(lambda e, psg=psg: e.tensor_copy(out=XA[:, 0:16, :], in_=psg[:, 0:128].rearrange("p (g h) -> p g h", h=8)), [pkg, "XA"], ["XA"])
                    src, dst, sk_, dk_ = XA, XB, "XA", "XB"
                    for sh in (1, 2, 4, 8):
                        V(lambda e, src=src, dst=dst: e.tensor_copy(out=dst[:], in_=src[:]), [sk_], [dk_])
                        V(lambda e, src=src, dst=dst, sh=sh: e.tensor_tensor(out=dst[:, 0:16 - sh, :], in0=src[:, 0:16 - sh, :],
                                                                             in1=src[:, sh:16, :], op=ALU.add), [sk_, dk_], [dk_])
                        src, dst, sk_, dk_ = dst, src, dk_, sk_
                    psw, pkw = PS.next()
                    S.op("tensor", lambda e, psw=psw: e.matmul(psw[:, 0:128], lhsT=cstf[:, 5, :], rhs=lfpg[:].rearrange("p g h -> p (g h)"),
                                                               start=True, stop=True), reads=LFK + ["cstf"], writes=[pkw])
                    V(lambda e, psw=psw, src=src: e.tensor_tensor(out=Gb[:], in0=psw[:, 0:128].rearrange("p (g h) -> p g h", h=8),
                                                                  in1=src[:, 1:17, :], op=ALU.add), [pkw, sk_], ["Gb"])
                    for g in range(4):
                        Kg, kgk = KgR.next()
                        KGK = [kgk + "_%d" % j for j in range(4)]
                        for j in range(4):
                            c = b * 16 + g * 4 + j
                            S.op("gpsimd", lambda e, j=j, c=c, Kg=Kg: e.indirect_dma_start(
                                out=Kg[:, j, :], out_offset=None, in_=cache_k,
                                in_offset=bass.IndirectOffsetOnAxis(ap=idx[:, c:c + 1], axis=0)), reads=["idx"], writes=[KGK[j]], dma=KGK[j])
                        for t in range(4):
                            qr, qk_ = QR.next()
                            S.op("sync", lambda e, b=b, t=t, qr=qr: e.dma_start(out=qr[:], in_=qscr[4 * b + t:4 * b + t + 1, :].partition_broadcast(128)),
                                 reads=["qscr"], writes=[qk_], dma=qk_)
                            V(lambda e, qr=qr, Kg=Kg: e.tensor_tensor(out=prod[:], in0=Kg[:], in1=bc_mid(qr[:], 4), op=ALU.mult),
                              KGK + [qk_], VGK)
                            V(lambda e, t=t, g=g: e.tensor_reduce(out=Sx4[:, t, :, g * 4:(g + 1) * 4].rearrange("p h g -> p g h"),
                                                                  in_=prod[:].rearrange("p g (h d) -> p g h d", h=8), axis=AX.X, op=ALU.add),
                              VGK, ["Sx"])
                    V(lambda e: e.tensor_tensor(out=Sx4, in0=Sx4, in1=bc_mid(Gb[:].rearrange("p g h -> p h g"), 4), op=ALU.add),
                      ["Sx", "Gb"], ["Sx"])
                    A(lambda e: e.activation(out=Pp[:], in_=Sx[:], func=AF.Exp), ["Sx"], ["Pp"])
                    V(lambda e: e.tensor_reduce(out=Pr[:, :].rearrange("p (t h) -> p t h", t=4), in_=Pp4, axis=AX.X, op=ALU.add), ["Pp"], ["Pr"])
                    Pr3 = Pr[:, :].rearrange("p (t h) -> p t h", t=4)
                    for g in range(4):
                        for j in range(4):
                            c = b * 16 + g * 4 + j
                            S.op("gpsimd", lambda e, j=j, c=c: e.indirect_dma_start(
                                out=Vg[:, j, :], out_offset=None, in_=cache_v,
                                in_offset=bass.IndirectOffsetOnAxis(ap=idx[:, c:c + 1], axis=0)), reads=["idx"], writes=["Vg%d" % j], dma="Vg%d" % j)
                        for j in range(4):
                            pg = g * 4 + j
                            for h in range(8):
                                col = (h * NSEQ + b) * 4
                                S.op("tensor", lambda e, j=j, pg=pg, h=h, col=col: e.matmul(
                                    NUMs[:, col:col + 4], lhsT=Vg[:, j, (h // 2) * 128:(h // 2 + 1) * 128], rhs=Pp4[:, :, h, pg],
                                    start=False, stop=False, skip_group_check=True), reads=["Vg%d" % j, "Pp"], writes=["acc0"], inc=(h == 7))
                    for h in range(8):
                        col = (h * NSEQ + b) * 4
                        S.op("tensor", lambda e, h=h, col=col: e.matmul(DENs[:, col:col + 4], lhsT=onesf, rhs=Pr3[:, :, h],
                                                                        start=False, stop=False, skip_group_check=True),
                             reads=["Pr", "cstf"], writes=["acc1"], inc=(h == 7))
                for b in range(NSEQ):
                    for h in range(8):
                        col = (h * NSEQ + b) * 4
                        S.op("tensor", lambda e, h=h, col=col, b=b: e.matmul(DENs[:, col:col + 4], lhsT=onesf[0:TS, :], rhs=Pnew[:, h, 4 * b:4 * b + 4],
                                                                             start=False, stop=True, skip_group_check=True),
                             reads=["Pnew", "cstf"], writes=["acc1"], inc=False)
                        S.op("tensor", lambda e, h=h, col=col, b=b: e.matmul(NUMs[:, col:col + 4], lhsT=vnew[:, (h // 2) * 128:(h // 2 + 1) * 128],
                                                                             rhs=Pnew[:, h, 4 * b:4 * b + 4], start=False, stop=True, skip_group_check=True),
                             reads=["Pnew", "vnew"], writes=["acc0"], inc=(h == 7))
                for h in range(8):
                    pair, r0 = h // 2, (h % 2) * 64
                    V(lambda e, h=h, r0=r0: e.reciprocal(out=rec[r0:r0 + 64, 0:TS], in_=DENs[r0:r0 + 64, h * TS:(h + 1) * TS]), ["acc1"], ["rec"])
                    V(lambda e, h=h, r0=r0, pair=pair: e.tensor_tensor(out=attT[r0:r0 + 64, pair, 0:TS], in0=NUMs[r0:r0 + 64, h * TS:(h + 1) * TS],
                                                                       in1=rec[r0:r0 + 64, 0:TS], op=ALU.mult), ["acc0", "rec"], ["attT"])
            if "S" not in KSKIP and "M" not in KSKIP:
                post_mix(TS, NSEQ, 4, 1, xT_smp, yT_s, 0)
            S.barrier()
            S.emit()
            S.reset()

        with contextlib.ExitStack() as pst:
            KTp = pst.enter_context(nc.sbuf_tensor("KTp", [128, 4, 2 * TOK], BF16))
            Vres = pst.enter_context(nc.sbuf_tensor("Vres", [128, 32, 512], BF16))
            NFk = pst.enter_context(nc.sbuf_tensor("NFk", [128, 32, 8], F32))
            psb = lambda name, shape, dt=F32: pst.enter_context(nc.sbuf_tensor(name, list(shape), dt))
            FQa = psb("FQa", [128, 8, T], BF16)
            V(lambda e: e.memset(FQa[:], 0.0), [], ["FQa"])
            BIAS = psb("BIAS", [128, 32, 8])
            PTR = Ring([psb("PT%d" % i, [128, T], BF16) for i in range(2)], "PT")
            CstP = psb("CstP", [128, 4, 128]); nstP = psb("nstP", [128, 4]); mstP = psb("mstP", [128, 4, 1])
            V(lambda e: e.memset(CstP[:], 0.0), [], ["Cst"])
            V(lambda e: e.memset(nstP[:], 0.0), [], ["nst"])
            V(lambda e: e.memset(mstP[:], 0.0), [], ["mst"])
            V(lambda e: e.memset(rawp[:], 0.0), ["rawp"], ["rawp"])
            V(lambda e: e.memset(carry[:], 0.0), ["carry"], ["carry"])
            V(lambda e: e.memset(crel[:], 0.0), ["crel"], ["crel"])
            onescol = cstf[:, 1, 0:1]
            for ti in range(NT):
                w_in_phase("pre", xT_pre[:, :, ti * T:(ti + 1) * T], T, 1, T, 0, None, None, None, 0, ti * T, KTp, Vres, None)
                for tb in range(2):
                    fox_block(tb, ti * 2 + tb, False)
                    mchunk(128, 1, 128, tb, tb * 128, cstf[:, 2, :], onescol, lambda b: CstP[:], lambda b: nstP[:], mstP[:], False, cstb[:, 2, :])
                V(lambda e: e.tensor_copy(out=rawp[:, :, 0:3], in_=rawp[:, :, T:T + 3]), ["rawp"], ["rawp"])
            for (tl, kk) in ((CstP, "Cst"), (nstP, "nst"), (mstP, "mst"), (carry, "carry")):
                V(lambda e, tl=tl: e.tensor_scalar(out=tl[:], in0=tl[:], scalar1=flg[:, 0:1], scalar2=None, op0=ALU.mult), [kk, "flg"], [kk])
            V(lambda e: e.tensor_scalar(out=rawp[:, :, 0:3], in0=rawp[:, :, 0:3], scalar1=flg[:, 0:1], scalar2=None, op0=ALU.mult),
              ["rawp", "flg"], ["rawp"])
            for ti in range(NT):
                xs_ = xT_own[:, :, ti * T:(ti + 1) * T]
                V(lambda e: e.tensor_copy(out=Rt[:], in_=carry[:]), ["carry"], ["Rt"])
                V(lambda e: e.memset(crel[:], 0.0), ["crel"], ["crel"])
                w_in_phase("own", xs_, T, 1, T, 0, kT_o, v_o, lf_o, ti * T, TOK + ti * T, KTp, Vres, None)
                for tb in range(2):
                    fox_block(tb, 16 + ti * 2 + tb, True)
                    mchunk(128, 1, 128, tb, tb * 128, cstf[:, 2, :], onescol, lambda b: CstP[:], lambda b: nstP[:], mstP[:], True, cstb[:, 2, :])
                if ti == NT - 1:
                    store(cv_o, rawp[:, :, T:T + 3], "rawp", "st_rawp")
                V(lambda e: e.tensor_copy(out=rawp[:, :, 0:3], in_=rawp[:, :, T:T + 3]), ["rawp"], ["rawp"])
                if "F" in KSKIP:
                    continue
                V(lambda e: e.tensor_copy(out=FQa[0:64], in_=Ff[:, :, 0:T]), ["cacc"], ["FQa"])
                V(lambda e: e.tensor_tensor(out=FQa[32:64], in0=Ff[32:64, :, 0:T], in1=FQa[32:64], op=ALU.subtract), ["cacc", "FQa"], ["FQa"])
                nkb = 16 + 2 * (ti + 1)
                V(lambda e, nkb=nkb: e.tensor_tensor(out=BIAS[:, 0:nkb, :], in0=NFk[:, 0:nkb, :], in1=bc_mid(Rt[:], nkb), op=ALU.add),
                  ["NFk", "Rt"], ["BIAS"])
                V(lambda e: e.tensor_scalar(out=BIAS[:, 0:16, :], in0=BIAS[:, 0:16, :], scalar1=FLAGM[:, 0:1], scalar2=None, op0=ALU.add),
                  ["BIAS", "FLAGM"], ["BIAS"])
                for h in range(8 if "P" not in KSKIP else 0):
                    pair, r0 = h // 2, (h % 2) * 64
                    num, den = ACC[2 * (h % 2)], ACC[2 * (h % 2) + 1]
                    nk_, dk_ = "acc%d" % (2 * (h % 2)), "acc%d" % (2 * (h % 2) + 1)
                    def pv_den(kb, ptile, ptk, pair=pair, num=num, den=den, nk_=nk_, dk_=dk_, nkb=nkb):
                        S.op("tensor", lambda e: e.matmul(
                            num[:, 0:T], lhsT=Vres[:, kb, pair * 128:(pair + 1) * 128], rhs=ptile[:], start=(kb == 0), stop=(kb == nkb - 1)),
                            reads=["Vr", ptk], writes=[nk_], inc=False)
                        S.op("tensor", lambda e: e.matmul(
                            den[:, 0:T], lhsT=onesb, rhs=ptile[:], start=(kb == 0), stop=(kb == nkb - 1)),
                            reads=["cstb", ptk], writes=[dk_])
                    prev = None
                    for kb in range(nkb):
                        dj = kb - (nkb - 2)
                        ps, pk = PS.next()
                        S.op("tensor", lambda e, ps=ps, kb=kb, pair=pair, r0=r0: e.matmul(
                            ps[:, 0:T], lhsT=KTp[r0:r0 + 64, pair, kb * 128:(kb + 1) * 128], rhs=QTp[r0:r0 + 64, pair, 0:T],
                            start=True, stop=False), reads=["KT", "QTp"], writes=[pk], inc=False)
                        S.op("tensor", lambda e, ps=ps, h=h, dj=dj: e.matmul(ps[:, 0:T], lhsT=AUGK, rhs=FQa[:, h, :], start=False, stop=(dj < 0)),
                             reads=["cst2b", "FQa"], writes=[pk], inc=(dj < 0))
                        if dj >= 0:
                            S.op("tensor", lambda e, ps=ps, dj=dj: e.matmul(ps[:, 0:T], lhsT=identb, rhs=MASKD(dj), start=False, stop=True),
                                 reads=["cstb", "cst2b"], writes=[pk])
                        ptile, ptk = PTR.next()
                        A(lambda e, ps=ps, ptile=ptile, kb=kb, h=h: e.activation(out=ptile[:], in_=ps[:, 0:T], func=AF.Exp,
                                                                                 bias=BIAS[:, kb, h:h + 1]), [pk, "BIAS"], [ptk])
                        if prev is not None:
                            pv_den(*prev)
                        prev = (kb, ptile, ptk)
                    pv_den(*prev)
                    V(lambda e, r0=r0, den=den: e.reciprocal(out=rec[r0:r0 + 64, 0:T], in_=den[r0:r0 + 64, 0:T]), [dk_], ["rec"])
                    V(lambda e, r0=r0, num=num, pair=pair: e.tensor_tensor(out=attT[r0:r0 + 64, pair, 0:T], in0=num[r0:r0 + 64, 0:T],
                                                                           in1=rec[r0:r0 + 64, 0:T], op=ALU.mult), [nk_, "rec"], ["attT"])
                if "P" not in KSKIP and "M" not in KSKIP:
                    post_mix(T, 1, T, 0, xs_, yT_o, ti * T)
            store(C_o, CstP[:], "Cst", "st_Cst")
            store(n_o, nstP[:], "nst", "st_nst")
            store(m_o, mstP[0:1, :, 0], "mst", "st_mst")
            S.barrier()
            S.emit()
            S.reset()

        S.barrier()
        S.emit()
    return nc


_NC_CACHE = {}


def _rk(w):
    K, N = w.shape
    return np.ascontiguousarray(w.reshape(K // 128, 128, N).transpose(1, 0, 2))


def _bk(w):
    K, N = w.shape
    return np.ascontiguousarray(w.reshape(K // 128, 128, N // 512, 512).transpose(1, 2, 0, 3))


def kernel(x_prompt, x_sample, cache_k, cache_v, cache_logf, page_table, state_C, state_n, state_m, state_conv,
           c_prompt, c_sample, w_ada, b_ada, g_pre_mix, g_post_mix, g_pre_mlp, g_post_mlp, w_in, b_fox_f,
           b_ml_i, b_ml_f, conv_w, conv_b, w_proj_a, w_proj_b, w_out, w_up, w_down):
    f32 = np.float32
    if "nc" not in _NC_CACHE:
        _NC_CACHE["nc"] = build()
    nc = _NC_CACHE["nc"]
    x_prompt = np.asarray(x_prompt, f32)
    x_sample = np.asarray(x_sample, f32)
    w_in0 = np.asarray(w_in, f32)[0]
    w_in_p = np.concatenate([w_in0[:, O_AQ:O_AQ + 512], w_in0[:, O_AK:O_AK + 512], w_in0[:, O_AV:O_AV + 512],
                             w_in0[:, O_BQ:O_BQ + 512], w_in0[:, O_BK:O_BK + 512], w_in0[:, O_BV:O_BV + 512],
                             w_in0[:, O_BO:O_BO + 512], w_in0[:, O_GA:O_GA + 1024], w_in0[:, O_GB:O_GB + 1024]], axis=1)
    w_in_r = _bk(w_in_p)
    w_g = _rk(np.concatenate([w_in0[:, O_AF:O_AF + 8], w_in0[:, O_BI:O_BI + 8]], axis=1))
    w_ada_r = _bk(np.asarray(w_ada, f32)[0])
    b_ada_r = np.ascontiguousarray(np.asarray(b_ada, f32)[0].reshape(48, 128).T)
    g4 = np.stack([np.asarray(g, f32)[0].reshape(8, 128).T for g in (g_pre_mix, g_post_mix, g_pre_mlp, g_post_mlp)], axis=1)
    g4 = np.ascontiguousarray(g4)
    b_g = np.concatenate([np.asarray(b_fox_f, f32)[0], np.asarray(b_ml_i, f32)[0], np.asarray(b_ml_f, f32)[0]])
    b_g = np.ascontiguousarray(np.broadcast_to(b_g[None, :], (128, 16)))
    cst = np.zeros((128, 6, 128), f32)
    cst[:, 5, :] = np.tril(np.ones((128, 128)), -1)
    cst2 = np.zeros((128, 3, 256), f32)
    kk = np.arange(128)[:, None]; qq = np.arange(256)[None, :]
    cst2[:, 0, :] = np.where(qq >= kk, 0.0, -30000.0)
    cst2[:, 1, :] = np.where(qq >= kk + 128, 0.0, -30000.0)
    k6 = np.arange(64)[:, None]; q6 = np.arange(64)[None, :]
    cst2[0:64, 2, 0:64] = np.where((k6 // 4 == q6 // 4) & (k6 <= q6), 0.0, -30000.0)
    cst2[0, 2, 64:192] = 1.0
    cst2[32, 2, 64:192] = 1.0
    cst[:, 4, 127] = np.arange(128)
    w_pa_r = _bk(np.asarray(w_proj_a, f32)[0]); w_pb_r = _bk(np.asarray(w_proj_b, f32)[0])
    w_o_r = _bk(np.asarray(w_out, f32)[0]); w_u_r = _bk(np.asarray(w_up, f32)[0])
    w_d_r = np.ascontiguousarray(np.asarray(w_down, f32)[0].reshape(4, 8, 128, 2, 512).transpose(2, 3, 0, 1, 4)).reshape(128, 8, 8, 512)
    ck = np.asarray(cache_k, f32).reshape(2560 * 128, 512); cv = np.asarray(cache_v, f32).reshape(2560 * 128, 512)
    clf = np.asarray(cache_logf, f32).reshape(2560 * 128, 8)
    ptab = np.asarray(page_table).astype(np.int32)
    cst[:, 0, :] = np.eye(128)
    cst[:, 1, :] = 1.0
    cst[:, 2, :] = np.triu(np.ones((128, 128)))
    for b in range(NSEQ):
        cst[4 * b:4 * b + 4, 3, 4 * b:4 * b + 4] = np.triu(np.ones((4, 4)))
        cst[4 * b:4 * b + 4, 4, b] = 1.0
    conv_w_r = np.ascontiguousarray(np.asarray(conv_w, f32)[0].reshape(4, 8, 128).transpose(2, 1, 0))
    conv_b_r = np.ascontiguousarray(np.asarray(conv_b, f32)[0].reshape(8, 128).T)
    state_C = np.asarray(state_C, f32); state_n = np.asarray(state_n, f32); state_m = np.asarray(state_m, f32)
    state_conv = np.asarray(state_conv, f32)

    def fm(x):
        return np.ascontiguousarray(x.T.reshape(8, 128, x.shape[0]).transpose(1, 0, 2))

    in_maps = []
    for c in range(NCORE):
        s, half = c // 2, c % 2
        xo = x_prompt[s, half * TOK:(half + 1) * TOK]
        xs = x_sample[c * NSEQ:(c + 1) * NSEQ].reshape(TS, D)
        cc = np.concatenate([np.asarray(c_prompt, f32)[s:s + 1], np.asarray(c_sample, f32)[c * NSEQ:(c + 1) * NSEQ]], axis=0)
        in_maps.append({
            "xT_own": fm(xo), "xT_smp": fm(xs), "cT": fm(cc),
            "xT_pre": fm(x_prompt[s, 0:TOK]), "flag": np.full((128, 1), float(half), f32),
            "conv_w": conv_w_r, "conv_b": conv_b_r, "cst2": cst2,
            "w_pa": w_pa_r, "w_pb": w_pb_r, "w_o": w_o_r, "w_u": w_u_r, "w_d": w_d_r,
            **({"cache_k": ck, "cache_v": cv, "cache_lf": clf} if USE_CACHE else {}),
            "pt": np.ascontiguousarray(ptab[c * NSEQ:(c + 1) * NSEQ].reshape(1, NSEQ * 16)),
            "sC": np.ascontiguousarray(state_C[0, c * NSEQ:(c + 1) * NSEQ].transpose(2, 0, 1, 3)),
            "sN": np.ascontiguousarray(state_n[0, c * NSEQ:(c + 1) * NSEQ].transpose(2, 0, 1)),
            "sM": np.ascontiguousarray(state_m[0, c * NSEQ:(c + 1) * NSEQ][None]),
            "sCV": np.ascontiguousarray(state_conv[0, c * NSEQ:(c + 1) * NSEQ].reshape(NSEQ, 3, 8, 128).transpose(3, 2, 0, 1)),
            "w_ada": w_ada_r, "b_ada": b_ada_r, "g4": g4, "w_in": w_in_r, "w_g": w_g, "b_g": b_g, "cst": cst,
        })
    res = run_bass_kernel_spmd(nc, in_maps, core_ids=list(range(NCORE))).results

    B, SEQ = 4, 4096
    y_prompt = np.zeros((B, SEQ, D), f32)
    y_sample = np.zeros((128, 4, D), f32)
    new_k_p = np.zeros((1, B, SEQ, A_H, A_D), f32)
    new_v_p = np.zeros((1, B, SEQ, A_H, A_D), f32)
    new_lf_p = np.zeros((1, B, SEQ, A_H), f32)
    new_C_p = np.zeros((1, B, B_H, B_D, B_D), f32)
    new_n_p = np.zeros((1, B, B_H, B_D), f32)
    new_m_p = np.zeros((1, B, B_H), f32)
    new_cv_p = np.zeros((1, B, 3, D), f32)
    new_k_s = np.zeros((1, 128, 4, A_H, A_D), f32)
    new_v_s = np.zeros((1, 128, 4, A_H, A_D), f32)
    new_lf_s = np.zeros((1, 128, 4, A_H), f32)
    new_C_s = np.zeros((1, 128, B_H, B_D, B_D), f32)
    new_n_s = np.zeros((1, 128, B_H, B_D), f32)
    new_m_s = np.zeros((1, 128, B_H), f32)
    new_cv_s = np.zeros((1, 128, 3, D), f32)
    for c in range(NCORE):
        s, half = c // 2, c % 2
        r = res[c]
        sl = slice(half * TOK, (half + 1) * TOK)
        y_prompt[s, sl] = r["yT_o"].transpose(2, 1, 0).reshape(TOK, D)
        y_sample[c * NSEQ:(c + 1) * NSEQ] = r["yT_s"].transpose(2, 1, 0).reshape(NSEQ, 4, D)
        new_k_p[0, s, sl] = r["kT_o"].T.reshape(TOK, A_H, A_D)
        new_v_p[0, s, sl] = r["v_o"].reshape(TOK, A_H, A_D)
        new_lf_p[0, s, sl] = r["lf_o"]
        if half == 1:
            new_cv_p[0, s] = r["cv_o"].transpose(2, 1, 0).reshape(3, D)
        ss = slice(c * NSEQ, (c + 1) * NSEQ)
        new_k_s[0, ss] = r["kT_s"].T.reshape(NSEQ, 4, A_H, A_D)
        new_v_s[0, ss] = r["v_s"].reshape(NSEQ, 4, A_H, A_D)
        new_lf_s[0, ss] = r["lf_s"].reshape(NSEQ, 4, A_H)
        new_cv_s[0, ss] = r["cv_s"].reshape(128, 8, NSEQ, 7)[:, :, :, 4:7].transpose(2, 3, 1, 0).reshape(NSEQ, 3, D)
        new_C_s[0, ss] = r["C_s"].reshape(128, NSEQ, 4, 128).transpose(1, 2, 0, 3)
        new_n_s[0, ss] = r["n_s"].reshape(128, NSEQ, 4).transpose(1, 2, 0)
        new_m_s[0, ss] = r["m_s"].reshape(NSEQ, 4)
        if half == 1:
            new_C_p[0, s] = r["C_o"].transpose(1, 0, 2)
            new_n_p[0, s] = r["n_o"].T
            new_m_p[0, s] = r["m_o"][0]
    return (y_prompt, y_sample, new_k_p, new_v_p, new_lf_p, new_C_p, new_n_p, new_m_p, new_cv_p,
            new_k_s, new_v_s, new_lf_s, new_C_s, new_n_s, new_m_s, new_cv_s)
```
